# Optimizing a Trainium2 kernel written in Bass

```python
import jax, jax.numpy as jnp
from jax import lax
import numpy as np

D_MODEL = 1024
BATCH = 2
SEQ = 16384
DEPTH = 1
DEC_BATCH = 1
DEC_SEQ = 16384
PAST_LEN = 128

MLA_HEADS = 8
MLA_NOPE_DIM = 64
MLA_ROPE_DIM = 32
MLA_V_DIM = 64
Q_LORA_RANK = 384
KV_LORA_RANK = 256
D_ATTN = MLA_HEADS * MLA_V_DIM
MLSTM_HEADS = 4
MLSTM_HEAD_DIM = 128
D_MLSTM = MLSTM_HEADS * MLSTM_HEAD_DIM
D_MIX = D_ATTN + D_MLSTM
MLSTM_CHUNK = 128
Q_BLOCK = 128
CONV_WIDTH = 3
D_FF = 2816
ROPE_THETA = 10000.0
EPS = 1e-6
N_IN = Q_LORA_RANK + KV_LORA_RANK + MLA_ROPE_DIM + 4 * D_MLSTM + 4 * MLSTM_HEADS
IN_SPLITS = tuple(np.cumsum([Q_LORA_RANK, KV_LORA_RANK, MLA_ROPE_DIM, 2 * D_MLSTM, D_MLSTM, D_MLSTM]).tolist())

kernel_name = 'hymba_mla_mlstm_adaln_encoder'


def rmsnorm(x, g):
    xf = x.astype(jnp.float32)
    y = xf * lax.rsqrt(jnp.mean(xf * xf, axis=-1, keepdims=True) + EPS)
    return (y * g.astype(jnp.float32)).astype(x.dtype)


def modulate(h, shift, scale):
    return h * (1 + scale[:, None, :]) + shift[:, None, :]


def dwconv(x, w, b):
    C = x.shape[-1]
    pad = CONV_WIDTH // 2
    y = lax.conv_general_dilated(x, w[:, None, :].astype(x.dtype), window_strides=(1,),
                                 padding=((pad, pad),), dimension_numbers=('NWC', 'WIO', 'NWC'),
                                 feature_group_count=C)
    return y + b.astype(x.dtype)


def rope(x, pos):
    half = x.shape[-1] // 2
    inv = 1.0 / (ROPE_THETA ** (jnp.arange(half, dtype=jnp.float32) * (2.0 / x.shape[-1])))
    ang = pos.astype(jnp.float32)[:, None] * inv[None, :]
    cos = jnp.cos(ang)[None, :, None, :].astype(x.dtype)
    sin = jnp.sin(ang)[None, :, None, :].astype(x.dtype)
    x1, x2 = x[..., :half], x[..., half:]
    return jnp.concatenate([x1 * cos - x2 * sin, x2 * cos + x1 * sin], axis=-1)


def attend_blocks(q, k, v):
    B, S, H, DQK = q.shape
    DV = v.shape[-1]
    nb = S // Q_BLOCK
    scale = DQK ** -0.5
    qb = q.reshape(B, nb, Q_BLOCK, H, DQK).transpose(1, 0, 2, 3, 4)

    def one_block(qi):
        s = jnp.einsum('bqhd,bkhd->bhqk', qi, k, preferred_element_type=jnp.float32) * scale
        p = jax.nn.softmax(s, axis=-1).astype(v.dtype)
        return jnp.einsum('bhqk,bkhd->bqhd', p, v)

    o = lax.map(one_block, qb)
    return o.transpose(1, 0, 2, 3, 4).reshape(B, S, H * DV)


def mla_branch(cq, ckv, kr, q_norm_g, kv_norm_g, w_uq, w_ukv):
    B, S, _ = cq.shape
    pos = jnp.arange(S, dtype=jnp.int32)
    q = (rmsnorm(cq, q_norm_g) @ w_uq).reshape(B, S, MLA_HEADS, MLA_NOPE_DIM + MLA_ROPE_DIM)
    q = jnp.concatenate([q[..., :MLA_NOPE_DIM], rope(q[..., MLA_NOPE_DIM:], pos)], axis=-1)
    kv = (rmsnorm(ckv, kv_norm_g) @ w_ukv).reshape(B, S, MLA_HEADS, MLA_NOPE_DIM + MLA_V_DIM)
    k_r = rope(kr[:, :, None, :], pos)
    k = jnp.concatenate([kv[..., :MLA_NOPE_DIM],
                         jnp.broadcast_to(k_r, (B, S, MLA_HEADS, MLA_ROPE_DIM))], axis=-1)
    v = kv[..., MLA_NOPE_DIM:]
    return attend_blocks(q, k, v)


def mlstm_chunkwise(q, k, v, log_i, log_f):
    B, S, H, DK = q.shape
    DV = v.shape[-1]
    L = MLSTM_CHUNK
    nc = S // L
    f32 = jnp.float32

    def chunks(a):
        return a.astype(f32).reshape(B, nc, L, H, a.shape[-1]).transpose(1, 0, 3, 2, 4)

    def gchunks(a):
        return a.astype(f32).reshape(B, nc, L, H).transpose(1, 0, 3, 2)

    qc = chunks(q) * (DK ** -0.5)
    kc, vc = chunks(k), chunks(v)
    lic, lfc = gchunks(log_i), gchunks(log_f)
    tril = jnp.tril(jnp.ones((L, L), dtype=bool))

    def step(carry, inp):
        C, n, m = carry
        qi, ki, vi, li, lf = inp
        b = jnp.cumsum(lf, axis=-1)
        dmat = jnp.where(tril, b[..., :, None] - b[..., None, :] + li[..., None, :], -jnp.inf)
        inter = b + m[..., None]
        m_t = jnp.maximum(inter, jnp.max(dmat, axis=-1))
        w_intra = jnp.exp(dmat - m_t[..., None])
        w_state = jnp.exp(inter - m_t)
        s = jnp.einsum('bhtd,bhsd->bhts', qi, ki) * w_intra
        num = jnp.einsum('bhts,bhse->bhte', s, vi) + w_state[..., None] * jnp.einsum('bhtd,bhde->bhte', qi, C)
        den = jnp.sum(s, axis=-1) + w_state * jnp.einsum('bhtd,bhd->bht', qi, n)
        h = num / jnp.maximum(jnp.abs(den), jnp.exp(-m_t))[..., None]
        b_last = b[..., -1]
        a = b_last[..., None] - b + li
        m_new = jnp.maximum(b_last + m, jnp.max(a, axis=-1))
        w_s = jnp.exp(a - m_new[..., None])
        decay = jnp.exp(b_last + m - m_new)
        C = decay[..., None, None] * C + jnp.einsum('bhsd,bhse->bhde', ki * w_s[..., None], vi)
        n = decay[..., None] * n + jnp.einsum('bhs,bhsd->bhd', w_s, ki)
        return (C, n, m_new), h

    init = (jnp.zeros((B, H, DK, DV), f32), jnp.zeros((B, H, DK), f32), jnp.zeros((B, H), f32))
    _, hs = lax.scan(step, init, (qc, kc, vc, lic, lfc))
    return hs.transpose(1, 0, 3, 2, 4).reshape(B, S, H, DV)


def mlstm_branch(qk_raw, v_raw, o_raw, gates, conv_m_w, conv_m_b, mh_norm_g):
    B, S, _ = qk_raw.shape
    qk = jax.nn.silu(dwconv(qk_raw, conv_m_w, conv_m_b))
    q = qk[..., :D_MLSTM].reshape(B, S, MLSTM_HEADS, MLSTM_HEAD_DIM)
    k = qk[..., D_MLSTM:].reshape(B, S, MLSTM_HEADS, MLSTM_HEAD_DIM)
    v = v_raw.reshape(B, S, MLSTM_HEADS, MLSTM_HEAD_DIM)
    g = gates.astype(jnp.float32)
    i_f, f_f, i_b, f_b = jnp.split(g, 4, axis=-1)
    h_fwd = mlstm_chunkwise(q, k, v, i_f, jax.nn.log_sigmoid(f_f))
    flip = lambda a: jnp.flip(a, axis=1)
    h_bwd = flip(mlstm_chunkwise(flip(q), flip(k), flip(v), flip(i_b), flip(jax.nn.log_sigmoid(f_b))))
    h = h_fwd + h_bwd
    mu = jnp.mean(h, axis=-1, keepdims=True)
    var = jnp.mean(jnp.square(h - mu), axis=-1, keepdims=True)
    hn = (h - mu) * lax.rsqrt(var + EPS) * mh_norm_g.astype(jnp.float32).reshape(MLSTM_HEADS, MLSTM_HEAD_DIM)
    return hn.reshape(B, S, D_MLSTM).astype(o_raw.dtype) * jax.nn.sigmoid(o_raw)


def conv_ffn(h, w_up, cw, cb, w_down):
    u = dwconv(h @ w_up, cw, cb)
    a, g = jnp.split(u, 2, axis=-1)
    return (jax.nn.silu(g) * a) @ w_down


def encoder_trunk(x, c, norm1_g, w_ada, b_ada, w_in, b_gate, q_norm_g, kv_norm_g, w_uq, w_ukv,
                  conv_m_w, conv_m_b, mh_norm_g, w_out, norm2_g, w_up, conv_f_w, conv_f_b, w_down, final_g):
    for l in range(DEPTH):
        mod = jax.nn.silu(c) @ w_ada[l] + b_ada[l]
        sh1, sc1, g1, sh2, sc2, g2 = jnp.split(mod, 6, axis=-1)
        h = modulate(rmsnorm(x, norm1_g[l]), sh1, sc1)
        proj = h @ w_in[l]
        cq, ckv, kr, qk_raw, v_m, o_m, gates = jnp.split(proj, IN_SPLITS, axis=-1)
        attn = mla_branch(cq, ckv, kr, q_norm_g[l], kv_norm_g[l], w_uq[l], w_ukv[l])
        mem = mlstm_branch(qk_raw, v_m, o_m, gates + b_gate[l], conv_m_w[l], conv_m_b[l], mh_norm_g[l])
        x = x + g1[:, None, :] * (jnp.concatenate([attn, mem], axis=-1) @ w_out[l])
        h = modulate(rmsnorm(x, norm2_g[l]), sh2, sc2)
        x = x + g2[:, None, :] * conv_ffn(h, w_up[l], conv_f_w[l], conv_f_b[l], w_down[l])
    return rmsnorm(x, final_g)


def setup_inputs(seed: int = 0) -> dict:
    key = jax.random.key(seed)
    ks = jax.random.split(key, 28)
    f32 = jnp.float32

    def nrm(k, shape, scale):
        return jax.random.normal(k, shape, f32) * scale

    def gain(k, shape):
        return 1.0 + 0.02 * jax.random.normal(k, shape, f32)

    fb_fwd = jnp.linspace(3.0, 6.0, MLSTM_HEADS, dtype=f32)[None, :] + nrm(ks[22], (DEPTH, MLSTM_HEADS), 0.1)
    fb_bwd = jnp.linspace(3.0, 6.0, MLSTM_HEADS, dtype=f32)[None, :] + nrm(ks[23], (DEPTH, MLSTM_HEADS), 0.1)
    b_gate = jnp.concatenate([nrm(ks[24], (DEPTH, MLSTM_HEADS), 0.1), fb_fwd,
                              nrm(ks[25], (DEPTH, MLSTM_HEADS), 0.1), fb_bwd], axis=-1)
    return {
        'x_prompt': nrm(ks[0], (BATCH, SEQ, D_MODEL), 1.0),
        'x_sample': nrm(ks[1], (DEC_BATCH, DEC_SEQ, D_MODEL), 1.0),
        'c_prompt': nrm(ks[2], (BATCH, D_MODEL), 1.0),
        'c_sample': nrm(ks[3], (DEC_BATCH, D_MODEL), 1.0),
        'norm1_g': gain(ks[4], (DEPTH, D_MODEL)),
        'w_ada': nrm(ks[5], (DEPTH, D_MODEL, 6 * D_MODEL), 0.5 * D_MODEL ** -0.5),
        'b_ada': nrm(ks[6], (DEPTH, 6 * D_MODEL), 0.02),
        'w_in': nrm(ks[7], (DEPTH, D_MODEL, N_IN), D_MODEL ** -0.5),
        'b_gate': b_gate,
        'q_norm_g': gain(ks[8], (DEPTH, Q_LORA_RANK)),
        'kv_norm_g': gain(ks[9], (DEPTH, KV_LORA_RANK)),
        'w_uq': nrm(ks[10], (DEPTH, Q_LORA_RANK, MLA_HEADS * (MLA_NOPE_DIM + MLA_ROPE_DIM)), Q_LORA_RANK ** -0.5),
        'w_ukv': nrm(ks[11], (DEPTH, KV_LORA_RANK, MLA_HEADS * (MLA_NOPE_DIM + MLA_V_DIM)), KV_LORA_RANK ** -0.5),
        'conv_m_w': nrm(ks[12], (DEPTH, CONV_WIDTH, 2 * D_MLSTM), CONV_WIDTH ** -0.5),
        'conv_m_b': nrm(ks[13], (DEPTH, 2 * D_MLSTM), 0.02),
        'mh_norm_g': gain(ks[14], (DEPTH, D_MLSTM)),
        'w_out': nrm(ks[15], (DEPTH, D_MIX, D_MODEL), D_MIX ** -0.5),
        'norm2_g': gain(ks[16], (DEPTH, D_MODEL)),
        'w_up': nrm(ks[17], (DEPTH, D_MODEL, 2 * D_FF), D_MODEL ** -0.5),
        'conv_f_w': nrm(ks[18], (DEPTH, CONV_WIDTH, 2 * D_FF), CONV_WIDTH ** -0.5),
        'conv_f_b': nrm(ks[19], (DEPTH, 2 * D_FF), 0.02),
        'w_down': nrm(ks[20], (DEPTH, D_FF, D_MODEL), D_FF ** -0.5),
        'final_g': gain(ks[21], (D_MODEL,)),
    }


def reference(x_prompt, x_sample, c_prompt, c_sample, norm1_g, w_ada, b_ada, w_in, b_gate, q_norm_g,
              kv_norm_g, w_uq, w_ukv, conv_m_w, conv_m_b, mh_norm_g, w_out, norm2_g, w_up, conv_f_w,
              conv_f_b, w_down, final_g):
    y_prompt = encoder_trunk(x_prompt, c_prompt, norm1_g, w_ada, b_ada, w_in, b_gate, q_norm_g, kv_norm_g,
                             w_uq, w_ukv, conv_m_w, conv_m_b, mh_norm_g, w_out, norm2_g, w_up, conv_f_w,
                             conv_f_b, w_down, final_g)
    y_sample = encoder_trunk(x_sample, c_sample, norm1_g, w_ada, b_ada, w_in, b_gate, q_norm_g, kv_norm_g,
                             w_uq, w_ukv, conv_m_w, conv_m_b, mh_norm_g, w_out, norm2_g, w_up, conv_f_w,
                             conv_f_b, w_down, final_g)
    return (y_prompt, y_sample)
```

```python
import os
import contextlib
import numpy as np
import concourse.bass as bass
import concourse.mybir as mybir
from concourse.bass_utils import run_bass_kernel_spmd

F32 = mybir.dt.float32
BF16 = mybir.dt.bfloat16
AF = mybir.ActivationFunctionType
ALU = mybir.AluOpType
AX = mybir.AxisListType

D = 1024
DFF = 2816
NIN = 2736
EPS = 1e-6
CQ0, CKV0, KR0, MQK0, MV0, MO0, G0 = 0, 384, 640, 672, 1696, 2208, 2720
ROT0 = 2736
WINC = 2864
QUEUES = ("pe", "act", "dve", "pool", "sp")
SB0 = 16512
SBTOP = 229344


class Prog:
    def __init__(self, nc):
        self.nc = nc
        self.ops = []
        self.lastw = {}
        self.readers = {}
        self.last_on_eng = {}
        self.last_on_chan = {}
        self.pending = {e: set() for e in QUEUES}

    def add(self, eng, fn, reads=(), writes=(), chan=None):
        idx = len(self.ops)
        deps = set(self.pending[eng])
        self.pending[eng] = set()
        for r in reads:
            w = self.lastw.get(r)
            if w is not None:
                deps.add(w)
        for w_ in writes:
            w = self.lastw.get(w_)
            if w is not None:
                deps.add(w)
            deps.update(self.readers.get(w_, ()))
        for r in reads:
            self.readers.setdefault(r, []).append(idx)
        for w_ in writes:
            self.lastw[w_] = idx
            self.readers[w_] = []
        self.ops.append(dict(eng=eng, fn=fn, deps=deps, chan=chan, marked=(chan is not None)))
        self.last_on_eng[eng] = idx
        if chan is not None:
            self.last_on_chan[chan] = idx
        return idx

    def barrier(self):
        front = set(self.last_on_eng.values()) | set(self.last_on_chan.values())
        for e in QUEUES:
            self.pending[e] |= front
        self.lastw = {}
        self.readers = {}

    def emit(self):
        nc = self.nc
        ops = self.ops
        self.barrier()
        self.add("sp", None)

        def semkey(o):
            return ("C", o["chan"]) if o["chan"] is not None else ("E", o["eng"])

        for o in ops:
            red = {}
            for d in o["deps"]:
                p = ops[d]
                if p["fn"] is None:
                    continue
                k = semkey(p)
                if p["chan"] is None and p["eng"] == "pe" and o["eng"] == "pe" and o["chan"] is None:
                    continue
                if k not in red or red[k] < d:
                    red[k] = d
            o["rdeps"] = red
        waited = {e: {} for e in QUEUES}
        for o in ops:
            keep = {}
            for k, d in o["rdeps"].items():
                if waited[o["eng"]].get(k, -1) >= d:
                    continue
                keep[k] = d
                waited[o["eng"]][k] = d
                ops[d]["marked"] = True
            o["rdeps"] = keep
        cnt = {}
        for o in ops:
            if o["marked"] and o["fn"] is not None:
                k = semkey(o)
                inc = 16 if o["chan"] is not None else 1
                cnt[k] = cnt.get(k, 0) + inc
                o["val"] = cnt[k]
                o["inc"] = inc
        keys = list(cnt.keys())
        self.nsem = len(keys)
        self.maxcnt = max(cnt.values())
        sems = {}
        with contextlib.ExitStack() as st:
            for j, k in enumerate(keys):
                sems[k] = st.enter_context(nc.semaphore("s%d" % j))
            per = {e: [o for o in ops if o["eng"] == e] for e in QUEUES}
            blk = st.enter_context(nc.Block())

            def run(engobj, lst):
                for o in lst:
                    for k, d in o["rdeps"].items():
                        engobj.wait_ge(sems[k], ops[d]["val"])
                    if o["fn"] is None:
                        continue
                    ins = o["fn"](engobj)
                    if o["marked"]:
                        ins.then_inc(sems[semkey(o)], o["inc"])

            @blk.tensor
            def _(e):
                run(e, per["pe"])

            @blk.scalar
            def _(e):
                run(e, per["act"])

            @blk.vector
            def _(e):
                run(e, per["dve"])

            @blk.gpsimd
            def _(e):
                run(e, per["pool"])

            @blk.sync
            def _(e):
                run(e, per["sp"])


class Ctx:
    def __init__(self, nc):
        self.nc = nc
        self.P = Prog(nc)
        self.off = SB0
        self.nid = 0
        self.rr = 0

    def reset(self):
        self.off = SB0

    def sb(self, shape, dt):
        n = 1
        for s in shape[1:]:
            n *= s
        nbytes = n * (4 if dt == F32 else 2)
        nbytes = (nbytes + 63) // 64 * 64
        self.nid += 1
        t = self.nc.alloc_sbuf_tensor_at("sb%d" % self.nid, list(shape), dt, offset=self.off)
        self.off += nbytes
        assert self.off <= SBTOP, ("SBUF overflow", self.off)
        return t

    def mm(self, out, lhsT, rhs, start, stop, r, w):
        self.P.add("pe", lambda e: e.matmul(out, lhsT=lhsT, rhs=rhs, start=start, stop=stop), r, w)

    def tr(self, out, in_, ident, r, w):
        self.P.add("pe", lambda e: e.transpose(out=out, in_=in_, identity=ident), r, w)

    def act(self, out, in_, func, r, w, bias=None, scale=None, accum=None):
        kw = {}
        if bias is not None:
            kw["bias"] = bias
        if scale is not None:
            kw["scale"] = scale
        if accum is not None:
            kw["accum_out"] = accum
        self.P.add("act", lambda e: e.activation(out=out, in_=in_, func=func, **kw), r, w)

    def ts(self, eng, out, in0, s1, s2, op0, op1, r, w):
        if op1 is None:
            self.P.add(eng, lambda e: e.tensor_scalar(out=out, in0=in0, scalar1=s1, scalar2=None, op0=op0), r, w)
        else:
            self.P.add(eng, lambda e: e.tensor_scalar(out=out, in0=in0, scalar1=s1, scalar2=s2, op0=op0, op1=op1), r, w)

    def tt(self, eng, out, in0, in1, op, r, w):
        self.P.add(eng, lambda e: e.tensor_tensor(out=out, in0=in0, in1=in1, op=op), r, w)

    def stt(self, eng, out, in0, scalar, in1, op0, op1, r, w):
        self.P.add(eng, lambda e: e.scalar_tensor_tensor(out=out, in0=in0, scalar=scalar, in1=in1, op0=op0, op1=op1), r, w)

    def cp(self, eng, out, in_, r, w):
        if eng == "act":
            self.P.add("act", lambda e: e.activation(out=out, in_=in_, func=AF.Identity), r, w)
        else:
            self.P.add(eng, lambda e: e.tensor_copy(out=out, in_=in_), r, w)

    def memset(self, eng, ap, val, w):
        self.P.add(eng, lambda e: e.memset(ap, val), (), w)

    def recip(self, out, in_, r, w):
        self.P.add("dve", lambda e: e.reciprocal(out=out, in_=in_), r, w)

    def dma(self, q, out, in_, r, w, chan, slow=False):
        if slow:
            self.P.add(q, lambda e: e.dma_start(out=out, in_=in_, allow_slow_non_contiguous=True), r, w, chan=chan)
        else:
            self.P.add(q, lambda e: e.dma_start(out=out, in_=in_), r, w, chan=chan)

    def rsqrt(self, buf, key):
        self.act(buf, buf, AF.Sqrt, [key], [key])
        self.recip(buf, buf, [key], [key])


def build_program(S):
    NCH = S // 128
    HALF = S // 2
    OWNC = HALF // 128 + 1
    OWN = OWNC * 128
    NT = S // 512
    NTO = (OWN + 1 + 511) // 512
    NQT = (OWN + 511) // 512
    NFT = HALF // 512
    LNS = float(np.log(128.0 ** -0.5))
    ASC = float(96.0 ** -0.5)

    nc = bass.Bass("TRN2", target_bir_lowering=False)
    K = Ctx(nc)
    STOP = int(os.environ.get("KSTOP", "99"))
    SUB = int(os.environ.get("KSUB", "99"))
    VAR = int(os.environ.get("KVAR", "0"))
    P = K.P

    def din(name, shape):
        return nc.dram_tensor(name, list(shape), F32, kind="ExternalInput").ap()

    xs = din("xs", [S, D])
    cvec = din("cvec", [D])
    norm1_g = din("norm1_g", [D])
    w_ada = din("w_ada", [D, 6 * D])
    b_ada = din("b_ada", [6 * D])
    w_in = din("w_in", [D, NIN])
    b_gate = din("b_gate", [16])
    q_norm_g = din("q_norm_g", [384])
    kv_norm_g = din("kv_norm_g", [256])
    w_uq = din("w_uq", [384, 768])
    w_ukv = din("w_ukv", [256, 1024])
    conv_m_w = din("conv_m_w", [3, 1024])
    conv_m_b = din("conv_m_b", [1024])
    mh_norm_g = din("mh_norm_g", [512])
    w_out = din("w_out", [1024, 1024])
    norm2_g = din("norm2_g", [D])
    w_up = din("w_up", [D, 2 * DFF])
    conv_f_w = din("conv_f_w", [3, 2 * DFF])
    conv_f_b = din("conv_f_b", [2 * DFF])
    w_down = din("w_down", [DFF, D])
    final_g = din("final_g", [D])
    ident_d = din("ident", [128, 128])
    cos_d = din("cos_t", [32, S])
    sin_d = din("sin_t", [32, S])
    tri_d = din("tri", [2, 128, 128])
    mneg_d = din("mneg", [2, 128, 128])
    y_out = nc.dram_tensor("y", [HALF, D], F32, kind="ExternalOutput").ap()

    def dscr(name, shape, dt):
        return nc.dram_tensor(name, list(shape), dt).ap()

    modrow = dscr("modrow", [6 * D], F32)
    KTs = dscr("KTs", [128, 8, S], BF16)
    VAs = dscr("VAs", [8, 128, NCH, 66], BF16)
    QTs = dscr("QTs", [128, 8, NQT * 512], BF16)
    MQT = dscr("MQT", [128, 4, S + 512], BF16)
    MKT = dscr("MKT", [128, 4, S + 512], BF16)
    MVA = dscr("MVA", [NCH, 128, 4, 130], BF16)
    MG = dscr("MG", [NCH, 128, 16], F32)
    SO = dscr("SO", [NT * 4, 128, 512], F32)
    HA = dscr("HA", [OWNC, 128, 512], F32)
    ATs = dscr("ATs", [4, 128, NQT * 512], BF16)
    MEMT = dscr("MEMT", [4, 128, OWN], BF16)
    X1 = dscr("X1", [OWN, D], F32)
    H2T = dscr("H2T", [8, 128, OWN], BF16)

    banks = [nc.alloc_psum_tensor("bank%d" % i, [128, 512], F32) for i in range(8)]

    identf = K.sb([128, 128], F32)
    identb = K.sb([128, 128], BF16)
    onesf = K.sb([128, 128], F32)
    modT = K.sb([128, 48], F32)
    A1 = K.sb([128, 8], F32)
    A2 = K.sb([128, 8], F32)
    g12 = K.sb([128, 2, D], F32)
    PERSIST_END = K.off

    K.dma("sp", identf[:], ident_d[:, :], [], ["identf"], "ld0")
    K.cp("dve", identb[:], identf[:], ["identf"], ["identb"])
    K.memset("pool", onesf[:], 1.0, ["onesf"])
    cT = K.sb([128, 8], F32)
    K.dma("sp", cT[:], cvec.rearrange("(c p) -> p c", p=128), [], ["cT"], "ld1", slow=True)
    csil = K.sb([128, 8], F32)
    K.act(csil[:], cT[:], AF.Silu, ["cT"], ["csil"])
    crep = K.sb([128, 8, 128], F32)
    K.cp("dve", crep[:], csil[:].unsqueeze(2).to_broadcast([128, 8, 128]), ["csil"], ["crep"])
    badab = K.sb([128, 6 * D], F32)
    K.dma("sp", badab[:], b_ada.partition_broadcast(128), [], ["badab"], "ld2")
    modbc = K.sb([128, 6 * D], F32)
    wst = [K.sb([128, 8, 512], F32) for _ in range(2)]
    for ct in range(12):
        sl = ct % 2
        K.dma("sp", wst[sl][:], w_ada[:, ct * 512:(ct + 1) * 512].rearrange("(c p) n -> p c n", p=128),
              [], ["wst%d" % sl], "ldw%d" % sl)
        bk = banks[ct % 2]
        for kc in range(8):
            K.mm(bk[:], crep[:, kc, :], wst[sl][:, kc, :], kc == 0, kc == 7,
                 ["crep", "wst%d" % sl], ["bk%d" % (ct % 2)])
        K.tt("dve", modbc[:, ct * 512:(ct + 1) * 512], bk[:], badab[:, ct * 512:(ct + 1) * 512], ALU.add,
             ["bk%d" % (ct % 2), "badab"], ["modbc"])
    K.dma("pool", modrow.rearrange("(o n) -> o n", o=1), modbc[0:1, :], ["modbc"], ["modrow"], "st0")
    K.dma("sp", modT[:], modrow.rearrange("(j p) -> p j", p=128), ["modrow"], ["modT"], "ld3", slow=True)
    K.cp("act", g12[:, 0, :], modbc[:, 2 * D:3 * D], ["modbc"], ["g12"])
    K.cp("act", g12[:, 1, :], modbc[:, 5 * D:6 * D], ["modbc"], ["g12"])
    gT = K.sb([128, 2, 8], F32)
    K.dma("sp", gT[:, 0, :], norm1_g.rearrange("(c p) -> p c", p=128), [], ["gT"], "ld4", slow=True)
    K.dma("sp", gT[:, 1, :], norm2_g.rearrange("(c p) -> p c", p=128), [], ["gT"], "ld5", slow=True)
    K.stt("dve", A1[:], modT[:, 8:16], 1.0, gT[:, 0, :], ALU.add, ALU.mult, ["modT", "gT"], ["A1"])
    K.stt("dve", A2[:], modT[:, 32:40], 1.0, gT[:, 1, :], ALU.add, ALU.mult, ["modT", "gT"], ["A2"])
    B1 = modT[:, 0:8]
    B2 = modT[:, 24:32]
    P.barrier()

    if STOP <= 0:
        P.emit()
        return nc
    K.off = PERSIST_END
    winb = K.sb([128, 8, WINC], BF16)
    wuqb = K.sb([128, 3, 8, 256], BF16)
    wukb = K.sb([128, 2, 8, 128], BF16)
    wuvb = K.sb([128, 2, 8, 64], BF16)
    gq = K.sb([128, 3], F32)
    gkv = K.sb([128, 2], F32)
    cmw = K.sb([128, 3, 8], F32)
    cmb = K.sb([128, 8], F32)
    bgb = K.sb([128, 16], F32)
    off_a = K.off
    stg = [K.sb([128, NIN], F32) for _ in range(2)]
    K.dma("sp", gq[:], q_norm_g.rearrange("(c p) -> p c", p=128), [], ["gq"], "ld0", slow=True)
    K.dma("sp", gkv[:], kv_norm_g.rearrange("(c p) -> p c", p=128), [], ["gkv"], "ld1", slow=True)
    K.memset("pool", winb[:, :, NIN:WINC], 0.0, ["winb"])
    K.memset("pool", wuqb[:], 0.0, ["wuqb"])
    K.memset("pool", wukb[:], 0.0, ["wukb"])
    cengs = ["dve", "pool", "act"]
    for kc in range(8):
        sl = kc % 2
        K.dma("sp", stg[sl][:], w_in[kc * 128:(kc + 1) * 128, :], [], ["stg%d" % sl], "ldw%d" % sl)
        K.cp(cengs[kc % 3], winb[:, kc, 0:NIN], stg[sl][:], ["stg%d" % sl], ["winb"])
        K.ts("dve", winb[:, kc, ROT0:ROT0 + 16], stg[sl][:, KR0 + 16:KR0 + 32], -1.0, None, ALU.mult, None,
             ["stg%d" % sl], ["winb"])
        K.cp("dve", winb[:, kc, ROT0 + 16:ROT0 + 32], stg[sl][:, KR0:KR0 + 16], ["stg%d" % sl], ["winb"])
    for kc in range(3):
        sl = kc % 2
        K.dma("sp", stg[sl][:, 0:768], w_uq[kc * 128:(kc + 1) * 128, :], [], ["stg%d" % sl], "ldw%d" % sl)
        src = stg[sl][:, 0:768].rearrange("p (h e) -> p h e", h=8)
        g = gq[:, kc:kc + 1]
        rs, ws = ["stg%d" % sl, "gq"], ["wuqb"]
        K.ts("dve", wuqb[:, kc, :, 0:32], src[:, :, 64:96], g, None, ALU.mult, None, rs, ws)
        K.ts("pool", wuqb[:, kc, :, 32:96], src[:, :, 0:64], g, None, ALU.mult, None, rs, ws)
        K.ts("dve", wuqb[:, kc, :, 128:144], src[:, :, 80:96], g, -1.0, ALU.mult, ALU.mult, rs, ws)
        K.ts("dve", wuqb[:, kc, :, 144:160], src[:, :, 64:80], g, None, ALU.mult, None, rs, ws)
    for kc in range(2):
        sl = (kc + 1) % 2
        K.dma("sp", stg[sl][:, 0:1024], w_ukv[kc * 128:(kc + 1) * 128, :], [], ["stg%d" % sl], "ldw%d" % sl)
        src = stg[sl][:, 0:1024].rearrange("p (h e) -> p h e", h=8)
        g = gkv[:, kc:kc + 1]
        rs = ["stg%d" % sl, "gkv"]
        K.ts("dve", wukb[:, kc, :, 32:96], src[:, :, 0:64], g, None, ALU.mult, None, rs, ["wukb"])
        K.ts("pool", wuvb[:, kc, :, :], src[:, :, 64:128], g, None, ALU.mult, None, rs, ["wuvb"])
    for j in range(3):
        K.dma("sp", cmw[:, j, :], conv_m_w[j].rearrange("(c p) -> p c", p=128), [], ["cmw"], "ld2", slow=True)
    K.dma("sp", cmb[:], conv_m_b.rearrange("(c p) -> p c", p=128), [], ["cmb"], "ld3", slow=True)
    K.dma("sp", bgb[:], b_gate.partition_broadcast(128), [], ["bgb"], "ld4")
    P.barrier()
    K.off = off_a

    xt = [K.sb([128, 4, D], F32) for _ in range(2)]
    xnb = K.sb([128, 4, D], BF16)
    junk = K.sb([128, D], BF16)
    ssq = K.sb([128, 4], F32)
    hT = K.sb([128, 8, 512], BF16)
    cqT = K.sb([128, 3, 512], BF16)
    ckvT = K.sb([128, 2, 512], BF16)
    cst = K.sb([32, 512], F32)
    snt = K.sb([32, 512], F32)
    rt1 = K.sb([32, 512], F32)
    rt2 = K.sb([32, 512], F32)
    krr = K.sb([32, 512], BF16)
    ktile = K.sb([128, 8, 512], BF16)
    qtile = ktile
    vat = K.sb([128, 8, 4, 66], BF16)
    rawb = [K.sb([128, 514], F32) for _ in range(2)]
    cv = [K.sb([128, 512], F32) for _ in range(2)]
    mqk = K.sb([128, 8, 512], BF16)
    mva = K.sb([128, 4, 4, 130], BF16)
    sot = K.sb([128, 4, 512], F32)
    gt = K.sb([128, 4, 16], F32)
    K.memset("pool", vat[:, :, :, 64:66], 0.0, ["vat"])
    K.memset("pool", vat[:, :, :, 64:65], 1.0, ["vat"])
    K.memset("pool", mva[:, :, :, 128:130], 0.0, ["mva"])
    K.memset("pool", mva[:, :, :, 128:129], 1.0, ["mva"])
    carry = K.sb([128, 8, 2], F32)
    K.memset("pool", carry[:], 0.0, ["carry"])

    hT2 = [hT, K.sb([128, 8, 512], BF16)]
    cst2 = [cst, K.sb([32, 512], F32)]
    snt2 = [snt, K.sb([32, 512], F32)]
    ssq2 = [ssq, K.sb([128, 4], F32)]

    def front(ti):
        t0 = ti * 512
        sl = ti % 2
        S_ = "%d" % sl
        K.dma("sp", xt[sl][:], xs[t0:t0 + 512, :].rearrange("(j p) n -> p j n", p=128), [], ["xt" + S_], "ldx" + S_)
        K.dma("sp", cst2[sl][:], cos_d[:, t0:t0 + 512], [], ["cst" + S_], "ldc" + S_)
        K.dma("sp", snt2[sl][:], sin_d[:, t0:t0 + 512], [], ["snt" + S_], "lds" + S_)
        sq = ssq2[sl]
        for j in range(4):
            K.act(junk[:], xt[sl][:, j, :], AF.Square, ["xt" + S_], ["junk", "ssq" + S_], accum=sq[:, j:j + 1])
        K.ts("dve", sq[:], sq[:], 1.0 / D, EPS, ALU.mult, ALU.add, ["ssq" + S_], ["ssq" + S_])
        K.rsqrt(sq[:], "ssq" + S_)
        for j in range(4):
            K.act(xnb[:, j, :], xt[sl][:, j, :], AF.Copy, ["xt" + S_, "ssq" + S_], ["xnb%d" % j], scale=sq[:, j:j + 1])
        for kc in range(8):
            bk = banks[kc % 2]
            pb = bk[:].bitcast(BF16)
            for j in range(4):
                K.tr(pb[:, j * 128:(j + 1) * 128], xnb[:, j, kc * 128:(kc + 1) * 128], identb[:],
                     ["xnb%d" % j, "identb"], ["bk%d" % (kc % 2)])
            if kc % 2 == 0:
                K.ts("dve", hT2[sl][:, kc, :], pb[:, 0:512], A1[:, kc:kc + 1], B1[:, kc:kc + 1], ALU.mult, ALU.add,
                     ["bk%d" % (kc % 2), "A1", "modT"], ["hT%d_%d" % (sl, kc)])
            else:
                K.act(hT2[sl][:, kc, :], pb[:, 0:512], AF.Identity, ["bk%d" % (kc % 2), "A1", "modT"], ["hT%d_%d" % (sl, kc)],
                      bias=B1[:, kc:kc + 1], scale=A1[:, kc:kc + 1])

    lat2 = [K.sb([128, 4, 640], BF16) for _ in range(2)]
    latf2 = [K.sb([128, 640], F32) for _ in range(2)]
    ssl2 = [K.sb([128, 2], F32) for _ in range(2)]
    krr2 = [krr, K.sb([32, 512], BF16)]
    if os.environ.get("KDBG"):
        print("phase A sbuf top", K.off, SBTOP)

    def back1(ti):
        t0 = ti * 512
        own = ti < NTO
        sl = ti % 2
        S_ = "%d" % sl
        hT = hT2[sl]
        cst = cst2[sl]
        snt = snt2[sl]
        hTr = ["hT%d_%d" % (sl, kc) for kc in range(8)]
        for j in range(4):
            tsl = slice(j * 128, (j + 1) * 128)
            lf_ = latf2[j % 2]
            lfk = "latf%d" % (j % 2)
            sk = "ssl%d" % (j % 2)
            ss = ssl2[j % 2]
            lk = "lat%d_%d" % (sl, j)
            for (bi, c0, c1) in ((2, 0, 512), (3, 512, 640)):
                for kc in range(8):
                    K.mm(banks[bi][:, 0:c1 - c0], hT[:, kc, tsl], winb[:, kc, c0:c1], kc == 0, kc == 7,
                         hTr + ["winb"], ["bk%d" % bi])
            K.act(lf_[:, 0:512], banks[2][:, 0:512], AF.Identity, ["bk2"], [lfk])
            K.act(lf_[:, 512:640], banks[3][:, 0:128], AF.Identity, ["bk3"], [lfk])
            for kc in range(8):
                K.mm(banks[6][:], hT[:, kc, tsl], winb[:, kc, MV0:MV0 + 512], kc == 0, kc == 7, hTr + ["winb"], ["bk6"])
            K.cp("act", mva[:, j, :, 0:128], banks[6][:].rearrange("p (h e) -> p h e", h=4), ["bk6"], ["mva"])
            for kc in range(8):
                K.mm(banks[7][:, 0:16], hT[:, kc, tsl], winb[:, kc, G0:G0 + 16], kc == 0, kc == 7, hTr + ["winb"], ["bk7"])
            K.tt("dve", gt[:, j, :], banks[7][:, 0:16], bgb[:], ALU.add, ["bk7", "bgb"], ["gt"])
            if own:
                for kc in range(8):
                    K.mm(banks[5][:], hT[:, kc, tsl], winb[:, kc, MO0:MO0 + 512], kc == 0, kc == 7, hTr + ["winb"], ["bk5"])
                K.act(sot[:, j, :], banks[5][:], AF.Sigmoid, ["bk5"], ["sot"])
            K.act(junk[:, 0:384], lf_[:, 0:384], AF.Square, [lfk], ["junk", sk], accum=ss[:, 0:1])
            K.act(junk[:, 384:640], lf_[:, 384:640], AF.Square, [lfk], ["junk", sk], accum=ss[:, 1:2])
            K.ts("dve", ss[:, 0:1], ss[:, 0:1], 1.0 / 384, EPS, ALU.mult, ALU.add, [sk], [sk])
            K.ts("dve", ss[:, 1:2], ss[:, 1:2], 1.0 / 256, EPS, ALU.mult, ALU.add, [sk], [sk])
            K.rsqrt(ss[:], sk)
            K.ts("dve", lat2[sl][:, j, 0:384], lf_[:, 0:384], ss[:, 0:1], None, ALU.mult, None, [lfk, sk], [lk])
            K.ts("pool", lat2[sl][:, j, 384:640], lf_[:, 384:640], ss[:, 1:2], None, ALU.mult, None, [lfk, sk], [lk])
        gv = gt[:].rearrange("p j (d g h) -> p j d g h", d=2, g=2)
        fcols = gv[:, :, :, 1, :]
        K.act(fcols, fcols, AF.Exp, ["gt"], ["gt"], scale=-1.0)
        K.act(fcols, fcols, AF.Ln, ["gt"], ["gt"], bias=1.0)
        K.ts("dve", fcols, fcols, -1.0, None, ALU.mult, None, ["gt"], ["gt"])
        K.dma("pool", MG[ti * 4:(ti + 1) * 4].rearrange("j p g -> p j g"), gt[:], ["gt"], ["MG"], "st1")
        K.dma("pool", MVA[ti * 4:(ti + 1) * 4].rearrange("j p h e -> p j h e"), mva[:], ["mva"], ["MVA"], "st2")
        if own:
            K.dma("pool", SO[ti * 4:(ti + 1) * 4].rearrange("j p n -> p j n"), sot[:], ["sot"], ["SO"], "st4")
        for kc in range(8):
            K.mm(banks[0][:], winb[:, kc, KR0:KR0 + 128], hT[:, kc, :], kc == 0, kc == 7, hTr + ["winb"], ["bk0"])
        for kc in range(8):
            K.mm(banks[1][:], winb[:, kc, ROT0:ROT0 + 128], hT[:, kc, :], kc == 0, kc == 7, hTr + ["winb"], ["bk1"])
        K.tt("dve", rt1[:], banks[0][0:32, :], cst[:], ALU.mult, ["bk0", "cst" + S_], ["rt1"])
        K.tt("dve", rt2[:], banks[1][0:32, :], snt[:], ALU.mult, ["bk1", "snt" + S_], ["rt2"])
        K.tt("pool", krr2[sl][:], rt1[:], rt2[:], ALU.add, ["rt1", "rt2"], ["krr" + S_])

    def conv_part(ti):
        t0 = ti * 512
        last = ti == NT
        sl = ti % 2
        hT = hT2[sl]
        hTr = ["hT%d_%d" % (sl, kc) for kc in range(8)]
        for c in range(8):
            if c < 4 and not (ti < NTO + 1):
                continue
            rb = rawb[c % 2]
            rk = "rawb%d" % (c % 2)
            K.cp("pool", rb[:, 0:2], carry[:, c, :], ["carry%d" % c], [rk])
            if not last:
                bk = banks[6 + (c % 2)]
                for kc in range(8):
                    K.mm(bk[:], winb[:, kc, MQK0 + c * 128:MQK0 + (c + 1) * 128], hT[:, kc, :], kc == 0, kc == 7,
                         hTr + ["winb"], ["bk%d" % (6 + c % 2)])
                K.cp("act", rb[:, 2:514], bk[:], ["bk%d" % (6 + c % 2)], [rk])
            else:
                K.memset("pool", rb[:, 2:514], 0.0, [rk])
            K.cp("pool", carry[:, c, :], rb[:, 512:514], [rk], ["carry%d" % c])
            cb = cv[c % 2]
            ck = "cv%d" % (c % 2)
            K.ts("dve", cb[:], rb[:, 1:513], cmw[:, 1, c:c + 1], None, ALU.mult, None, [rk, "cmw"], [ck])
            K.stt("dve", cb[:], rb[:, 0:512], cmw[:, 0, c:c + 1], cb[:], ALU.mult, ALU.add, [rk, "cmw", ck], [ck])
            K.stt("dve", cb[:], rb[:, 2:514], cmw[:, 2, c:c + 1], cb[:], ALU.mult, ALU.add, [rk, "cmw", ck], [ck])
            K.act(mqk[:, c, :], cb[:], AF.Silu, [ck, "cmb"], ["mqk%d" % c], bias=cmb[:, c:c + 1])
        if ti < NTO + 1:
            K.dma("pool", MQT[:, :, t0:t0 + 512], mqk[:, 0:4, :], ["mqk%d" % c for c in range(4)], ["MQT"], "st7")
        K.dma("pool", MKT[:, :, t0:t0 + 512], mqk[:, 4:8, :], ["mqk%d" % c for c in range(4, 8)], ["MKT"], "st8")

    def back2(ti):
        t0 = ti * 512
        sl = ti % 2
        S_ = "%d" % sl
        cst = cst2[sl]
        snt = snt2[sl]
        for j in range(4):
            tsl = slice(j * 128, (j + 1) * 128)
            lk = "lat%d_%d" % (sl, j)
            lt = lat2[sl]
            pb = banks[4][:].bitcast(BF16)
            pb2 = banks[1][:].bitcast(BF16)
            for c in range(3):
                K.tr(pb[:, c * 128:(c + 1) * 128], lt[:, j, c * 128:(c + 1) * 128], identb[:], [lk, "identb"], ["bk4"])
            for c in range(2):
                K.tr(pb2[:, c * 128:(c + 1) * 128], lt[:, j, (3 + c) * 128:(4 + c) * 128], identb[:], [lk, "identb"], ["bk1"])
            K.cp("dve", cqT[:, :, tsl], pb[:, 0:384].rearrange("p (c t) -> p c t", c=3), ["bk4"], ["cqT"])
            K.cp("act", ckvT[:, :, tsl], pb2[:, 0:256].rearrange("p (c t) -> p c t", c=2), ["bk1"], ["ckvT"])
            for kc in range(2):
                K.mm(banks[5][:], ckvT[:, kc, tsl], wuvb[:, kc, :, :].rearrange("p h e -> p (h e)"), kc == 0, kc == 1,
                     ["ckvT", "wuvb"], ["bk5"])
            K.cp("act", vat[:, :, j, 0:64], banks[5][:].rearrange("p (h e) -> p h e", h=8), ["bk5"], ["vat"])
        K.dma("pool", VAs[:, :, ti * 4:(ti + 1) * 4, :].rearrange("h p j e -> p h (j e)"), vat[:].rearrange("p h j e -> p h (j e)"), ["vat"], ["VAs"], "st3")
        for h in range(8):
            bk = banks[2 + (h % 2)]
            for kc in range(2):
                K.mm(bk[:], wukb[:, kc, h, :], ckvT[:, kc, :], kc == 0, kc == 1, ["ckvT", "wukb"], ["bk%d" % (2 + h % 2)])
            if h % 2 == 0:
                K.cp("act", ktile[:, h, :], bk[:, :], ["bk%d" % (2 + h % 2)], ["ktile"])
            else:
                K.cp("dve", ktile[:, h, :], bk[:, :], ["bk%d" % (2 + h % 2)], ["ktile"])
        K.cp("pool", ktile[0:32, :, :], krr2[sl][:].unsqueeze(1).to_broadcast([32, 8, 512]), ["krr" + S_, "ktile"], ["ktile"])
        K.dma("pool", KTs[:, :, t0:t0 + 512], ktile[:], ["ktile"], ["KTs"], "st5")
        if ti < NQT:
            for h in range(8):
                ba, bb = banks[4], banks[5]
                for kc in range(3):
                    K.mm(ba[:], wuqb[:, kc, h, 0:128], cqT[:, kc, :], kc == 0, kc == 2, ["cqT", "wuqb"], ["bk4"])
                for kc in range(3):
                    K.mm(bb[:], wuqb[:, kc, h, 128:256], cqT[:, kc, :], kc == 0, kc == 2, ["cqT", "wuqb"], ["bk5"])
                K.tt("dve", rt1[:], ba[0:32, :], cst[:], ALU.mult, ["bk4", "cst" + S_], ["rt1"])
                K.tt("dve", rt2[:], bb[0:32, :], snt[:], ALU.mult, ["bk5", "snt" + S_], ["rt2"])
                K.cp("dve", qtile[:, h, :], ba[:, :], ["bk4"], ["ktile"])
                K.tt("pool", qtile[0:32, h, :], rt1[:], rt2[:], ALU.add, ["rt1", "rt2", "ktile"], ["ktile"])
            K.dma("pool", QTs[:, :, t0:t0 + 512], qtile[:], ["ktile"], ["QTs"], "st6")

    front(0)
    back1(0)
    conv_part(0)
    for ti in range(NT):
        if ti + 1 < NT:
            front(ti + 1)
            back1(ti + 1)
            conv_part(ti + 1)
        back2(ti)
        if ti + 2 < NT:
            pass
    conv_part(NT)
    P.barrier()

    if STOP <= 1:
        P.emit()
        return nc
    K.off = PERSIST_END
    kth = [K.sb([128, S], BF16) for _ in range(2)]
    vah = [K.sb([128, NCH * 66 + 64], BF16) for _ in range(2)]
    qb = [K.sb([128, 512], BF16) for _ in range(2)]
    pt = [K.sb([128, 512], BF16) for _ in range(4)]
    osb = K.sb([128, 512], F32)
    rden = K.sb([128, 512], F32)
    atb = [K.sb([64, 512], BF16) for _ in range(2)]
    sel = K.sb([128, 128], F32)
    K.memset("pool", sel[:], 0.0, ["sel"])
    K.memset("pool", sel[64:65, :], 1.0, ["sel"])
    K.memset("pool", rden[:], 0.0, ["rden0"])
    for sl in range(2):
        K.memset("pool", vah[sl][:, NCH * 66:NCH * 66 + 64], 0.0, ["vah%d" % sl])
    osb2 = [osb, K.sb([128, 512], F32)]
    rden2 = [rden, K.sb([128, 512], F32)]
    K.memset("pool", rden2[1][:], 0.0, ["rden1"])
    its = []
    for h in range(8):
        for qi in range(NQT):
            qn = min(512, OWN - qi * 512)
            for kt in range(NCH):
                its.append((h, qi, kt, qn))
    LOOK = 3
    tails = []

    def tail_fn(h, qi, qs, qn, tslot):
        def fn():
            K.mm(banks[6][:, 0:qn], sel[:], rden2[tslot][:, 0:qn], True, True, ["sel", "rden%d" % tslot], ["bk6"])
            K.tt("dve", atb[qs][:, 0:qn], osb2[tslot][0:64, 0:qn], banks[6][0:64, 0:qn], ALU.mult,
                 ["osb%d" % tslot, "bk6"], ["atb%d" % qs])
            K.dma("pool", ATs[h // 2, (h % 2) * 64:(h % 2) * 64 + 64, qi * 512:qi * 512 + qn], atb[qs][:, 0:qn],
                  ["atb%d" % qs], ["ATs"], "sta%d" % qs)
        return fn

    ntile = 0
    for step in range(len(its) + LOOK):
        if step < len(its):
            h, qi, kt, qn = its[step]
            hs = h % 2
            qs = (h * NQT + qi) % 2
            if qi == 0 and kt == 0:
                K.dma("sp", kth[hs][:], KTs[:, h, :], ["KTs"], ["kth%d" % hs], "ldk%d" % hs)
                K.dma("sp", vah[hs][:, 0:NCH * 66], VAs[h].rearrange("p j e -> p (j e)"), ["VAs"], ["vah%d" % hs], "ldv%d" % hs)
            if kt == 0:
                K.dma("sp", qb[qs][:, 0:qn], QTs[:, h, qi * 512:qi * 512 + qn], ["QTs"], ["qb%d" % qs], "ldq%d" % qs)
            sbk = step % 4
            K.mm(banks[sbk][:, 0:qn], kth[hs][:, kt * 128:(kt + 1) * 128], qb[qs][:, 0:qn], True, True,
                 ["kth%d" % hs, "qb%d" % qs], ["bk%d" % sbk])
            K.act(pt[sbk][:, 0:qn], banks[sbk][:, 0:qn], AF.Exp, ["bk%d" % sbk], ["pt%d" % sbk], scale=ASC)
        for (at, fn) in [t for t in tails if t[0] == step]:
            fn()
        tails = [t for t in tails if t[0] != step]
        j = step - LOOK
        if j >= 0:
            h, qi, kt, qn = its[j]
            hs = h % 2
            qs = (h * NQT + qi) % 2
            sbk = j % 4
            ob = banks[4 + qs]
            okey = "bk%d" % (4 + qs)
            K.mm(ob[:, 0:qn], vah[hs][:, kt * 66:kt * 66 + 128], pt[sbk][:, 0:qn], kt == 0, kt == NCH - 1,
                 ["vah%d" % hs, "pt%d" % sbk], [okey])
            if kt == NCH - 1:
                tslot = ntile % 2
                ntile += 1
                K.cp("dve", osb2[tslot][0:65, 0:qn], ob[0:65, 0:qn], [okey], ["osb%d" % tslot])
                K.recip(rden2[tslot][64:65, 0:qn], osb2[tslot][64:65, 0:qn], ["osb%d" % tslot], ["rden%d" % tslot])
                tails.append((step + 2, tail_fn(h, qi, qs, qn, tslot)))
    for (at, fn) in tails:
        fn()
    P.barrier()

    if STOP <= 2:
        P.emit()
        return nc
    K.off = PERSIST_END
    trif = K.sb([128, 2, 128], F32)
    K.dma("sp", trif[:], tri_d.rearrange("d s t -> s d t"), [], ["trif"], "ld0")
    gmh = K.sb([128, 512], F32)
    K.dma("sp", gmh[:], mh_norm_g.partition_broadcast(128), [], ["gmh"], "ld2")
    Cf = K.sb([128, 4, 128], F32)
    Cb = K.sb([128, 4, 128], BF16)
    Cn = K.sb([128, 4], F32)
    Cnb = K.sb([128, 4], BF16)
    qTt = [K.sb([128, 4, 128], BF16) for _ in range(3)]
    kTt = [K.sb([128, 4, 128], BF16) for _ in range(3)]
    vat2 = [K.sb([128, 4, 130], BF16) for _ in range(3)]
    gtt = [K.sb([128, 16], F32) for _ in range(3)]
    hat = [K.sb([128, 512], F32) for _ in range(3)]
    sot2 = [K.sb([128, 512], F32) for _ in range(3)]
    bb8 = [K.sb([128, 8], F32) for _ in range(3)]
    g4 = [K.sb([128, 4], F32) for _ in range(3)]
    egs4 = [K.sb([128, 4], F32) for _ in range(3)]
    ws4 = [K.sb([128, 4], F32) for _ in range(3)]
    dc4 = [K.sb([128, 4], F32) for _ in range(3)]
    wq4 = [K.sb([128, 4], F32) for _ in range(3)]
    dd8 = [K.sb([128, 8], F32) for _ in range(3)]
    den4 = [K.sb([128, 4], F32) for _ in range(3)]
    un4 = [K.sb([128, 4], F32) for _ in range(3)]
    TL4 = [K.sb([128, 4, 128], F32) for _ in range(3)]
    EM4 = [K.sb([128, 4, 128], F32) for _ in range(3)]
    WT4 = [K.sb([128, 4, 128], F32) for _ in range(3)]
    PT4 = [K.sb([128, 4, 128], BF16) for _ in range(3)]
    KW4 = [K.sb([128, 4, 128], BF16) for _ in range(3)]
    tmpi4 = [K.sb([128, 4, 128], F32) for _ in range(3)]
    nd4 = [K.sb([128, 4, 128], F32) for _ in range(3)]
    hout = [K.sb([128, 512], F32) for _ in range(3)]
    st4 = K.sb([128, 4], F32)
    hc = K.sb([128, 512], F32)
    hsq = K.sb([128, 512], F32)
    hnb = K.sb([128, 512], BF16)
    memt = [K.sb([128, 4, 128], BF16) for _ in range(3)]
    lnsb = K.sb([128, 1], F32)
    K.memset("pool", lnsb[:], LNS, ["lnsb"])

    def bc_t(ap4):
        return ap4.unsqueeze(2).to_broadcast([128, 4, 128])

    def stage_E(dirn, c, full, sl):
        S_ = "%d" % sl
        t1 = c * 128 + 1
        K.dma("sp", kTt[sl][:], MKT[:, :, t1:t1 + 128], ["MKT"], ["kTt" + S_], "lck" + S_)
        yield
        K.dma("sp", vat2[sl][:], MVA[c], ["MVA"], ["vat2" + S_], "lcv" + S_)
        yield
        K.dma("sp", gtt[sl][:], MG[c], ["MG"], ["gtt" + S_], "lcg" + S_)
        yield
        if full:
            K.dma("sp", qTt[sl][:], MQT[:, :, t1:t1 + 128], ["MQT"], ["qTt" + S_], "lcq" + S_)
            yield
            if dirn == 1:
                K.dma("sp", hat[sl][:], HA[c], ["HA"], ["hat" + S_], "lch" + S_)
                K.dma("sp", sot2[sl][:], SO[c], ["SO"], ["sot2" + S_], "lcs" + S_)
        li4 = gtt[sl][:, 8 * dirn:8 * dirn + 4]
        lf4 = gtt[sl][:, 8 * dirn + 4:8 * dirn + 8]
        gk = "gtt" + S_
        b4 = bb8[sl][:, 0:4]
        bl4 = bb8[sl][:, 4:8]
        K.mm(banks[0][:, 0:4], trif[:, dirn, :], lf4, True, True, ["trif", gk], ["bk0"])
        yield
        K.mm(banks[0][:, 4:8], onesf[:], lf4, True, True, ["onesf", gk], ["bk0"])
        yield
        K.cp("dve", bb8[sl][:], banks[0][:, 0:8], ["bk0"], ["bb8" + S_])
        yield
        K.tt("dve", g4[sl][:], li4, b4, ALU.subtract, [gk, "bb8" + S_], ["g4" + S_])
        yield
        K.tt("dve", ws4[sl][:], g4[sl][:], bl4, ALU.add, ["g4" + S_, "bb8" + S_], ["ws4" + S_])
        yield
        K.act(ws4[sl][:], ws4[sl][:], AF.Exp, ["ws4" + S_], ["ws4" + S_])
        yield
        K.act(dc4[sl][:], bl4, AF.Exp, ["bb8" + S_], ["dc4" + S_])
        yield
        pbk = banks[5][:].bitcast(BF16)
        for hd in range(4):
            K.tr(pbk[:, hd * 128:(hd + 1) * 128], kTt[sl][:, hd, :], identb[:], ["kTt" + S_, "identb"], ["bk5"])
        K.tt("dve", KW4[sl][:], pbk[:, 0:512].rearrange("p (h d) -> p h d", h=4), bc_t(ws4[sl][:]), ALU.mult,
             ["bk5", "ws4" + S_], ["KW4" + S_])
        yield
        if full:
            K.act(egs4[sl][:], g4[sl][:], AF.Exp, ["g4" + S_, "lnsb"], ["egs4" + S_], bias=lnsb[:])
            yield
            K.act(wq4[sl][:], b4, AF.Exp, ["bb8" + S_, "lnsb"], ["wq4" + S_], bias=lnsb[:])
            yield
            K.tt("pool", TL4[sl][:], trif[:, dirn, :].unsqueeze(1).to_broadcast([128, 4, 128]), bc_t(lf4), ALU.mult,
                 ["trif", gk], ["TL4" + S_])
            yield
            K.tt("pool", EM4[sl][:], trif[:, dirn, :].unsqueeze(1).to_broadcast([128, 4, 128]), bc_t(egs4[sl][:]), ALU.mult,
                 ["trif", "egs4" + S_], ["EM4" + S_])
            yield
            for hd in range(4):
                K.mm(banks[2][:, hd * 128:(hd + 1) * 128], kTt[sl][:, hd, :], qTt[sl][:, hd, :], True, True,
                     ["kTt" + S_, "qTt" + S_], ["bk2"])
            K.mm(banks[1][:], onesf[:], TL4[sl][:].rearrange("p h t -> p (h t)"), True, True, ["onesf", "TL4" + S_], ["bk1"])
            yield
            K.act(WT4[sl][:].rearrange("p h t -> p (h t)"), banks[1][:], AF.Exp, ["bk1"], ["WT4" + S_])
            yield
            K.tt("pool", WT4[sl][:], WT4[sl][:], EM4[sl][:], ALU.mult, ["WT4" + S_, "EM4" + S_], ["WT4" + S_])
            yield
            K.tt("dve", PT4[sl][:].rearrange("p h t -> p (h t)"), banks[2][:], WT4[sl][:].rearrange("p h t -> p (h t)"),
                 ALU.mult, ["bk2", "WT4" + S_], ["PT4" + S_])
            yield

    def stage_M(dirn, c, full, sl):
        S_ = "%d" % sl
        if full:
            for hd in range(4):
                K.mm(banks[4][:, hd * 128:(hd + 1) * 128], qTt[sl][:, hd, :], Cb[:, hd, :], True, True,
                     ["qTt" + S_, "Cb"], ["bk4"])
            for hd in range(4):
                K.mm(banks[7][:, 4 + hd:5 + hd], qTt[sl][:, hd, :], Cnb[:, hd:hd + 1], True, True,
                     ["qTt" + S_, "Cnb"], ["bk7"])
            K.tt("dve", tmpi4[sl][:], banks[4][:].rearrange("p (h e) -> p h e", h=4), bc_t(wq4[sl][:]), ALU.mult,
                 ["bk4", "wq4" + S_], ["tmpi4" + S_])
            K.cp("dve", dd8[sl][:, 4:8], banks[7][:, 4:8], ["bk7"], ["ddq" + S_])
        for hd in range(4):
            K.mm(banks[6][:, hd * 128:(hd + 1) * 128], KW4[sl][:, hd, :], vat2[sl][:, hd, 0:128], True, True,
                 ["KW4" + S_, "vat2" + S_], ["bk6"])
        for hd in range(4):
            K.mm(banks[7][:, 8 + hd:9 + hd], KW4[sl][:, hd, :], vat2[sl][:, hd, 128:129], True, True,
                 ["KW4" + S_, "vat2" + S_], ["bk7"])
        K.tt("pool", Cf[:], Cf[:], bc_t(dc4[sl][:]), ALU.mult, ["Cf", "dc4" + S_, "Cb"], ["Cf"])
        K.tt("dve", Cf[:].rearrange("p h e -> p (h e)"), Cf[:].rearrange("p h e -> p (h e)"), banks[6][:], ALU.add,
             ["Cf", "bk6"], ["Cf"])
        K.cp("act", Cb[:], Cf[:], ["Cf"], ["Cb"])
        K.cp("dve", un4[sl][:], banks[7][:, 8:12], ["bk7"], ["un4" + S_])
        K.tt("dve", Cn[:], Cn[:], dc4[sl][:], ALU.mult, ["Cn", "dc4" + S_], ["Cn"])
        K.tt("dve", Cn[:], Cn[:], un4[sl][:], ALU.add, ["Cn", "un4" + S_], ["Cn"])
        K.cp("dve", Cnb[:], Cn[:], ["Cn"], ["Cnb"])

    def stage_L(dirn, c, full, sl):
        S_ = "%d" % sl
        if not full:
            return
        yield
        for hd in range(4):
            K.mm(banks[3][:, hd * 128:(hd + 1) * 128], PT4[sl][:, hd, :], vat2[sl][:, hd, 0:128], True, True,
                 ["PT4" + S_, "vat2" + S_], ["bk3"])
            yield
        for hd in range(4):
            K.mm(banks[7][:, hd:hd + 1], PT4[sl][:, hd, :], vat2[sl][:, hd, 128:129], True, True,
                 ["PT4" + S_, "vat2" + S_], ["bk7"])
            yield
        K.tt("dve", nd4[sl][:].rearrange("p h e -> p (h e)"), banks[3][:], tmpi4[sl][:].rearrange("p h e -> p (h e)"),
             ALU.add, ["bk3", "tmpi4" + S_], ["nd4" + S_])
        yield
        K.cp("dve", dd8[sl][:, 0:4], banks[7][:, 0:4], ["bk7"], ["ddi" + S_])
        yield
        K.tt("dve", den4[sl][:], dd8[sl][:, 4:8], wq4[sl][:], ALU.mult, ["ddq" + S_, "wq4" + S_], ["den4" + S_])
        yield
        K.tt("dve", den4[sl][:], den4[sl][:], dd8[sl][:, 0:4], ALU.add, ["den4" + S_, "ddi" + S_], ["den4" + S_])
        yield
        K.act(den4[sl][:], den4[sl][:], AF.Abs, ["den4" + S_], ["den4" + S_])
        yield
        K.ts("dve", den4[sl][:], den4[sl][:], 1.0, None, ALU.max, None, ["den4" + S_], ["den4" + S_])
        yield
        K.recip(den4[sl][:], den4[sl][:], ["den4" + S_], ["den4" + S_])
        yield
        K.tt("pool", hout[sl][:].rearrange("p (h e) -> p h e", h=4), nd4[sl][:], bc_t(den4[sl][:]), ALU.mult,
             ["nd4" + S_, "den4" + S_], ["hout" + S_])
        yield
        if dirn == 0:
            K.dma("pool", HA[c], hout[sl][:], ["hout" + S_], ["HA"], "sth" + S_)
            yield
        else:
            hs_ = hout[sl]
            K.tt("dve", hs_[:], hs_[:], hat[sl][:], ALU.add, ["hout" + S_, "hat" + S_], ["hout" + S_])
            yield
            hv = hs_[:].rearrange("p (h e) -> p h e", h=4)
            K.P.add("dve", lambda e, hv=hv: e.tensor_reduce(out=st4[:], in_=hv, axis=AX.X, op=ALU.add),
                    ["hout" + S_], ["st4"])
            yield
            K.ts("dve", st4[:], st4[:], 1.0 / 128, None, ALU.mult, None, ["st4"], ["st4"])
            yield
            hcv = hc[:].rearrange("p (h e) -> p h e", h=4)
            K.tt("dve", hcv, hv, bc_t(st4[:]), ALU.subtract, ["hout" + S_, "st4"], ["hc"])
            yield
            K.tt("pool", hsq[:], hc[:], hc[:], ALU.mult, ["hc"], ["hsq"])
            yield
            hqv = hsq[:].rearrange("p (h e) -> p h e", h=4)
            K.P.add("dve", lambda e, hqv=hqv: e.tensor_reduce(out=st4[:], in_=hqv, axis=AX.X, op=ALU.add),
                    ["hsq"], ["st4"])
            yield
            K.ts("dve", st4[:], st4[:], 1.0 / 128, EPS, ALU.mult, ALU.add, ["st4"], ["st4"])
            yield
            K.rsqrt(st4[:], "st4")
            K.tt("pool", hcv, hcv, bc_t(st4[:]), ALU.mult, ["hc", "st4"], ["hc"])
            yield
            K.tt("pool", hc[:], hc[:], gmh[:], ALU.mult, ["hc", "gmh"], ["hc"])
            yield
            K.tt("dve", hnb[:], hc[:], sot2[sl][:], ALU.mult, ["hc", "sot2" + S_], ["hnb"])
            yield
            pbk7 = banks[0][:].bitcast(BF16)
            for hd in range(4):
                K.tr(pbk7[:, 512 + hd * 128:512 + (hd + 1) * 128], hnb[:, hd * 128:(hd + 1) * 128], identb[:],
                     ["hnb", "identb"], ["bk0"])
            K.cp("act", memt[sl][:], pbk7[:, 512:1024].rearrange("p (h t) -> p h t", h=4), ["bk0"], ["memt" + S_])
            yield
            K.dma("pool", MEMT[:, :, c * 128:(c + 1) * 128].rearrange("h d t -> d h t"), memt[sl][:],
                  ["memt" + S_], ["MEMT"], "stm" + S_)
            yield

    step = 0
    for dirn in range(2):
        K.memset("pool", Cf[:], 0.0, ["Cf"])
        K.memset("pool", Cb[:], 0.0, ["Cb"])
        K.memset("pool", Cn[:], 0.0, ["Cn"])
        K.memset("pool", Cnb[:], 0.0, ["Cnb"])
        if dirn == 0:
            order = [(c, True) for c in range(OWNC)]
        else:
            order = [(c, False) for c in range(NCH - 1, OWNC - 1, -1)] + [(c, True) for c in range(OWNC - 1, -1, -1)]
        sls = [(step + i) % 3 for i in range(len(order))]
        step += len(order)
        n = len(order)
        def run2(ga, gb):
            da = db = False
            while not (da and db):
                if not da:
                    try:
                        next(ga)
                    except StopIteration:
                        da = True
                if not db:
                    try:
                        next(gb)
                    except StopIteration:
                        db = True

        for _ in stage_E(dirn, order[0][0], order[0][1], sls[0]):
            pass
        for i in range(n):
            stage_M(dirn, order[i][0], order[i][1], sls[i])
            gl = stage_L(dirn, order[i][0], order[i][1], sls[i])
            if i + 1 < n:
                run2(stage_E(dirn, order[i + 1][0], order[i + 1][1], sls[i + 1]), gl)
            else:
                for _ in gl:
                    pass
        P.barrier()

    if STOP <= 3:
        P.emit()
        return nc
    K.off = PERSIST_END
    woutb = K.sb([128, 8, D], BF16)
    stg2 = [K.sb([128, D], F32) for _ in range(2)]
    for kc in range(8):
        sl = kc % 2
        K.dma("sp", stg2[sl][:], w_out[kc * 128:(kc + 1) * 128, :], [], ["stg%d" % sl], "ldw%d" % sl)
        K.tt("dve" if kc % 2 == 0 else "pool", woutb[:, kc, :], stg2[sl][:], g12[:, 0, :], ALU.mult,
             ["stg%d" % sl, "g12"], ["woutb"])
    att = [K.sb([128, 4, 128], BF16) for _ in range(2)]
    met = [K.sb([128, 4, 128], BF16) for _ in range(2)]
    xc = [K.sb([128, D], F32) for _ in range(2)]
    x1t = [K.sb([128, D], F32) for _ in range(2)]
    junk2 = K.sb([128, D], F32)
    s1 = K.sb([128, 1], F32)
    xn2 = K.sb([128, D], BF16)
    h2t = [K.sb([128, 8, 128], BF16) for _ in range(2)]
    for c in range(OWNC):
        sl = c % 2
        tsl = slice(c * 128, (c + 1) * 128)
        K.dma("sp", att[sl][:], ATs[:, :, tsl].rearrange("a f t -> f a t"), ["ATs"], ["att%d" % sl], "lda%d" % sl)
        K.dma("sp", met[sl][:], MEMT[:, :, tsl].rearrange("h d t -> d h t"), ["MEMT"], ["met%d" % sl], "ldm%d" % sl)
        K.dma("sp", xc[sl][:], xs[tsl, :], [], ["xc%d" % sl], "ldx%d" % sl)
        for nh in range(2):
            bk = banks[nh]
            for kc in range(8):
                lhs = att[sl][:, kc, :] if kc < 4 else met[sl][:, kc - 4, :]
                K.mm(bk[:], lhs, woutb[:, kc, nh * 512:(nh + 1) * 512], kc == 0, kc == 7,
                     ["att%d" % sl, "met%d" % sl, "woutb"], ["bk%d" % nh])
            K.tt("dve", x1t[sl][:, nh * 512:(nh + 1) * 512], bk[:], xc[sl][:, nh * 512:(nh + 1) * 512], ALU.add,
                 ["bk%d" % nh, "xc%d" % sl], ["x1t%d" % sl])
        K.dma("pool", X1[tsl, :], x1t[sl][:], ["x1t%d" % sl], ["X1"], "stx%d" % sl)
        K.act(junk2[:], x1t[sl][:], AF.Square, ["x1t%d" % sl], ["junk2", "s1"], accum=s1[:])
        K.ts("dve", s1[:], s1[:], 1.0 / D, EPS, ALU.mult, ALU.add, ["s1"], ["s1"])
        K.rsqrt(s1[:], "s1")
        K.act(xn2[:], x1t[sl][:], AF.Copy, ["x1t%d" % sl, "s1"], ["xn2"], scale=s1[:])
        for half in range(2):
            pbk = banks[2 + half][:].bitcast(BF16)
            for k4 in range(4):
                kc = half * 4 + k4
                K.tr(pbk[:, k4 * 128:(k4 + 1) * 128], xn2[:, kc * 128:(kc + 1) * 128], identb[:], ["xn2", "identb"],
                     ["bk%d" % (2 + half)])
            for k4 in range(4):
                kc = half * 4 + k4
                if half == 0:
                    K.ts("dve", h2t[sl][:, kc, :], pbk[:, k4 * 128:(k4 + 1) * 128], A2[:, kc:kc + 1], B2[:, kc:kc + 1],
                         ALU.mult, ALU.add, ["bk%d" % (2 + half), "A2", "modT"], ["h2t%d" % sl])
                else:
                    K.act(h2t[sl][:, kc, :], pbk[:, k4 * 128:(k4 + 1) * 128], AF.Identity,
                          ["bk%d" % (2 + half), "A2", "modT"], ["h2t%d" % sl], bias=B2[:, kc:kc + 1], scale=A2[:, kc:kc + 1])
        K.dma("pool", H2T[:, :, tsl].rearrange("k p t -> p k t"), h2t[sl][:], ["h2t%d" % sl], ["H2T"], "sth%d" % sl)
    P.barrier()

    if STOP <= 4:
        P.emit()
        return nc
    K.off = PERSIST_END
    wupb = K.sb([128, 8, 2 * DFF], BF16)
    wdnb = K.sb([128, 22, D], BF16)
    cfw = K.sb([128, 3, 44], F32)
    cfb = K.sb([128, 44], F32)
    fgb = K.sb([128, D], F32)
    off_e = K.off
    stg3 = [K.sb([128, DFF], F32) for _ in range(2)]
    i3 = 0
    for kc in range(8):
        for hf in range(2):
            sl = i3 % 2
            K.dma("sp", stg3[sl][:], w_up[kc * 128:(kc + 1) * 128, hf * DFF:(hf + 1) * DFF], [], ["stg%d" % sl], "ldw%d" % sl)
            K.cp(cengs[i3 % 3], wupb[:, kc, hf * DFF:(hf + 1) * DFF], stg3[sl][:], ["stg%d" % sl], ["wupb"])
            i3 += 1
    for kc in range(22):
        sl = i3 % 2
        K.dma("sp", stg3[sl][:, 0:D], w_down[kc * 128:(kc + 1) * 128, :], [], ["stg%d" % sl], "ldw%d" % sl)
        K.tt("dve" if kc % 2 == 0 else "pool", wdnb[:, kc, :], stg3[sl][:, 0:D], g12[:, 1, :], ALU.mult,
             ["stg%d" % sl, "g12"], ["wdnb"])
        i3 += 1
    for j in range(3):
        K.dma("sp", cfw[:, j, :], conv_f_w[j].rearrange("(c p) -> p c", p=128), [], ["cfw"], "ld0", slow=True)
    K.dma("sp", cfb[:], conv_f_b.rearrange("(c p) -> p c", p=128), [], ["cfb"], "ld1", slow=True)
    K.dma("sp", fgb[:], final_g.partition_broadcast(128), [], ["fgb"], "ld2")
    P.barrier()
    K.off = off_e
    h2w = K.sb([128, 8, 514], BF16)
    ub = [K.sb([128, 514], F32) for _ in range(2)]
    ca2 = [K.sb([128, 512], F32) for _ in range(2)]
    cg2 = [K.sb([128, 512], F32) for _ in range(2)]
    actT = K.sb([128, 22, 512], BF16)
    x1c = [K.sb([128, D], F32) for _ in range(2)]
    yc1 = K.sb([128, D], F32)
    yc = [yc1, yc1]
    s2v = K.sb([128, 1], F32)
    for ti in range(NFT):
        t0 = ti * 512
        sl = ti % 2
        hk = "h2w"
        if ti == 0:
            K.memset("pool", h2w[:, :, 0:1], 0.0, [hk])
            K.dma("sp", h2w[:, :, 1:514], H2T[:, :, 0:513].rearrange("k p t -> p k t"), ["H2T"], [hk], "ldh0")
        else:
            K.dma("sp", h2w[:, :, :], H2T[:, :, t0 - 1:t0 + 513].rearrange("k p t -> p k t"), ["H2T"], [hk], "ldh0")
        for i in range(22):
            for (which, fc) in ((0, i), (1, 22 + i)):
                u = ub[which]
                uk = "ub%d" % which
                for nt, (n0, n1) in enumerate(((0, 258), (258, 514))):
                    bk = banks[which * 2 + nt]
                    for kc in range(8):
                        K.mm(bk[:, 0:n1 - n0], wupb[:, kc, fc * 128:(fc + 1) * 128], h2w[:, kc, n0:n1],
                             kc == 0, kc == 7, [hk, "wupb"], ["bk%d" % (which * 2 + nt)])
                    K.cp("act", u[:, n0:n1], bk[:, 0:n1 - n0], ["bk%d" % (which * 2 + nt)], [uk])
                dst = ca2[i % 2] if which == 0 else cg2[i % 2]
                dk = ("ca%d" if which == 0 else "cg%d") % (i % 2)
                eng = "dve"
                K.ts(eng, dst[:], u[:, 1:513], cfw[:, 1, fc:fc + 1], cfb[:, fc:fc + 1], ALU.mult, ALU.add, [uk, "cfw", "cfb"], [dk])
                K.stt(eng, dst[:], u[:, 0:512], cfw[:, 0, fc:fc + 1], dst[:], ALU.mult, ALU.add, [uk, "cfw", dk], [dk])
                K.stt(eng, dst[:], u[:, 2:514], cfw[:, 2, fc:fc + 1], dst[:], ALU.mult, ALU.add, [uk, "cfw", dk], [dk])
            K.act(cg2[i % 2][:], cg2[i % 2][:], AF.Silu, ["cg%d" % (i % 2)], ["cg%d" % (i % 2)])
            K.tt("pool", actT[:, i, :], cg2[i % 2][:], ca2[i % 2][:], ALU.mult, ["cg%d" % (i % 2), "ca%d" % (i % 2)], ["actT%d" % i])
        ak = ["actT%d" % i for i in range(22)]
        for j in range(4):
            cidx = ti * 4 + j
            s3 = cidx % 2
            tsl = slice(t0 + j * 128, t0 + (j + 1) * 128)
            K.dma("sp", x1c[s3][:], X1[tsl, :], ["X1"], ["x1c%d" % s3], "ldx%d" % s3)
            for nh in range(2):
                bk = banks[4 + nh]
                for kc in range(22):
                    K.mm(bk[:], actT[:, kc, j * 128:(j + 1) * 128], wdnb[:, kc, nh * 512:(nh + 1) * 512], kc == 0, kc == 21,
                         ak + ["wdnb"], ["bk%d" % (4 + nh)])
                K.tt("dve", x1c[s3][:, nh * 512:(nh + 1) * 512], bk[:], x1c[s3][:, nh * 512:(nh + 1) * 512], ALU.add,
                     ["bk%d" % (4 + nh), "x1c%d" % s3], ["x1c%d" % s3])
            K.act(yc[s3][:], x1c[s3][:], AF.Square, ["x1c%d" % s3], ["yc", "s2v"], accum=s2v[:])
            K.ts("dve", s2v[:], s2v[:], 1.0 / D, EPS, ALU.mult, ALU.add, ["s2v"], ["s2v"])
            K.rsqrt(s2v[:], "s2v")
            K.act(yc[s3][:], x1c[s3][:], AF.Copy, ["x1c%d" % s3, "s2v"], ["yc"], scale=s2v[:, 0:1])
            K.tt("pool", yc[s3][:], yc[s3][:], fgb[:], ALU.mult, ["yc", "fgb"], ["yc"])
            K.dma("pool", y_out[tsl, :], yc[s3][:], ["yc"], [], "sty%d" % s3)
    P.emit()
    return nc


_CACHE = {}


def _consts(S, flip):
    half = 16
    inv = (1.0 / (np.float32(10000.0) ** (np.arange(half, dtype=np.float32) * np.float32(2.0 / 32)))).astype(np.float32)
    pos = np.arange(S, dtype=np.float32)
    if flip:
        pos = pos[::-1].copy()
    ang = (pos[:, None] * inv[None, :]).astype(np.float32)
    cos = np.cos(ang.astype(np.float64)).astype(np.float32).T
    sin = np.sin(ang.astype(np.float64)).astype(np.float32).T
    cos_t = np.ascontiguousarray(np.concatenate([cos, cos], axis=0))
    sin_t = np.ascontiguousarray(np.concatenate([sin, sin], axis=0))
    s = np.arange(128)[:, None]
    t = np.arange(128)[None, :]
    tri = np.stack([(s <= t), (s >= t)]).astype(np.float32)
    mneg = ((1.0 - tri) * -30000.0).astype(np.float32)
    return dict(ident=np.eye(128, dtype=np.float32), cos_t=cos_t, sin_t=sin_t, tri=tri, mneg=mneg)


def kernel(x_prompt, x_sample, c_prompt, c_sample, norm1_g, w_ada, b_ada, w_in, b_gate, q_norm_g,
           kv_norm_g, w_uq, w_ukv, conv_m_w, conv_m_b, mh_norm_g, w_out, norm2_g, w_up, conv_f_w,
           conv_f_b, w_down, final_g):
    f = lambda a: np.ascontiguousarray(np.asarray(a, dtype=np.float32))
    x_prompt, x_sample, c_prompt, c_sample = f(x_prompt), f(x_sample), f(c_prompt), f(c_sample)
    S = x_prompt.shape[1]
    HALF = S // 2
    seqs = [(x_prompt[0], c_prompt[0]), (x_prompt[1], c_prompt[1]), (x_sample[0], c_sample[0])]
    w_in0 = f(w_in)[0]
    gperm = np.concatenate([np.arange(G0), G0 + np.array([8, 9, 10, 11, 12, 13, 14, 15, 0, 1, 2, 3, 4, 5, 6, 7])])
    shared = dict(norm1_g=f(norm1_g)[0], w_ada=f(w_ada)[0], b_ada=f(b_ada)[0], q_norm_g=f(q_norm_g)[0],
                  kv_norm_g=f(kv_norm_g)[0], w_uq=f(w_uq)[0], w_ukv=f(w_ukv)[0], conv_m_b=f(conv_m_b)[0],
                  mh_norm_g=f(mh_norm_g)[0], w_out=f(w_out)[0], norm2_g=f(norm2_g)[0], w_up=f(w_up)[0],
                  conv_f_b=f(conv_f_b)[0], w_down=f(w_down)[0], final_g=f(final_g))
    per_flip = []
    for flip in (False, True):
        d = dict(shared)
        d.update(_consts(S, flip))
        if flip:
            d["w_in"] = np.ascontiguousarray(w_in0[:, gperm])
            d["b_gate"] = np.ascontiguousarray(f(b_gate)[0][gperm[G0:] - G0])
            d["conv_m_w"] = np.ascontiguousarray(f(conv_m_w)[0][::-1])
            d["conv_f_w"] = np.ascontiguousarray(f(conv_f_w)[0][::-1])
        else:
            d["w_in"] = w_in0
            d["b_gate"] = f(b_gate)[0]
            d["conv_m_w"] = f(conv_m_w)[0]
            d["conv_f_w"] = f(conv_f_w)[0]
        per_flip.append(d)
    in_maps = []
    for core in range(8):
        cc = core % 6
        s, j = cc // 2, cc % 2
        d = dict(per_flip[j])
        xseq, cv = seqs[s]
        d["xs"] = np.ascontiguousarray(xseq[::-1]) if j == 1 else xseq
        d["cvec"] = cv
        in_maps.append(d)
    if S not in _CACHE:
        _CACHE[S] = build_program(S)
    nc = _CACHE[S]
    res = run_bass_kernel_spmd(nc, in_maps, core_ids=list(range(8)))
    outs = []
    for s in range(3):
        y0 = np.asarray(res.results[2 * s]["y"], dtype=np.float32)
        y1 = np.asarray(res.results[2 * s + 1]["y"], dtype=np.float32)[::-1]
        outs.append(np.concatenate([y0, y1], axis=0))
    y_prompt = np.stack([outs[0], outs[1]], axis=0)
    y_sample = outs[2][None]
    return (y_prompt, y_sample)
```

```python
import os
import contextlib
import numpy as np
import concourse.bass as bass
import concourse.mybir as mybir
from concourse.bass_utils import run_bass_kernel_spmd

F32 = mybir.dt.float32
BF16 = mybir.dt.bfloat16
AF = mybir.ActivationFunctionType
ALU = mybir.AluOpType
AX = mybir.AxisListType

D = 1024
DFF = 2816
NIN = 2736
EPS = 1e-6
CQ0, CKV0, KR0, MQK0, MV0, MO0, G0 = 0, 384, 640, 672, 1696, 2208, 2720
ROT0 = 2736
WINC = 2864
QUEUES = ("pe", "act", "dve", "pool", "sp")
SB0 = 16512
SBTOP = 229344


class Prog:
    def __init__(self, nc):
        self.nc = nc
        self.ops = []
        self.lastw = {}
        self.readers = {}
        self.last_on_eng = {}
        self.last_on_chan = {}
        self.pending = {e: set() for e in QUEUES}

    def add(self, eng, fn, reads=(), writes=(), chan=None):
        idx = len(self.ops)
        deps = set(self.pending[eng])
        self.pending[eng] = set()
        for r in reads:
            w = self.lastw.get(r)
            if w is not None:
                deps.add(w)
        for w_ in writes:
            w = self.lastw.get(w_)
            if w is not None:
                deps.add(w)
            deps.update(self.readers.get(w_, ()))
        for r in reads:
            self.readers.setdefault(r, []).append(idx)
        for w_ in writes:
            self.lastw[w_] = idx
            self.readers[w_] = []
        self.ops.append(dict(eng=eng, fn=fn, deps=deps, chan=chan, marked=(chan is not None)))
        self.last_on_eng[eng] = idx
        if chan is not None:
            self.last_on_chan[chan] = idx
        return idx

    def barrier(self):
        front = set(self.last_on_eng.values()) | set(self.last_on_chan.values())
        for e in QUEUES:
            self.pending[e] |= front
        self.lastw = {}
        self.readers = {}

    def emit(self):
        nc = self.nc
        ops = self.ops
        self.barrier()
        self.add("sp", None)

        def semkey(o):
            return ("C", o["chan"]) if o["chan"] is not None else ("E", o["eng"])

        for o in ops:
            red = {}
            for d in o["deps"]:
                p = ops[d]
                if p["fn"] is None:
                    continue
                k = semkey(p)
                if p["chan"] is None and p["eng"] == "pe" and o["eng"] == "pe" and o["chan"] is None:
                    continue
                if k not in red or red[k] < d:
                    red[k] = d
            o["rdeps"] = red
        waited = {e: {} for e in QUEUES}
        for o in ops:
            keep = {}
            for k, d in o["rdeps"].items():
                if waited[o["eng"]].get(k, -1) >= d:
                    continue
                keep[k] = d
                waited[o["eng"]][k] = d
                ops[d]["marked"] = True
            o["rdeps"] = keep
        cnt = {}
        for o in ops:
            if o["marked"] and o["fn"] is not None:
                k = semkey(o)
                inc = 16 if o["chan"] is not None else 1
                cnt[k] = cnt.get(k, 0) + inc
                o["val"] = cnt[k]
                o["inc"] = inc
        keys = list(cnt.keys())
        self.nsem = len(keys)
        self.maxcnt = max(cnt.values())
        sems = {}
        with contextlib.ExitStack() as st:
            for j, k in enumerate(keys):
                sems[k] = st.enter_context(nc.semaphore("s%d" % j))
            per = {e: [o for o in ops if o["eng"] == e] for e in QUEUES}
            blk = st.enter_context(nc.Block())

            def run(engobj, lst):
                for o in lst:
                    for k, d in o["rdeps"].items():
                        engobj.wait_ge(sems[k], ops[d]["val"])
                    if o["fn"] is None:
                        continue
                    ins = o["fn"](engobj)
                    if o["marked"]:
                        ins.then_inc(sems[semkey(o)], o["inc"])

            @blk.tensor
            def _(e):
                run(e, per["pe"])

            @blk.scalar
            def _(e):
                run(e, per["act"])

            @blk.vector
            def _(e):
                run(e, per["dve"])

            @blk.gpsimd
            def _(e):
                run(e, per["pool"])

            @blk.sync
            def _(e):
                run(e, per["sp"])


class Ctx:
    def __init__(self, nc):
        self.nc = nc
        self.P = Prog(nc)
        self.off = SB0
        self.nid = 0
        self.rr = 0

    def reset(self):
        self.off = SB0

    def sb(self, shape, dt):
        n = 1
        for s in shape[1:]:
            n *= s
        nbytes = n * (4 if dt == F32 else 2)
        nbytes = (nbytes + 63) // 64 * 64
        self.nid += 1
        t = self.nc.alloc_sbuf_tensor_at("sb%d" % self.nid, list(shape), dt, offset=self.off)
        self.off += nbytes
        assert self.off <= SBTOP, ("SBUF overflow", self.off)
        return t

    def mm(self, out, lhsT, rhs, start, stop, r, w):
        self.P.add("pe", lambda e: e.matmul(out, lhsT=lhsT, rhs=rhs, start=start, stop=stop), r, w)

    def tr(self, out, in_, ident, r, w):
        self.P.add("pe", lambda e: e.transpose(out=out, in_=in_, identity=ident), r, w)

    def act(self, out, in_, func, r, w, bias=None, scale=None, accum=None):
        kw = {}
        if bias is not None:
            kw["bias"] = bias
        if scale is not None:
            kw["scale"] = scale
        if accum is not None:
            kw["accum_out"] = accum
        self.P.add("act", lambda e: e.activation(out=out, in_=in_, func=func, **kw), r, w)

    def ts(self, eng, out, in0, s1, s2, op0, op1, r, w):
        if op1 is None:
            self.P.add(eng, lambda e: e.tensor_scalar(out=out, in0=in0, scalar1=s1, scalar2=None, op0=op0), r, w)
        else:
            self.P.add(eng, lambda e: e.tensor_scalar(out=out, in0=in0, scalar1=s1, scalar2=s2, op0=op0, op1=op1), r, w)

    def tt(self, eng, out, in0, in1, op, r, w):
        self.P.add(eng, lambda e: e.tensor_tensor(out=out, in0=in0, in1=in1, op=op), r, w)

    def stt(self, eng, out, in0, scalar, in1, op0, op1, r, w):
        self.P.add(eng, lambda e: e.scalar_tensor_tensor(out=out, in0=in0, scalar=scalar, in1=in1, op0=op0, op1=op1), r, w)

    def cp(self, eng, out, in_, r, w):
        if eng == "act":
            self.P.add("act", lambda e: e.activation(out=out, in_=in_, func=AF.Identity), r, w)
        else:
            self.P.add(eng, lambda e: e.tensor_copy(out=out, in_=in_), r, w)

    def memset(self, eng, ap, val, w):
        self.P.add(eng, lambda e: e.memset(ap, val), (), w)

    def recip(self, out, in_, r, w):
        self.P.add("dve", lambda e: e.reciprocal(out=out, in_=in_), r, w)

    def dma(self, q, out, in_, r, w, chan, slow=False):
        if slow:
            self.P.add(q, lambda e: e.dma_start(out=out, in_=in_, allow_slow_non_contiguous=True), r, w, chan=chan)
        else:
            self.P.add(q, lambda e: e.dma_start(out=out, in_=in_), r, w, chan=chan)

    def rsqrt(self, buf, key):
        self.act(buf, buf, AF.Sqrt, [key], [key])
        self.recip(buf, buf, [key], [key])


def build_program(S):
    NCH = S // 128
    HALF = S // 2
    OWNC = HALF // 128 + 1
    OWN = OWNC * 128
    NT = S // 512
    NTO = (OWN + 1 + 511) // 512
    NQT = (OWN + 511) // 512
    NFT = HALF // 512
    LNS = float(np.log(128.0 ** -0.5))
    ASC = float(96.0 ** -0.5)

    nc = bass.Bass("TRN2", target_bir_lowering=False)
    K = Ctx(nc)
    STOP = int(os.environ.get("KSTOP", "99"))
    SUB = int(os.environ.get("KSUB", "99"))
    VAR = int(os.environ.get("KVAR", "0"))
    P = K.P

    def din(name, shape):
        return nc.dram_tensor(name, list(shape), F32, kind="ExternalInput").ap()

    xs = din("xs", [S, D])
    cvec = din("cvec", [D])
    norm1_g = din("norm1_g", [D])
    w_ada = din("w_ada", [D, 6 * D])
    b_ada = din("b_ada", [6 * D])
    w_in = din("w_in", [D, NIN])
    b_gate = din("b_gate", [16])
    q_norm_g = din("q_norm_g", [384])
    kv_norm_g = din("kv_norm_g", [256])
    w_uq = din("w_uq", [384, 768])
    w_ukv = din("w_ukv", [256, 1024])
    conv_m_w = din("conv_m_w", [3, 1024])
    conv_m_b = din("conv_m_b", [1024])
    mh_norm_g = din("mh_norm_g", [512])
    w_out = din("w_out", [1024, 1024])
    norm2_g = din("norm2_g", [D])
    w_up = din("w_up", [D, 2 * DFF])
    conv_f_w = din("conv_f_w", [3, 2 * DFF])
    conv_f_b = din("conv_f_b", [2 * DFF])
    w_down = din("w_down", [DFF, D])
    final_g = din("final_g", [D])
    ident_d = din("ident", [128, 128])
    cos_d = din("cos_t", [32, S])
    sin_d = din("sin_t", [32, S])
    tri_d = din("tri", [2, 128, 128])
    mneg_d = din("mneg", [2, 128, 128])
    y_out = nc.dram_tensor("y", [HALF, D], F32, kind="ExternalOutput").ap()

    def dscr(name, shape, dt):
        return nc.dram_tensor(name, list(shape), dt).ap()

    modrow = dscr("modrow", [6 * D], F32)
    KTs = dscr("KTs", [128, 8, S], BF16)
    VAs = dscr("VAs", [8, 128, NCH, 66], BF16)
    QTs = dscr("QTs", [128, 8, NQT * 512], BF16)
    MQT = dscr("MQT", [128, 4, S + 512], BF16)
    MKT = dscr("MKT", [128, 4, S + 512], BF16)
    MVA = dscr("MVA", [NCH, 128, 4, 130], BF16)
    MG = dscr("MG", [NCH, 128, 16], F32)
    SO = dscr("SO", [NT * 4, 128, 512], F32)
    HA = dscr("HA", [OWNC, 128, 512], F32)
    ATs = dscr("ATs", [4, 128, NQT * 512], BF16)
    MEMT = dscr("MEMT", [4, 128, OWN], BF16)
    X1 = dscr("X1", [OWN, D], F32)
    H2T = dscr("H2T", [8, 128, OWN], BF16)

    banks = [nc.alloc_psum_tensor("bank%d" % i, [128, 512], F32) for i in range(8)]

    identf = K.sb([128, 128], F32)
    identb = K.sb([128, 128], BF16)
    onesf = K.sb([128, 128], F32)
    modT = K.sb([128, 48], F32)
    A1 = K.sb([128, 8], F32)
    A2 = K.sb([128, 8], F32)
    g12 = K.sb([128, 2, D], F32)
    PERSIST_END = K.off

    K.dma("sp", identf[:], ident_d[:, :], [], ["identf"], "ld0")
    K.cp("dve", identb[:], identf[:], ["identf"], ["identb"])
    K.memset("pool", onesf[:], 1.0, ["onesf"])
    cT = K.sb([128, 8], F32)
    K.dma("sp", cT[:], cvec.rearrange("(c p) -> p c", p=128), [], ["cT"], "ld1", slow=True)
    csil = K.sb([128, 8], F32)
    K.act(csil[:], cT[:], AF.Silu, ["cT"], ["csil"])
    crep = K.sb([128, 8, 128], F32)
    K.cp("dve", crep[:], csil[:].unsqueeze(2).to_broadcast([128, 8, 128]), ["csil"], ["crep"])
    badab = K.sb([128, 6 * D], F32)
    K.dma("sp", badab[:], b_ada.partition_broadcast(128), [], ["badab"], "ld2")
    modbc = K.sb([128, 6 * D], F32)
    wst = [K.sb([128, 8, 512], F32) for _ in range(2)]
    for ct in range(12):
        sl = ct % 2
        K.dma("sp", wst[sl][:], w_ada[:, ct * 512:(ct + 1) * 512].rearrange("(c p) n -> p c n", p=128),
              [], ["wst%d" % sl], "ldw%d" % sl)
        bk = banks[ct % 2]
        for kc in range(8):
            K.mm(bk[:], crep[:, kc, :], wst[sl][:, kc, :], kc == 0, kc == 7,
                 ["crep", "wst%d" % sl], ["bk%d" % (ct % 2)])
        K.tt("dve", modbc[:, ct * 512:(ct + 1) * 512], bk[:], badab[:, ct * 512:(ct + 1) * 512], ALU.add,
             ["bk%d" % (ct % 2), "badab"], ["modbc"])
    K.dma("pool", modrow.rearrange("(o n) -> o n", o=1), modbc[0:1, :], ["modbc"], ["modrow"], "st0")
    K.dma("sp", modT[:], modrow.rearrange("(j p) -> p j", p=128), ["modrow"], ["modT"], "ld3", slow=True)
    K.cp("act", g12[:, 0, :], modbc[:, 2 * D:3 * D], ["modbc"], ["g12"])
    K.cp("act", g12[:, 1, :], modbc[:, 5 * D:6 * D], ["modbc"], ["g12"])
    gT = K.sb([128, 2, 8], F32)
    K.dma("sp", gT[:, 0, :], norm1_g.rearrange("(c p) -> p c", p=128), [], ["gT"], "ld4", slow=True)
    K.dma("sp", gT[:, 1, :], norm2_g.rearrange("(c p) -> p c", p=128), [], ["gT"], "ld5", slow=True)
    K.stt("dve", A1[:], modT[:, 8:16], 1.0, gT[:, 0, :], ALU.add, ALU.mult, ["modT", "gT"], ["A1"])
    K.stt("dve", A2[:], modT[:, 32:40], 1.0, gT[:, 1, :], ALU.add, ALU.mult, ["modT", "gT"], ["A2"])
    B1 = modT[:, 0:8]
    B2 = modT[:, 24:32]
    P.barrier()

    if STOP <= 0:
        P.emit()
        return nc
    K.off = PERSIST_END
    winb = K.sb([128, 8, WINC], BF16)
    wuqb = K.sb([128, 3, 8, 256], BF16)
    wukb = K.sb([128, 2, 8, 128], BF16)
    wuvb = K.sb([128, 2, 8, 64], BF16)
    gq = K.sb([128, 3], F32)
    gkv = K.sb([128, 2], F32)
    cmw = K.sb([128, 3, 8], F32)
    cmb = K.sb([128, 8], F32)
    bgb = K.sb([128, 16], F32)
    off_a = K.off
    stg = [K.sb([128, NIN], F32) for _ in range(2)]
    K.dma("sp", gq[:], q_norm_g.rearrange("(c p) -> p c", p=128), [], ["gq"], "ld0", slow=True)
    K.dma("sp", gkv[:], kv_norm_g.rearrange("(c p) -> p c", p=128), [], ["gkv"], "ld1", slow=True)
    K.memset("pool", winb[:, :, NIN:WINC], 0.0, ["winb"])
    K.memset("pool", wuqb[:], 0.0, ["wuqb"])
    K.memset("pool", wukb[:], 0.0, ["wukb"])
    cengs = ["dve", "pool", "act"]
    for kc in range(8):
        sl = kc % 2
        K.dma("sp", stg[sl][:], w_in[kc * 128:(kc + 1) * 128, :], [], ["stg%d" % sl], "ldw%d" % sl)
        K.cp(cengs[kc % 3], winb[:, kc, 0:NIN], stg[sl][:], ["stg%d" % sl], ["winb"])
        K.ts("dve", winb[:, kc, ROT0:ROT0 + 16], stg[sl][:, KR0 + 16:KR0 + 32], -1.0, None, ALU.mult, None,
             ["stg%d" % sl], ["winb"])
        K.cp("dve", winb[:, kc, ROT0 + 16:ROT0 + 32], stg[sl][:, KR0:KR0 + 16], ["stg%d" % sl], ["winb"])
    for kc in range(3):
        sl = kc % 2
        K.dma("sp", stg[sl][:, 0:768], w_uq[kc * 128:(kc + 1) * 128, :], [], ["stg%d" % sl], "ldw%d" % sl)
        src = stg[sl][:, 0:768].rearrange("p (h e) -> p h e", h=8)
        g = gq[:, kc:kc + 1]
        rs, ws = ["stg%d" % sl, "gq"], ["wuqb"]
        K.ts("dve", wuqb[:, kc, :, 0:32], src[:, :, 64:96], g, None, ALU.mult, None, rs, ws)
        K.ts("pool", wuqb[:, kc, :, 32:96], src[:, :, 0:64], g, None, ALU.mult, None, rs, ws)
        K.ts("dve", wuqb[:, kc, :, 128:144], src[:, :, 80:96], g, -1.0, ALU.mult, ALU.mult, rs, ws)
        K.ts("dve", wuqb[:, kc, :, 144:160], src[:, :, 64:80], g, None, ALU.mult, None, rs, ws)
    for kc in range(2):
        sl = (kc + 1) % 2
        K.dma("sp", stg[sl][:, 0:1024], w_ukv[kc * 128:(kc + 1) * 128, :], [], ["stg%d" % sl], "ldw%d" % sl)
        src = stg[sl][:, 0:1024].rearrange("p (h e) -> p h e", h=8)
        g = gkv[:, kc:kc + 1]
        rs = ["stg%d" % sl, "gkv"]
        K.ts("dve", wukb[:, kc, :, 32:96], src[:, :, 0:64], g, None, ALU.mult, None, rs, ["wukb"])
        K.ts("pool", wuvb[:, kc, :, :], src[:, :, 64:128], g, None, ALU.mult, None, rs, ["wuvb"])
    for j in range(3):
        K.dma("sp", cmw[:, j, :], conv_m_w[j].rearrange("(c p) -> p c", p=128), [], ["cmw"], "ld2", slow=True)
    K.dma("sp", cmb[:], conv_m_b.rearrange("(c p) -> p c", p=128), [], ["cmb"], "ld3", slow=True)
    K.dma("sp", bgb[:], b_gate.partition_broadcast(128), [], ["bgb"], "ld4")
    P.barrier()
    K.off = off_a

    xt = [K.sb([128, 4, D], F32) for _ in range(2)]
    xnb = K.sb([128, 4, D], BF16)
    junk = K.sb([128, D], BF16)
    ssq = K.sb([128, 4], F32)
    hT = K.sb([128, 8, 512], BF16)
    cqT = K.sb([128, 3, 512], BF16)
    ckvT = K.sb([128, 2, 512], BF16)
    cst = K.sb([32, 512], F32)
    snt = K.sb([32, 512], F32)
    rt1 = K.sb([32, 512], F32)
    rt2 = K.sb([32, 512], F32)
    krr = K.sb([32, 512], BF16)
    ktile = K.sb([128, 8, 512], BF16)
    qtile = ktile
    vat = K.sb([128, 8, 4, 66], BF16)
    rawb = [K.sb([128, 514], F32) for _ in range(2)]
    cv = [K.sb([128, 512], F32) for _ in range(2)]
    mqk = K.sb([128, 8, 512], BF16)
    mva = K.sb([128, 4, 4, 130], BF16)
    sot = K.sb([128, 4, 512], F32)
    gt = K.sb([128, 4, 16], F32)
    K.memset("pool", vat[:, :, :, 64:66], 0.0, ["vat"])
    K.memset("pool", vat[:, :, :, 64:65], 1.0, ["vat"])
    K.memset("pool", mva[:, :, :, 128:130], 0.0, ["mva"])
    K.memset("pool", mva[:, :, :, 128:129], 1.0, ["mva"])
    carry = K.sb([128, 8, 2], F32)
    K.memset("pool", carry[:], 0.0, ["carry"])

    hT2 = [hT, K.sb([128, 8, 512], BF16)]
    cst2 = [cst, K.sb([32, 512], F32)]
    snt2 = [snt, K.sb([32, 512], F32)]
    ssq2 = [ssq, K.sb([128, 4], F32)]

    def front(ti):
        t0 = ti * 512
        sl = ti % 2
        S_ = "%d" % sl
        K.dma("sp", xt[sl][:], xs[t0:t0 + 512, :].rearrange("(j p) n -> p j n", p=128), [], ["xt" + S_], "ldx" + S_)
        K.dma("sp", cst2[sl][:], cos_d[:, t0:t0 + 512], [], ["cst" + S_], "ldc" + S_)
        K.dma("sp", snt2[sl][:], sin_d[:, t0:t0 + 512], [], ["snt" + S_], "lds" + S_)
        sq = ssq2[sl]
        for j in range(4):
            K.act(junk[:], xt[sl][:, j, :], AF.Square, ["xt" + S_], ["junk", "ssq" + S_], accum=sq[:, j:j + 1])
        K.ts("dve", sq[:], sq[:], 1.0 / D, EPS, ALU.mult, ALU.add, ["ssq" + S_], ["ssq" + S_])
        K.rsqrt(sq[:], "ssq" + S_)
        for j in range(4):
            K.act(xnb[:, j, :], xt[sl][:, j, :], AF.Copy, ["xt" + S_, "ssq" + S_], ["xnb%d" % j], scale=sq[:, j:j + 1])
        for kc in range(8):
            bk = banks[kc % 2]
            pb = bk[:].bitcast(BF16)
            for j in range(4):
                K.tr(pb[:, j * 128:(j + 1) * 128], xnb[:, j, kc * 128:(kc + 1) * 128], identb[:],
                     ["xnb%d" % j, "identb"], ["bk%d" % (kc % 2)])
            if kc % 2 == 0:
                K.ts("dve", hT2[sl][:, kc, :], pb[:, 0:512], A1[:, kc:kc + 1], B1[:, kc:kc + 1], ALU.mult, ALU.add,
                     ["bk%d" % (kc % 2), "A1", "modT"], ["hT%d_%d" % (sl, kc)])
            else:
                K.act(hT2[sl][:, kc, :], pb[:, 0:512], AF.Identity, ["bk%d" % (kc % 2), "A1", "modT"], ["hT%d_%d" % (sl, kc)],
                      bias=B1[:, kc:kc + 1], scale=A1[:, kc:kc + 1])

    lat2 = [K.sb([128, 4, 640], BF16) for _ in range(2)]
    latf2 = [K.sb([128, 640], F32) for _ in range(2)]
    ssl2 = [K.sb([128, 2], F32) for _ in range(2)]
    krr2 = [krr, K.sb([32, 512], BF16)]
    if os.environ.get("KDBG"):
        print("phase A sbuf top", K.off, SBTOP)

    def back1(ti):
        t0 = ti * 512
        own = ti < NTO
        sl = ti % 2
        S_ = "%d" % sl
        hT = hT2[sl]
        cst = cst2[sl]
        snt = snt2[sl]
        hTr = ["hT%d_%d" % (sl, kc) for kc in range(8)]
        for j in range(4):
            tsl = slice(j * 128, (j + 1) * 128)
            lf_ = latf2[j % 2]
            lfk = "latf%d" % (j % 2)
            sk = "ssl%d" % (j % 2)
            ss = ssl2[j % 2]
            lk = "lat%d_%d" % (sl, j)
            needq = ti < NQT
            q0 = 0 if needq else 384
            for (bi, c0, c1, o0) in ((2, q0, 512, q0), (3, 512, 640, 0)):
                for kc in range(8):
                    K.mm(banks[bi][:, o0:o0 + c1 - c0], hT[:, kc, tsl], winb[:, kc, c0:c1], kc == 0, kc == 7,
                         hTr + ["winb"], ["bk%d" % bi])
            K.act(lf_[:, q0:512], banks[2][:, q0:512], AF.Identity, ["bk2"], [lfk])
            K.act(lf_[:, 512:640], banks[3][:, 0:128], AF.Identity, ["bk3"], [lfk])
            for kc in range(8):
                K.mm(banks[6][:], hT[:, kc, tsl], winb[:, kc, MV0:MV0 + 512], kc == 0, kc == 7, hTr + ["winb"], ["bk6"])
            K.cp("act", mva[:, j, :, 0:128], banks[6][:].rearrange("p (h e) -> p h e", h=4), ["bk6"], ["mva"])
            for kc in range(8):
                K.mm(banks[7][:, 0:16], hT[:, kc, tsl], winb[:, kc, G0:G0 + 16], kc == 0, kc == 7, hTr + ["winb"], ["bk7"])
            K.tt("dve", gt[:, j, :], banks[7][:, 0:16], bgb[:], ALU.add, ["bk7", "bgb"], ["gt"])
            if own:
                for kc in range(8):
                    K.mm(banks[5][:], hT[:, kc, tsl], winb[:, kc, MO0:MO0 + 512], kc == 0, kc == 7, hTr + ["winb"], ["bk5"])
                K.act(sot[:, j, :], banks[5][:], AF.Sigmoid, ["bk5"], ["sot"])
            if needq:
                K.act(junk[:, 0:384], lf_[:, 0:384], AF.Square, [lfk], ["junk", sk], accum=ss[:, 0:1])
            else:
                K.memset("pool", ss[:, 0:1], 1.0, [sk])
            K.act(junk[:, 384:640], lf_[:, 384:640], AF.Square, [lfk], ["junk", sk], accum=ss[:, 1:2])
            K.ts("dve", ss[:, 0:1], ss[:, 0:1], 1.0 / 384, EPS, ALU.mult, ALU.add, [sk], [sk])
            K.ts("dve", ss[:, 1:2], ss[:, 1:2], 1.0 / 256, EPS, ALU.mult, ALU.add, [sk], [sk])
            K.rsqrt(ss[:], sk)
            if needq:
                K.ts("dve", lat2[sl][:, j, 0:384], lf_[:, 0:384], ss[:, 0:1], None, ALU.mult, None, [lfk, sk], [lk])
            K.ts("pool", lat2[sl][:, j, 384:640], lf_[:, 384:640], ss[:, 1:2], None, ALU.mult, None, [lfk, sk], [lk])
        gv = gt[:].rearrange("p j (d g h) -> p j d g h", d=2, g=2)
        fcols = gv[:, :, :, 1, :]
        K.act(fcols, fcols, AF.Exp, ["gt"], ["gt"], scale=-1.0)
        K.act(fcols, fcols, AF.Ln, ["gt"], ["gt"], bias=1.0)
        K.ts("dve", fcols, fcols, -1.0, None, ALU.mult, None, ["gt"], ["gt"])
        K.dma("pool", MG[ti * 4:(ti + 1) * 4].rearrange("j p g -> p j g"), gt[:], ["gt"], ["MG"], "st1")
        K.dma("pool", MVA[ti * 4:(ti + 1) * 4].rearrange("j p h e -> p j h e"), mva[:], ["mva"], ["MVA"], "st2")
        if own:
            K.dma("pool", SO[ti * 4:(ti + 1) * 4].rearrange("j p n -> p j n"), sot[:], ["sot"], ["SO"], "st4")
        for kc in range(8):
            K.mm(banks[0][:], winb[:, kc, KR0:KR0 + 128], hT[:, kc, :], kc == 0, kc == 7, hTr + ["winb"], ["bk0"])
        for kc in range(8):
            K.mm(banks[1][:], winb[:, kc, ROT0:ROT0 + 128], hT[:, kc, :], kc == 0, kc == 7, hTr + ["winb"], ["bk1"])
        K.tt("dve", rt1[:], banks[0][0:32, :], cst[:], ALU.mult, ["bk0", "cst" + S_], ["rt1"])
        K.tt("dve", rt2[:], banks[1][0:32, :], snt[:], ALU.mult, ["bk1", "snt" + S_], ["rt2"])
        K.tt("pool", krr2[sl][:], rt1[:], rt2[:], ALU.add, ["rt1", "rt2"], ["krr" + S_])

    def conv_part(ti):
        t0 = ti * 512
        last = ti == NT
        sl = ti % 2
        hT = hT2[sl]
        hTr = ["hT%d_%d" % (sl, kc) for kc in range(8)]
        for c in range(8):
            if c < 4 and not (ti < NTO + 1):
                continue
            rb = rawb[c % 2]
            rk = "rawb%d" % (c % 2)
            K.cp("pool", rb[:, 0:2], carry[:, c, :], ["carry%d" % c], [rk])
            if not last:
                bk = banks[6 + (c % 2)]
                for kc in range(8):
                    K.mm(bk[:], winb[:, kc, MQK0 + c * 128:MQK0 + (c + 1) * 128], hT[:, kc, :], kc == 0, kc == 7,
                         hTr + ["winb"], ["bk%d" % (6 + c % 2)])
                K.cp("act", rb[:, 2:514], bk[:], ["bk%d" % (6 + c % 2)], [rk])
            else:
                K.memset("pool", rb[:, 2:514], 0.0, [rk])
            K.cp("pool", carry[:, c, :], rb[:, 512:514], [rk], ["carry%d" % c])
            cb = cv[c % 2]
            ck = "cv%d" % (c % 2)
            K.ts("dve", cb[:], rb[:, 1:513], cmw[:, 1, c:c + 1], None, ALU.mult, None, [rk, "cmw"], [ck])
            K.stt("dve", cb[:], rb[:, 0:512], cmw[:, 0, c:c + 1], cb[:], ALU.mult, ALU.add, [rk, "cmw", ck], [ck])
            K.stt("dve", cb[:], rb[:, 2:514], cmw[:, 2, c:c + 1], cb[:], ALU.mult, ALU.add, [rk, "cmw", ck], [ck])
            K.act(mqk[:, c, :], cb[:], AF.Silu, [ck, "cmb"], ["mqk%d" % c], bias=cmb[:, c:c + 1])
        if ti < NTO + 1:
            K.dma("pool", MQT[:, :, t0:t0 + 512], mqk[:, 0:4, :], ["mqk%d" % c for c in range(4)], ["MQT"], "st7")
        K.dma("pool", MKT[:, :, t0:t0 + 512], mqk[:, 4:8, :], ["mqk%d" % c for c in range(4, 8)], ["MKT"], "st8")

    def back2(ti):
        t0 = ti * 512
        sl = ti % 2
        S_ = "%d" % sl
        cst = cst2[sl]
        snt = snt2[sl]
        for j in range(4):
            tsl = slice(j * 128, (j + 1) * 128)
            lk = "lat%d_%d" % (sl, j)
            lt = lat2[sl]
            pb = banks[4][:].bitcast(BF16)
            pb2 = banks[1][:].bitcast(BF16)
            if ti < NQT:
                for c in range(3):
                    K.tr(pb[:, c * 128:(c + 1) * 128], lt[:, j, c * 128:(c + 1) * 128], identb[:], [lk, "identb"], ["bk4"])
                K.cp("dve", cqT[:, :, tsl], pb[:, 0:384].rearrange("p (c t) -> p c t", c=3), ["bk4"], ["cqT"])
            for c in range(2):
                K.tr(pb2[:, c * 128:(c + 1) * 128], lt[:, j, (3 + c) * 128:(4 + c) * 128], identb[:], [lk, "identb"], ["bk1"])
            K.cp("act", ckvT[:, :, tsl], pb2[:, 0:256].rearrange("p (c t) -> p c t", c=2), ["bk1"], ["ckvT"])
            for kc in range(2):
                K.mm(banks[5][:], ckvT[:, kc, tsl], wuvb[:, kc, :, :].rearrange("p h e -> p (h e)"), kc == 0, kc == 1,
                     ["ckvT", "wuvb"], ["bk5"])
            K.cp("act", vat[:, :, j, 0:64], banks[5][:].rearrange("p (h e) -> p h e", h=8), ["bk5"], ["vat"])
        K.dma("pool", VAs[:, :, ti * 4:(ti + 1) * 4, :].rearrange("h p j e -> p h (j e)"), vat[:].rearrange("p h j e -> p h (j e)"), ["vat"], ["VAs"], "st3")
        for h in range(8):
            bk = banks[2 + (h % 2)]
            for kc in range(2):
                K.mm(bk[:], wukb[:, kc, h, :], ckvT[:, kc, :], kc == 0, kc == 1, ["ckvT", "wukb"], ["bk%d" % (2 + h % 2)])
            if h % 2 == 0:
                K.cp("act", ktile[:, h, :], bk[:, :], ["bk%d" % (2 + h % 2)], ["ktile"])
            else:
                K.cp("dve", ktile[:, h, :], bk[:, :], ["bk%d" % (2 + h % 2)], ["ktile"])
        K.cp("pool", ktile[0:32, :, :], krr2[sl][:].unsqueeze(1).to_broadcast([32, 8, 512]), ["krr" + S_, "ktile"], ["ktile"])
        K.dma("pool", KTs[:, :, t0:t0 + 512], ktile[:], ["ktile"], ["KTs"], "st5")
        if ti < NQT:
            for h in range(8):
                ba, bb = banks[4], banks[5]
                for kc in range(3):
                    K.mm(ba[:], wuqb[:, kc, h, 0:128], cqT[:, kc, :], kc == 0, kc == 2, ["cqT", "wuqb"], ["bk4"])
                for kc in range(3):
                    K.mm(bb[:], wuqb[:, kc, h, 128:256], cqT[:, kc, :], kc == 0, kc == 2, ["cqT", "wuqb"], ["bk5"])
                K.tt("dve", rt1[:], ba[0:32, :], cst[:], ALU.mult, ["bk4", "cst" + S_], ["rt1"])
                K.tt("dve", rt2[:], bb[0:32, :], snt[:], ALU.mult, ["bk5", "snt" + S_], ["rt2"])
                K.cp("dve", qtile[:, h, :], ba[:, :], ["bk4"], ["ktile"])
                K.tt("pool", qtile[0:32, h, :], rt1[:], rt2[:], ALU.add, ["rt1", "rt2", "ktile"], ["ktile"])
            K.dma("pool", QTs[:, :, t0:t0 + 512], qtile[:], ["ktile"], ["QTs"], "st6")

    front(0)
    back1(0)
    conv_part(0)
    for ti in range(NT):
        if ti + 1 < NT:
            front(ti + 1)
            back1(ti + 1)
            conv_part(ti + 1)
        back2(ti)
        if ti + 2 < NT:
            pass
    conv_part(NT)
    P.barrier()

    if STOP <= 1:
        P.emit()
        return nc
    K.off = PERSIST_END
    kth = [K.sb([128, S], BF16) for _ in range(2)]
    vah = [K.sb([128, NCH * 66 + 64], BF16) for _ in range(2)]
    qb = [K.sb([128, 512], BF16) for _ in range(2)]
    pt = [K.sb([128, 512], BF16) for _ in range(4)]
    osb = K.sb([128, 512], F32)
    rden = K.sb([128, 512], F32)
    atb = [K.sb([64, 512], BF16) for _ in range(2)]
    sel = K.sb([128, 128], F32)
    K.memset("pool", sel[:], 0.0, ["sel"])
    K.memset("pool", sel[64:65, :], 1.0, ["sel"])
    K.memset("pool", rden[:], 0.0, ["rden0"])
    for sl in range(2):
        K.memset("pool", vah[sl][:, NCH * 66:NCH * 66 + 64], 0.0, ["vah%d" % sl])
    osb2 = [osb, K.sb([128, 512], F32)]
    rden2 = [rden, K.sb([128, 512], F32)]
    K.memset("pool", rden2[1][:], 0.0, ["rden1"])
    its = []
    for h in range(8):
        for qi in range(NQT):
            qn = min(512, OWN - qi * 512)
            for kt in range(NCH):
                its.append((h, qi, kt, qn))
    LOOK = 3
    tails = []

    def tail_fn(h, qi, qs, qn, tslot):
        def fn():
            K.mm(banks[6][:, 0:qn], sel[:], rden2[tslot][:, 0:qn], True, True, ["sel", "rden%d" % tslot], ["bk6"])
            K.tt("dve", atb[qs][:, 0:qn], osb2[tslot][0:64, 0:qn], banks[6][0:64, 0:qn], ALU.mult,
                 ["osb%d" % tslot, "bk6"], ["atb%d" % qs])
            K.dma("pool", ATs[h // 2, (h % 2) * 64:(h % 2) * 64 + 64, qi * 512:qi * 512 + qn], atb[qs][:, 0:qn],
                  ["atb%d" % qs], ["ATs"], "sta%d" % qs)
        return fn

    ntile = 0
    for step in range(len(its) + LOOK):
        if step < len(its):
            h, qi, kt, qn = its[step]
            hs = h % 2
            qs = (h * NQT + qi) % 2
            if qi == 0 and kt == 0:
                K.dma("sp", kth[hs][:], KTs[:, h, :], ["KTs"], ["kth%d" % hs], "ldk%d" % hs)
                K.dma("sp", vah[hs][:, 0:NCH * 66], VAs[h].rearrange("p j e -> p (j e)"), ["VAs"], ["vah%d" % hs], "ldv%d" % hs)
            if kt == 0:
                K.dma("sp", qb[qs][:, 0:qn], QTs[:, h, qi * 512:qi * 512 + qn], ["QTs"], ["qb%d" % qs], "ldq%d" % qs)
            sbk = step % 4
            K.mm(banks[sbk][:, 0:qn], kth[hs][:, kt * 128:(kt + 1) * 128], qb[qs][:, 0:qn], True, True,
                 ["kth%d" % hs, "qb%d" % qs], ["bk%d" % sbk])
            K.act(pt[sbk][:, 0:qn], banks[sbk][:, 0:qn], AF.Exp, ["bk%d" % sbk], ["pt%d" % sbk], scale=ASC)
        for (at, fn) in [t for t in tails if t[0] == step]:
            fn()
        tails = [t for t in tails if t[0] != step]
        j = step - LOOK
        if j >= 0:
            h, qi, kt, qn = its[j]
            hs = h % 2
            qs = (h * NQT + qi) % 2
            sbk = j % 4
            ob = banks[4 + qs]
            okey = "bk%d" % (4 + qs)
            K.mm(ob[:, 0:qn], vah[hs][:, kt * 66:kt * 66 + 128], pt[sbk][:, 0:qn], kt == 0, kt == NCH - 1,
                 ["vah%d" % hs, "pt%d" % sbk], [okey])
            if kt == NCH - 1:
                tslot = ntile % 2
                ntile += 1
                K.cp("dve", osb2[tslot][0:65, 0:qn], ob[0:65, 0:qn], [okey], ["osb%d" % tslot])
                K.recip(rden2[tslot][64:65, 0:qn], osb2[tslot][64:65, 0:qn], ["osb%d" % tslot], ["rden%d" % tslot])
                tails.append((step + 2, tail_fn(h, qi, qs, qn, tslot)))
    for (at, fn) in tails:
        fn()
    P.barrier()

    if STOP <= 2:
        P.emit()
        return nc
    K.off = PERSIST_END
    trif = K.sb([128, 2, 128], F32)
    K.dma("sp", trif[:], tri_d.rearrange("d s t -> s d t"), [], ["trif"], "ld0")
    gmh = K.sb([128, 512], F32)
    K.dma("sp", gmh[:], mh_norm_g.partition_broadcast(128), [], ["gmh"], "ld2")
    Cf = K.sb([128, 4, 128], F32)
    Cb = K.sb([128, 4, 128], BF16)
    Cn = K.sb([128, 4], F32)
    Cnb = K.sb([128, 4], BF16)
    qTt = [K.sb([128, 4, 128], BF16) for _ in range(3)]
    kTt = [K.sb([128, 4, 128], BF16) for _ in range(3)]
    vat2 = [K.sb([128, 4, 130], BF16) for _ in range(3)]
    gtt = [K.sb([128, 16], F32) for _ in range(3)]
    hat = [K.sb([128, 512], F32) for _ in range(3)]
    sot2 = [K.sb([128, 512], F32) for _ in range(3)]
    bb8 = [K.sb([128, 8], F32) for _ in range(3)]
    g4 = [K.sb([128, 4], F32) for _ in range(3)]
    egs4 = [K.sb([128, 4], F32) for _ in range(3)]
    ws4 = [K.sb([128, 4], F32) for _ in range(3)]
    dc4 = [K.sb([128, 4], F32) for _ in range(3)]
    wq4 = [K.sb([128, 4], F32) for _ in range(3)]
    dd8 = [K.sb([128, 8], F32) for _ in range(3)]
    den4 = [K.sb([128, 4], F32) for _ in range(3)]
    un4 = [K.sb([128, 4], F32) for _ in range(3)]
    TL4 = [K.sb([128, 4, 128], F32) for _ in range(3)]
    EM4 = [K.sb([128, 4, 128], F32) for _ in range(3)]
    WT4 = [K.sb([128, 4, 128], F32) for _ in range(3)]
    PT4 = [K.sb([128, 4, 128], BF16) for _ in range(3)]
    KW4 = [K.sb([128, 4, 128], BF16) for _ in range(3)]
    tmpi4 = [K.sb([128, 4, 128], F32) for _ in range(3)]
    nd4 = [K.sb([128, 4, 128], F32) for _ in range(3)]
    hout = [K.sb([128, 512], F32) for _ in range(3)]
    st4 = K.sb([128, 4], F32)
    hc = K.sb([128, 512], F32)
    hsq = K.sb([128, 512], F32)
    hnb = K.sb([128, 512], BF16)
    memt = [K.sb([128, 4, 128], BF16) for _ in range(3)]
    lnsb = K.sb([128, 1], F32)
    K.memset("pool", lnsb[:], LNS, ["lnsb"])

    def bc_t(ap4):
        return ap4.unsqueeze(2).to_broadcast([128, 4, 128])

    def stage_E(dirn, c, full, sl):
        S_ = "%d" % sl
        t1 = c * 128 + 1
        K.dma("sp", kTt[sl][:], MKT[:, :, t1:t1 + 128], ["MKT"], ["kTt" + S_], "lck" + S_)
        yield
        K.dma("sp", vat2[sl][:], MVA[c], ["MVA"], ["vat2" + S_], "lcv" + S_)
        yield
        K.dma("sp", gtt[sl][:], MG[c], ["MG"], ["gtt" + S_], "lcg" + S_)
        yield
        if full:
            K.dma("sp", qTt[sl][:], MQT[:, :, t1:t1 + 128], ["MQT"], ["qTt" + S_], "lcq" + S_)
            yield
            if dirn == 1:
                K.dma("sp", hat[sl][:], HA[c], ["HA"], ["hat" + S_], "lch" + S_)
                K.dma("sp", sot2[sl][:], SO[c], ["SO"], ["sot2" + S_], "lcs" + S_)
        li4 = gtt[sl][:, 8 * dirn:8 * dirn + 4]
        lf4 = gtt[sl][:, 8 * dirn + 4:8 * dirn + 8]
        gk = "gtt" + S_
        b4 = bb8[sl][:, 0:4]
        bl4 = bb8[sl][:, 4:8]
        K.mm(banks[0][:, 0:4], trif[:, dirn, :], lf4, True, True, ["trif", gk], ["bk0"])
        yield
        K.mm(banks[0][:, 4:8], onesf[:], lf4, True, True, ["onesf", gk], ["bk0"])
        yield
        K.cp("dve", bb8[sl][:], banks[0][:, 0:8], ["bk0"], ["bb8" + S_])
        yield
        K.tt("dve", g4[sl][:], li4, b4, ALU.subtract, [gk, "bb8" + S_], ["g4" + S_])
        yield
        K.tt("dve", ws4[sl][:], g4[sl][:], bl4, ALU.add, ["g4" + S_, "bb8" + S_], ["ws4" + S_])
        yield
        K.act(ws4[sl][:], ws4[sl][:], AF.Exp, ["ws4" + S_], ["ws4" + S_])
        yield
        K.act(dc4[sl][:], bl4, AF.Exp, ["bb8" + S_], ["dc4" + S_])
        yield
        pbk = banks[5][:].bitcast(BF16)
        for hd in range(4):
            K.tr(pbk[:, hd * 128:(hd + 1) * 128], kTt[sl][:, hd, :], identb[:], ["kTt" + S_, "identb"], ["bk5"])
        K.tt("dve", KW4[sl][:], pbk[:, 0:512].rearrange("p (h d) -> p h d", h=4), bc_t(ws4[sl][:]), ALU.mult,
             ["bk5", "ws4" + S_], ["KW4" + S_])
        yield
        if full:
            K.act(egs4[sl][:], g4[sl][:], AF.Exp, ["g4" + S_, "lnsb"], ["egs4" + S_], bias=lnsb[:])
            yield
            K.act(wq4[sl][:], b4, AF.Exp, ["bb8" + S_, "lnsb"], ["wq4" + S_], bias=lnsb[:])
            yield
            K.tt("pool", TL4[sl][:], trif[:, dirn, :].unsqueeze(1).to_broadcast([128, 4, 128]), bc_t(lf4), ALU.mult,
                 ["trif", gk], ["TL4" + S_])
            yield
            K.tt("pool", EM4[sl][:], trif[:, dirn, :].unsqueeze(1).to_broadcast([128, 4, 128]), bc_t(egs4[sl][:]), ALU.mult,
                 ["trif", "egs4" + S_], ["EM4" + S_])
            yield
            for hd in range(4):
                K.mm(banks[2][:, hd * 128:(hd + 1) * 128], kTt[sl][:, hd, :], qTt[sl][:, hd, :], True, True,
                     ["kTt" + S_, "qTt" + S_], ["bk2"])
            K.mm(banks[1][:], onesf[:], TL4[sl][:].rearrange("p h t -> p (h t)"), True, True, ["onesf", "TL4" + S_], ["bk1"])
            yield
            K.act(WT4[sl][:].rearrange("p h t -> p (h t)"), banks[1][:], AF.Exp, ["bk1"], ["WT4" + S_])
            yield
            K.tt("pool", WT4[sl][:], WT4[sl][:], EM4[sl][:], ALU.mult, ["WT4" + S_, "EM4" + S_], ["WT4" + S_])
            yield
            K.tt("dve", PT4[sl][:].rearrange("p h t -> p (h t)"), banks[2][:], WT4[sl][:].rearrange("p h t -> p (h t)"),
                 ALU.mult, ["bk2", "WT4" + S_], ["PT4" + S_])
            yield

    def stage_M(dirn, c, full, sl):
        S_ = "%d" % sl
        if full:
            for hd in range(4):
                K.mm(banks[4][:, hd * 128:(hd + 1) * 128], qTt[sl][:, hd, :], Cb[:, hd, :], True, True,
                     ["qTt" + S_, "Cb"], ["bk4"])
            for hd in range(4):
                K.mm(banks[7][:, 4 + hd:5 + hd], qTt[sl][:, hd, :], Cnb[:, hd:hd + 1], True, True,
                     ["qTt" + S_, "Cnb"], ["bk7"])
            K.tt("dve", tmpi4[sl][:], banks[4][:].rearrange("p (h e) -> p h e", h=4), bc_t(wq4[sl][:]), ALU.mult,
                 ["bk4", "wq4" + S_], ["tmpi4" + S_])
            K.cp("dve", dd8[sl][:, 4:8], banks[7][:, 4:8], ["bk7"], ["ddq" + S_])
        for hd in range(4):
            K.mm(banks[6][:, hd * 128:(hd + 1) * 128], KW4[sl][:, hd, :], vat2[sl][:, hd, 0:128], True, True,
                 ["KW4" + S_, "vat2" + S_], ["bk6"])
        for hd in range(4):
            K.mm(banks[7][:, 8 + hd:9 + hd], KW4[sl][:, hd, :], vat2[sl][:, hd, 128:129], True, True,
                 ["KW4" + S_, "vat2" + S_], ["bk7"])
        K.tt("pool", Cf[:], Cf[:], bc_t(dc4[sl][:]), ALU.mult, ["Cf", "dc4" + S_, "Cb"], ["Cf"])
        K.tt("dve", Cf[:].rearrange("p h e -> p (h e)"), Cf[:].rearrange("p h e -> p (h e)"), banks[6][:], ALU.add,
             ["Cf", "bk6"], ["Cf"])
        K.cp("act", Cb[:], Cf[:], ["Cf"], ["Cb"])
        K.cp("dve", un4[sl][:], banks[7][:, 8:12], ["bk7"], ["un4" + S_])
        K.tt("dve", Cn[:], Cn[:], dc4[sl][:], ALU.mult, ["Cn", "dc4" + S_], ["Cn"])
        K.tt("dve", Cn[:], Cn[:], un4[sl][:], ALU.add, ["Cn", "un4" + S_], ["Cn"])
        K.cp("dve", Cnb[:], Cn[:], ["Cn"], ["Cnb"])

    def stage_L(dirn, c, full, sl):
        S_ = "%d" % sl
        if not full:
            return
        yield
        for hd in range(4):
            K.mm(banks[3][:, hd * 128:(hd + 1) * 128], PT4[sl][:, hd, :], vat2[sl][:, hd, 0:128], True, True,
                 ["PT4" + S_, "vat2" + S_], ["bk3"])
            yield
        for hd in range(4):
            K.mm(banks[7][:, hd:hd + 1], PT4[sl][:, hd, :], vat2[sl][:, hd, 128:129], True, True,
                 ["PT4" + S_, "vat2" + S_], ["bk7"])
            yield
        K.tt("dve", nd4[sl][:].rearrange("p h e -> p (h e)"), banks[3][:], tmpi4[sl][:].rearrange("p h e -> p (h e)"),
             ALU.add, ["bk3", "tmpi4" + S_], ["nd4" + S_])
        yield
        K.cp("dve", dd8[sl][:, 0:4], banks[7][:, 0:4], ["bk7"], ["ddi" + S_])
        yield
        K.tt("dve", den4[sl][:], dd8[sl][:, 4:8], wq4[sl][:], ALU.mult, ["ddq" + S_, "wq4" + S_], ["den4" + S_])
        yield
        K.tt("dve", den4[sl][:], den4[sl][:], dd8[sl][:, 0:4], ALU.add, ["den4" + S_, "ddi" + S_], ["den4" + S_])
        yield
        K.act(den4[sl][:], den4[sl][:], AF.Abs, ["den4" + S_], ["den4" + S_])
        yield
        K.ts("dve", den4[sl][:], den4[sl][:], 1.0, None, ALU.max, None, ["den4" + S_], ["den4" + S_])
        yield
        K.recip(den4[sl][:], den4[sl][:], ["den4" + S_], ["den4" + S_])
        yield
        K.tt("pool", hout[sl][:].rearrange("p (h e) -> p h e", h=4), nd4[sl][:], bc_t(den4[sl][:]), ALU.mult,
             ["nd4" + S_, "den4" + S_], ["hout" + S_])
        yield
        if dirn == 0:
            K.dma("pool", HA[c], hout[sl][:], ["hout" + S_], ["HA"], "sth" + S_)
            yield
        else:
            hs_ = hout[sl]
            K.tt("dve", hs_[:], hs_[:], hat[sl][:], ALU.add, ["hout" + S_, "hat" + S_], ["hout" + S_])
            yield
            hv = hs_[:].rearrange("p (h e) -> p h e", h=4)
            K.P.add("dve", lambda e, hv=hv: e.tensor_reduce(out=st4[:], in_=hv, axis=AX.X, op=ALU.add),
                    ["hout" + S_], ["st4"])
            yield
            K.ts("dve", st4[:], st4[:], 1.0 / 128, None, ALU.mult, None, ["st4"], ["st4"])
            yield
            hcv = hc[:].rearrange("p (h e) -> p h e", h=4)
            K.tt("dve", hcv, hv, bc_t(st4[:]), ALU.subtract, ["hout" + S_, "st4"], ["hc"])
            yield
            K.tt("pool", hsq[:], hc[:], hc[:], ALU.mult, ["hc"], ["hsq"])
            yield
            hqv = hsq[:].rearrange("p (h e) -> p h e", h=4)
            K.P.add("dve", lambda e, hqv=hqv: e.tensor_reduce(out=st4[:], in_=hqv, axis=AX.X, op=ALU.add),
                    ["hsq"], ["st4"])
            yield
            K.ts("dve", st4[:], st4[:], 1.0 / 128, EPS, ALU.mult, ALU.add, ["st4"], ["st4"])
            yield
            K.rsqrt(st4[:], "st4")
            K.tt("pool", hcv, hcv, bc_t(st4[:]), ALU.mult, ["hc", "st4"], ["hc"])
            yield
            K.tt("pool", hc[:], hc[:], gmh[:], ALU.mult, ["hc", "gmh"], ["hc"])
            yield
            K.tt("dve", hnb[:], hc[:], sot2[sl][:], ALU.mult, ["hc", "sot2" + S_], ["hnb"])
            yield
            pbk7 = banks[0][:].bitcast(BF16)
            for hd in range(4):
                K.tr(pbk7[:, 512 + hd * 128:512 + (hd + 1) * 128], hnb[:, hd * 128:(hd + 1) * 128], identb[:],
                     ["hnb", "identb"], ["bk0"])
            K.cp("act", memt[sl][:], pbk7[:, 512:1024].rearrange("p (h t) -> p h t", h=4), ["bk0"], ["memt" + S_])
            yield
            K.dma("pool", MEMT[:, :, c * 128:(c + 1) * 128].rearrange("h d t -> d h t"), memt[sl][:],
                  ["memt" + S_], ["MEMT"], "stm" + S_)
            yield

    step = 0
    for dirn in range(2):
        K.memset("pool", Cf[:], 0.0, ["Cf"])
        K.memset("pool", Cb[:], 0.0, ["Cb"])
        K.memset("pool", Cn[:], 0.0, ["Cn"])
        K.memset("pool", Cnb[:], 0.0, ["Cnb"])
        if dirn == 0:
            order = [(c, True) for c in range(OWNC)]
        else:
            order = [(c, False) for c in range(NCH - 1, OWNC - 1, -1)] + [(c, True) for c in range(OWNC - 1, -1, -1)]
        sls = [(step + i) % 3 for i in range(len(order))]
        step += len(order)
        n = len(order)
        def run2(ga, gb):
            da = db = False
            while not (da and db):
                if not da:
                    try:
                        next(ga)
                    except StopIteration:
                        da = True
                if not db:
                    try:
                        next(gb)
                    except StopIteration:
                        db = True

        for _ in stage_E(dirn, order[0][0], order[0][1], sls[0]):
            pass
        for i in range(n):
            stage_M(dirn, order[i][0], order[i][1], sls[i])
            gl = stage_L(dirn, order[i][0], order[i][1], sls[i])
            if i + 1 < n:
                run2(stage_E(dirn, order[i + 1][0], order[i + 1][1], sls[i + 1]), gl)
            else:
                for _ in gl:
                    pass
        P.barrier()

    if STOP <= 3:
        P.emit()
        return nc
    K.off = PERSIST_END
    woutb = K.sb([128, 8, D], BF16)
    stg2 = [K.sb([128, D], F32) for _ in range(2)]
    for kc in range(8):
        sl = kc % 2
        K.dma("sp", stg2[sl][:], w_out[kc * 128:(kc + 1) * 128, :], [], ["stg%d" % sl], "ldw%d" % sl)
        K.tt("dve" if kc % 2 == 0 else "pool", woutb[:, kc, :], stg2[sl][:], g12[:, 0, :], ALU.mult,
             ["stg%d" % sl, "g12"], ["woutb"])
    att = [K.sb([128, 4, 128], BF16) for _ in range(2)]
    met = [K.sb([128, 4, 128], BF16) for _ in range(2)]
    xc = [K.sb([128, D], F32) for _ in range(2)]
    x1t = [K.sb([128, D], F32) for _ in range(2)]
    junk2 = K.sb([128, D], F32)
    s1 = K.sb([128, 1], F32)
    xn2 = K.sb([128, D], BF16)
    h2t = [K.sb([128, 8, 128], BF16) for _ in range(2)]
    def d_stage1(c):
        sl = c % 2
        tsl = slice(c * 128, (c + 1) * 128)
        K.dma("sp", att[sl][:], ATs[:, :, tsl].rearrange("a f t -> f a t"), ["ATs"], ["att%d" % sl], "lda%d" % sl)
        K.dma("sp", met[sl][:], MEMT[:, :, tsl].rearrange("h d t -> d h t"), ["MEMT"], ["met%d" % sl], "ldm%d" % sl)
        K.dma("sp", xc[sl][:], xs[tsl, :], [], ["xc%d" % sl], "ldx%d" % sl)
        for nh in range(2):
            bk = banks[nh]
            for kc in range(8):
                lhs = att[sl][:, kc, :] if kc < 4 else met[sl][:, kc - 4, :]
                K.mm(bk[:], lhs, woutb[:, kc, nh * 512:(nh + 1) * 512], kc == 0, kc == 7,
                     ["att%d" % sl, "met%d" % sl, "woutb"], ["bk%d" % nh])
            K.tt("dve", x1t[sl][:, nh * 512:(nh + 1) * 512], bk[:], xc[sl][:, nh * 512:(nh + 1) * 512], ALU.add,
                 ["bk%d" % nh, "xc%d" % sl], ["x1t%d" % sl])
        K.dma("pool", X1[tsl, :], x1t[sl][:], ["x1t%d" % sl], ["X1"], "stx%d" % sl)
    def d_stage2(c):
        sl = c % 2
        tsl = slice(c * 128, (c + 1) * 128)
        K.act(junk2[:], x1t[sl][:], AF.Square, ["x1t%d" % sl], ["junk2", "s1"], accum=s1[:])
        K.ts("dve", s1[:], s1[:], 1.0 / D, EPS, ALU.mult, ALU.add, ["s1"], ["s1"])
        K.rsqrt(s1[:], "s1")
        K.act(xn2[:], x1t[sl][:], AF.Copy, ["x1t%d" % sl, "s1"], ["xn2"], scale=s1[:])
        for half in range(2):
            pbk = banks[2 + half][:].bitcast(BF16)
            for k4 in range(4):
                kc = half * 4 + k4
                K.tr(pbk[:, k4 * 128:(k4 + 1) * 128], xn2[:, kc * 128:(kc + 1) * 128], identb[:], ["xn2", "identb"],
                     ["bk%d" % (2 + half)])
            for k4 in range(4):
                kc = half * 4 + k4
                if half == 0:
                    K.ts("dve", h2t[sl][:, kc, :], pbk[:, k4 * 128:(k4 + 1) * 128], A2[:, kc:kc + 1], B2[:, kc:kc + 1],
                         ALU.mult, ALU.add, ["bk%d" % (2 + half), "A2", "modT"], ["h2t%d" % sl])
                else:
                    K.act(h2t[sl][:, kc, :], pbk[:, k4 * 128:(k4 + 1) * 128], AF.Identity,
                          ["bk%d" % (2 + half), "A2", "modT"], ["h2t%d" % sl], bias=B2[:, kc:kc + 1], scale=A2[:, kc:kc + 1])
        K.dma("pool", H2T[:, :, tsl].rearrange("k p t -> p k t"), h2t[sl][:], ["h2t%d" % sl], ["H2T"], "sth%d" % sl)
    d_stage1(0)
    for c in range(OWNC):
        if c + 1 < OWNC:
            d_stage1(c + 1)
        d_stage2(c)
    P.barrier()

    if STOP <= 4:
        P.emit()
        return nc
    K.off = PERSIST_END
    wupb = K.sb([128, 8, 2 * DFF], BF16)
    wdnb = K.sb([128, 22, D], BF16)
    cfw = K.sb([128, 3, 44], F32)
    cfb = K.sb([128, 44], F32)
    fgb = K.sb([128, D], F32)
    off_e = K.off
    stg3 = [K.sb([128, DFF], F32) for _ in range(2)]
    i3 = 0
    for kc in range(8):
        for hf in range(2):
            sl = i3 % 2
            K.dma("sp", stg3[sl][:], w_up[kc * 128:(kc + 1) * 128, hf * DFF:(hf + 1) * DFF], [], ["stg%d" % sl], "ldw%d" % sl)
            K.cp(cengs[i3 % 3], wupb[:, kc, hf * DFF:(hf + 1) * DFF], stg3[sl][:], ["stg%d" % sl], ["wupb"])
            i3 += 1
    for kc in range(22):
        sl = i3 % 2
        K.dma("sp", stg3[sl][:, 0:D], w_down[kc * 128:(kc + 1) * 128, :], [], ["stg%d" % sl], "ldw%d" % sl)
        K.tt("dve" if kc % 2 == 0 else "pool", wdnb[:, kc, :], stg3[sl][:, 0:D], g12[:, 1, :], ALU.mult,
             ["stg%d" % sl, "g12"], ["wdnb"])
        i3 += 1
    for j in range(3):
        K.dma("sp", cfw[:, j, :], conv_f_w[j].rearrange("(c p) -> p c", p=128), [], ["cfw"], "ld0", slow=True)
    K.dma("sp", cfb[:], conv_f_b.rearrange("(c p) -> p c", p=128), [], ["cfb"], "ld1", slow=True)
    K.dma("sp", fgb[:], final_g.partition_broadcast(128), [], ["fgb"], "ld2")
    P.barrier()
    K.off = off_e
    h2w = K.sb([128, 8, 514], BF16)
    ub = [K.sb([128, 514], F32) for _ in range(2)]
    ca2 = [K.sb([128, 512], F32) for _ in range(2)]
    cg2 = [K.sb([128, 512], F32) for _ in range(2)]
    actT = K.sb([128, 22, 512], BF16)
    x1c = [K.sb([128, D], F32) for _ in range(2)]
    yc1 = K.sb([128, D], F32)
    yc = [yc1, yc1]
    s2v = K.sb([128, 1], F32)
    for ti in range(NFT):
        t0 = ti * 512
        sl = ti % 2
        hk = "h2w"
        if ti == 0:
            K.memset("pool", h2w[:, :, 0:1], 0.0, [hk])
            K.dma("sp", h2w[:, :, 1:514], H2T[:, :, 0:513].rearrange("k p t -> p k t"), ["H2T"], [hk], "ldh0")
        else:
            K.dma("sp", h2w[:, :, :], H2T[:, :, t0 - 1:t0 + 513].rearrange("k p t -> p k t"), ["H2T"], [hk], "ldh0")
        for i in range(22):
            for (which, fc) in ((0, i), (1, 22 + i)):
                u = ub[which]
                uk = "ub%d" % which
                for nt, (n0, n1) in enumerate(((0, 258), (258, 514))):
                    bk = banks[which * 2 + nt]
                    for kc in range(8):
                        K.mm(bk[:, 0:n1 - n0], wupb[:, kc, fc * 128:(fc + 1) * 128], h2w[:, kc, n0:n1],
                             kc == 0, kc == 7, [hk, "wupb"], ["bk%d" % (which * 2 + nt)])
                    K.cp("act", u[:, n0:n1], bk[:, 0:n1 - n0], ["bk%d" % (which * 2 + nt)], [uk])
                dst = ca2[i % 2] if which == 0 else cg2[i % 2]
                dk = ("ca%d" if which == 0 else "cg%d") % (i % 2)
                eng = "dve"
                K.ts(eng, dst[:], u[:, 1:513], cfw[:, 1, fc:fc + 1], cfb[:, fc:fc + 1], ALU.mult, ALU.add, [uk, "cfw", "cfb"], [dk])
                K.stt(eng, dst[:], u[:, 0:512], cfw[:, 0, fc:fc + 1], dst[:], ALU.mult, ALU.add, [uk, "cfw", dk], [dk])
                K.stt(eng, dst[:], u[:, 2:514], cfw[:, 2, fc:fc + 1], dst[:], ALU.mult, ALU.add, [uk, "cfw", dk], [dk])
            K.act(cg2[i % 2][:], cg2[i % 2][:], AF.Silu, ["cg%d" % (i % 2)], ["cg%d" % (i % 2)])
            K.tt("pool", actT[:, i, :], cg2[i % 2][:], ca2[i % 2][:], ALU.mult, ["cg%d" % (i % 2), "ca%d" % (i % 2)], ["actT%d" % i])
        ak = ["actT%d" % i for i in range(22)]
        for j in range(4):
            cidx = ti * 4 + j
            s3 = cidx % 2
            tsl = slice(t0 + j * 128, t0 + (j + 1) * 128)
            K.dma("sp", x1c[s3][:], X1[tsl, :], ["X1"], ["x1c%d" % s3], "ldx%d" % s3)
            for nh in range(2):
                bk = banks[4 + nh]
                for kc in range(22):
                    K.mm(bk[:], actT[:, kc, j * 128:(j + 1) * 128], wdnb[:, kc, nh * 512:(nh + 1) * 512], kc == 0, kc == 21,
                         ak + ["wdnb"], ["bk%d" % (4 + nh)])
                K.tt("dve", x1c[s3][:, nh * 512:(nh + 1) * 512], bk[:], x1c[s3][:, nh * 512:(nh + 1) * 512], ALU.add,
                     ["bk%d" % (4 + nh), "x1c%d" % s3], ["x1c%d" % s3])
            K.act(yc[s3][:], x1c[s3][:], AF.Square, ["x1c%d" % s3], ["yc", "s2v"], accum=s2v[:])
            K.ts("dve", s2v[:], s2v[:], 1.0 / D, EPS, ALU.mult, ALU.add, ["s2v"], ["s2v"])
            K.rsqrt(s2v[:], "s2v")
            K.act(yc[s3][:], x1c[s3][:], AF.Copy, ["x1c%d" % s3, "s2v"], ["yc"], scale=s2v[:, 0:1])
            K.tt("pool", yc[s3][:], yc[s3][:], fgb[:], ALU.mult, ["yc", "fgb"], ["yc"])
            K.dma("pool", y_out[tsl, :], yc[s3][:], ["yc"], [], "sty%d" % s3)
    P.emit()
    return nc


_CACHE = {}


def _consts(S, flip):
    half = 16
    inv = (1.0 / (np.float32(10000.0) ** (np.arange(half, dtype=np.float32) * np.float32(2.0 / 32)))).astype(np.float32)
    pos = np.arange(S, dtype=np.float32)
    if flip:
        pos = pos[::-1].copy()
    ang = (pos[:, None] * inv[None, :]).astype(np.float32)
    cos = np.cos(ang.astype(np.float64)).astype(np.float32).T
    sin = np.sin(ang.astype(np.float64)).astype(np.float32).T
    cos_t = np.ascontiguousarray(np.concatenate([cos, cos], axis=0))
    sin_t = np.ascontiguousarray(np.concatenate([sin, sin], axis=0))
    s = np.arange(128)[:, None]
    t = np.arange(128)[None, :]
    tri = np.stack([(s <= t), (s >= t)]).astype(np.float32)
    mneg = ((1.0 - tri) * -30000.0).astype(np.float32)
    return dict(ident=np.eye(128, dtype=np.float32), cos_t=cos_t, sin_t=sin_t, tri=tri, mneg=mneg)


def kernel(x_prompt, x_sample, c_prompt, c_sample, norm1_g, w_ada, b_ada, w_in, b_gate, q_norm_g,
           kv_norm_g, w_uq, w_ukv, conv_m_w, conv_m_b, mh_norm_g, w_out, norm2_g, w_up, conv_f_w,
           conv_f_b, w_down, final_g):
    f = lambda a: np.ascontiguousarray(np.asarray(a, dtype=np.float32))
    x_prompt, x_sample, c_prompt, c_sample = f(x_prompt), f(x_sample), f(c_prompt), f(c_sample)
    S = x_prompt.shape[1]
    HALF = S // 2
    seqs = [(x_prompt[0], c_prompt[0]), (x_prompt[1], c_prompt[1]), (x_sample[0], c_sample[0])]
    w_in0 = f(w_in)[0]
    gperm = np.concatenate([np.arange(G0), G0 + np.array([8, 9, 10, 11, 12, 13, 14, 15, 0, 1, 2, 3, 4, 5, 6, 7])])
    shared = dict(norm1_g=f(norm1_g)[0], w_ada=f(w_ada)[0], b_ada=f(b_ada)[0], q_norm_g=f(q_norm_g)[0],
                  kv_norm_g=f(kv_norm_g)[0], w_uq=f(w_uq)[0], w_ukv=f(w_ukv)[0], conv_m_b=f(conv_m_b)[0],
                  mh_norm_g=f(mh_norm_g)[0], w_out=f(w_out)[0], norm2_g=f(norm2_g)[0], w_up=f(w_up)[0],
                  conv_f_b=f(conv_f_b)[0], w_down=f(w_down)[0], final_g=f(final_g))
    per_flip = []
    for flip in (False, True):
        d = dict(shared)
        d.update(_consts(S, flip))
        if flip:
            d["w_in"] = np.ascontiguousarray(w_in0[:, gperm])
            d["b_gate"] = np.ascontiguousarray(f(b_gate)[0][gperm[G0:] - G0])
            d["conv_m_w"] = np.ascontiguousarray(f(conv_m_w)[0][::-1])
            d["conv_f_w"] = np.ascontiguousarray(f(conv_f_w)[0][::-1])
        else:
            d["w_in"] = w_in0
            d["b_gate"] = f(b_gate)[0]
            d["conv_m_w"] = f(conv_m_w)[0]
            d["conv_f_w"] = f(conv_f_w)[0]
        per_flip.append(d)
    in_maps = []
    for core in range(8):
        cc = core % 6
        s, j = cc // 2, cc % 2
        d = dict(per_flip[j])
        xseq, cv = seqs[s]
        d["xs"] = np.ascontiguousarray(xseq[::-1]) if j == 1 else xseq
        d["cvec"] = cv
        in_maps.append(d)
    if S not in _CACHE:
        _CACHE[S] = build_program(S)
    nc = _CACHE[S]
    res = run_bass_kernel_spmd(nc, in_maps, core_ids=list(range(8)))
    outs = []
    for s in range(3):
        y0 = np.asarray(res.results[2 * s]["y"], dtype=np.float32)
        y1 = np.asarray(res.results[2 * s + 1]["y"], dtype=np.float32)[::-1]
        outs.append(np.concatenate([y0, y1], axis=0))
    y_prompt = np.stack([outs[0], outs[1]], axis=0)
    y_sample = outs[2][None]
    return (y_prompt, y_sample)
```

```python
import os
import contextlib
import numpy as np
import concourse.bass as bass
import concourse.mybir as mybir
from concourse.bass_utils import run_bass_kernel_spmd

F32 = mybir.dt.float32
BF16 = mybir.dt.bfloat16
AF = mybir.ActivationFunctionType
ALU = mybir.AluOpType
AX = mybir.AxisListType

D = 1024
DFF = 2816
NIN = 2736
EPS = 1e-6
CQ0, CKV0, KR0, MQK0, MV0, MO0, G0 = 0, 384, 640, 672, 1696, 2208, 2720
ROT0 = 2736
WINC = 2864
QUEUES = ("pe", "act", "dve", "pool", "sp")
SB0 = 16512
SBTOP = 229344


class Prog:
    def __init__(self, nc):
        self.nc = nc
        self.ops = []
        self.lastw = {}
        self.readers = {}
        self.last_on_eng = {}
        self.last_on_chan = {}
        self.pending = {e: set() for e in QUEUES}

    def add(self, eng, fn, reads=(), writes=(), chan=None):
        idx = len(self.ops)
        deps = set(self.pending[eng])
        self.pending[eng] = set()
        for r in reads:
            w = self.lastw.get(r)
            if w is not None:
                deps.add(w)
        for w_ in writes:
            w = self.lastw.get(w_)
            if w is not None:
                deps.add(w)
            deps.update(self.readers.get(w_, ()))
        for r in reads:
            self.readers.setdefault(r, []).append(idx)
        for w_ in writes:
            self.lastw[w_] = idx
            self.readers[w_] = []
        self.ops.append(dict(eng=eng, fn=fn, deps=deps, chan=chan, marked=(chan is not None)))
        self.last_on_eng[eng] = idx
        if chan is not None:
            self.last_on_chan[chan] = idx
        return idx

    def barrier(self):
        front = set(self.last_on_eng.values()) | set(self.last_on_chan.values())
        for e in QUEUES:
            self.pending[e] |= front
        self.lastw = {}
        self.readers = {}

    def emit(self):
        nc = self.nc
        ops = self.ops
        self.barrier()
        self.add("sp", None)

        def semkey(o):
            return ("C", o["chan"]) if o["chan"] is not None else ("E", o["eng"])

        for o in ops:
            red = {}
            for d in o["deps"]:
                p = ops[d]
                if p["fn"] is None:
                    continue
                k = semkey(p)
                if p["chan"] is None and p["eng"] == "pe" and o["eng"] == "pe" and o["chan"] is None:
                    continue
                if k not in red or red[k] < d:
                    red[k] = d
            o["rdeps"] = red
        waited = {e: {} for e in QUEUES}
        for o in ops:
            keep = {}
            for k, d in o["rdeps"].items():
                if waited[o["eng"]].get(k, -1) >= d:
                    continue
                keep[k] = d
                waited[o["eng"]][k] = d
                ops[d]["marked"] = True
            o["rdeps"] = keep
        cnt = {}
        for o in ops:
            if o["marked"] and o["fn"] is not None:
                k = semkey(o)
                inc = 16 if o["chan"] is not None else 1
                cnt[k] = cnt.get(k, 0) + inc
                o["val"] = cnt[k]
                o["inc"] = inc
        keys = list(cnt.keys())
        self.nsem = len(keys)
        self.maxcnt = max(cnt.values())
        sems = {}
        with contextlib.ExitStack() as st:
            for j, k in enumerate(keys):
                sems[k] = st.enter_context(nc.semaphore("s%d" % j))
            per = {e: [o for o in ops if o["eng"] == e] for e in QUEUES}
            blk = st.enter_context(nc.Block())

            def run(engobj, lst):
                for o in lst:
                    for k, d in o["rdeps"].items():
                        engobj.wait_ge(sems[k], ops[d]["val"])
                    if o["fn"] is None:
                        continue
                    ins = o["fn"](engobj)
                    if o["marked"]:
                        ins.then_inc(sems[semkey(o)], o["inc"])

            @blk.tensor
            def _(e):
                run(e, per["pe"])

            @blk.scalar
            def _(e):
                run(e, per["act"])

            @blk.vector
            def _(e):
                run(e, per["dve"])

            @blk.gpsimd
            def _(e):
                run(e, per["pool"])

            @blk.sync
            def _(e):
                run(e, per["sp"])


class Ctx:
    def __init__(self, nc):
        self.nc = nc
        self.P = Prog(nc)
        self.off = SB0
        self.nid = 0
        self.rr = 0

    def reset(self):
        self.off = SB0

    def sb(self, shape, dt):
        n = 1
        for s in shape[1:]:
            n *= s
        nbytes = n * (4 if dt == F32 else 2)
        nbytes = (nbytes + 63) // 64 * 64
        self.nid += 1
        t = self.nc.alloc_sbuf_tensor_at("sb%d" % self.nid, list(shape), dt, offset=self.off)
        self.off += nbytes
        assert self.off <= SBTOP, ("SBUF overflow", self.off)
        return t

    def mm(self, out, lhsT, rhs, start, stop, r, w):
        self.P.add("pe", lambda e: e.matmul(out, lhsT=lhsT, rhs=rhs, start=start, stop=stop), r, w)

    def tr(self, out, in_, ident, r, w):
        self.P.add("pe", lambda e: e.transpose(out=out, in_=in_, identity=ident), r, w)

    def act(self, out, in_, func, r, w, bias=None, scale=None, accum=None):
        kw = {}
        if bias is not None:
            kw["bias"] = bias
        if scale is not None:
            kw["scale"] = scale
        if accum is not None:
            kw["accum_out"] = accum
        self.P.add("act", lambda e: e.activation(out=out, in_=in_, func=func, **kw), r, w)

    def ts(self, eng, out, in0, s1, s2, op0, op1, r, w):
        if op1 is None:
            self.P.add(eng, lambda e: e.tensor_scalar(out=out, in0=in0, scalar1=s1, scalar2=None, op0=op0), r, w)
        else:
            self.P.add(eng, lambda e: e.tensor_scalar(out=out, in0=in0, scalar1=s1, scalar2=s2, op0=op0, op1=op1), r, w)

    def tt(self, eng, out, in0, in1, op, r, w):
        self.P.add(eng, lambda e: e.tensor_tensor(out=out, in0=in0, in1=in1, op=op), r, w)

    def stt(self, eng, out, in0, scalar, in1, op0, op1, r, w):
        self.P.add(eng, lambda e: e.scalar_tensor_tensor(out=out, in0=in0, scalar=scalar, in1=in1, op0=op0, op1=op1), r, w)

    def cp(self, eng, out, in_, r, w):
        if eng == "act":
            self.P.add("act", lambda e: e.activation(out=out, in_=in_, func=AF.Identity), r, w)
        else:
            self.P.add(eng, lambda e: e.tensor_copy(out=out, in_=in_), r, w)

    def memset(self, eng, ap, val, w):
        self.P.add(eng, lambda e: e.memset(ap, val), (), w)

    def recip(self, out, in_, r, w):
        self.P.add("dve", lambda e: e.reciprocal(out=out, in_=in_), r, w)

    def dma(self, q, out, in_, r, w, chan, slow=False):
        if slow:
            self.P.add(q, lambda e: e.dma_start(out=out, in_=in_, allow_slow_non_contiguous=True), r, w, chan=chan)
        else:
            self.P.add(q, lambda e: e.dma_start(out=out, in_=in_), r, w, chan=chan)

    def rsqrt(self, buf, key):
        self.act(buf, buf, AF.Sqrt, [key], [key])
        self.recip(buf, buf, [key], [key])


def build_program(S):
    NCH = S // 128
    HALF = S // 2
    OWNC = HALF // 128 + 1
    OWN = OWNC * 128
    NT = S // 512
    NTO = (OWN + 1 + 511) // 512
    NQT = (OWN + 511) // 512
    NFT = HALF // 512
    LNS = float(np.log(128.0 ** -0.5))
    ASC = float(96.0 ** -0.5)

    nc = bass.Bass("TRN2", target_bir_lowering=False)
    K = Ctx(nc)
    STOP = int(os.environ.get("KSTOP", "99"))
    SUB = int(os.environ.get("KSUB", "99"))
    VAR = int(os.environ.get("KVAR", "0"))
    P = K.P

    def din(name, shape):
        return nc.dram_tensor(name, list(shape), F32, kind="ExternalInput").ap()

    xs = din("xs", [S, D])
    cvec = din("cvec", [D])
    norm1_g = din("norm1_g", [D])
    w_ada = din("w_ada", [D, 6 * D])
    b_ada = din("b_ada", [6 * D])
    w_in = din("w_in", [D, NIN])
    b_gate = din("b_gate", [16])
    q_norm_g = din("q_norm_g", [384])
    kv_norm_g = din("kv_norm_g", [256])
    w_uq = din("w_uq", [384, 768])
    w_ukv = din("w_ukv", [256, 1024])
    conv_m_w = din("conv_m_w", [3, 1024])
    conv_m_b = din("conv_m_b", [1024])
    mh_norm_g = din("mh_norm_g", [512])
    w_out = din("w_out", [1024, 1024])
    norm2_g = din("norm2_g", [D])
    w_up = din("w_up", [D, 2 * DFF])
    conv_f_w = din("conv_f_w", [3, 2 * DFF])
    conv_f_b = din("conv_f_b", [2 * DFF])
    w_down = din("w_down", [DFF, D])
    final_g = din("final_g", [D])
    ident_d = din("ident", [128, 128])
    cos_d = din("cos_t", [32, S])
    sin_d = din("sin_t", [32, S])
    tri_d = din("tri", [2, 128, 128])
    mneg_d = din("mneg", [2, 128, 128])
    y_out = nc.dram_tensor("y", [HALF, D], F32, kind="ExternalOutput").ap()

    def dscr(name, shape, dt):
        return nc.dram_tensor(name, list(shape), dt).ap()

    modrow = dscr("modrow", [6 * D], F32)
    KTs = dscr("KTs", [128, 8, S], BF16)
    VAs = dscr("VAs", [8, 128, NCH, 66], BF16)
    QTs = dscr("QTs", [128, 8, NQT * 512], BF16)
    MQT = dscr("MQT", [128, 4, S + 512], BF16)
    MKT = dscr("MKT", [128, 4, S + 512], BF16)
    MVA = dscr("MVA", [NCH, 128, 4, 130], BF16)
    MG = dscr("MG", [NCH, 128, 16], F32)
    SO = dscr("SO", [NT * 4, 128, 512], F32)
    HA = dscr("HA", [OWNC, 128, 512], F32)
    HB = dscr("HB", [OWNC, 128, 512], F32)
    ATs = dscr("ATs", [4, 128, NQT * 512], BF16)
    MEMT = dscr("MEMT", [4, 128, OWN], BF16)
    X1 = dscr("X1", [OWN, D], F32)
    H2T = dscr("H2T", [8, 128, OWN], BF16)

    banks = [nc.alloc_psum_tensor("bank%d" % i, [128, 512], F32) for i in range(8)]

    identf = K.sb([128, 128], F32)
    identb = K.sb([128, 128], BF16)
    onesf = K.sb([128, 128], F32)
    modT = K.sb([128, 48], F32)
    A1 = K.sb([128, 8], F32)
    A2 = K.sb([128, 8], F32)
    g12 = K.sb([128, 2, D], F32)
    PERSIST_END = K.off

    K.dma("sp", identf[:], ident_d[:, :], [], ["identf"], "ld0")
    K.cp("dve", identb[:], identf[:], ["identf"], ["identb"])
    K.memset("pool", onesf[:], 1.0, ["onesf"])
    cT = K.sb([128, 8], F32)
    K.dma("sp", cT[:], cvec.rearrange("(c p) -> p c", p=128), [], ["cT"], "ld1", slow=True)
    csil = K.sb([128, 8], F32)
    K.act(csil[:], cT[:], AF.Silu, ["cT"], ["csil"])
    crep = K.sb([128, 8, 128], F32)
    K.cp("dve", crep[:], csil[:].unsqueeze(2).to_broadcast([128, 8, 128]), ["csil"], ["crep"])
    badab = K.sb([128, 6 * D], F32)
    K.dma("sp", badab[:], b_ada.partition_broadcast(128), [], ["badab"], "ld2")
    modbc = K.sb([128, 6 * D], F32)
    wst = [K.sb([128, 8, 512], F32) for _ in range(2)]
    for ct in range(12):
        sl = ct % 2
        K.dma("sp", wst[sl][:], w_ada[:, ct * 512:(ct + 1) * 512].rearrange("(c p) n -> p c n", p=128),
              [], ["wst%d" % sl], "ldw%d" % sl)
        bk = banks[ct % 2]
        for kc in range(8):
            K.mm(bk[:], crep[:, kc, :], wst[sl][:, kc, :], kc == 0, kc == 7,
                 ["crep", "wst%d" % sl], ["bk%d" % (ct % 2)])
        K.tt("dve", modbc[:, ct * 512:(ct + 1) * 512], bk[:], badab[:, ct * 512:(ct + 1) * 512], ALU.add,
             ["bk%d" % (ct % 2), "badab"], ["modbc"])
    K.dma("pool", modrow.rearrange("(o n) -> o n", o=1), modbc[0:1, :], ["modbc"], ["modrow"], "st0")
    K.dma("sp", modT[:], modrow.rearrange("(j p) -> p j", p=128), ["modrow"], ["modT"], "ld3", slow=True)
    K.cp("act", g12[:, 0, :], modbc[:, 2 * D:3 * D], ["modbc"], ["g12"])
    K.cp("act", g12[:, 1, :], modbc[:, 5 * D:6 * D], ["modbc"], ["g12"])
    gT = K.sb([128, 2, 8], F32)
    K.dma("sp", gT[:, 0, :], norm1_g.rearrange("(c p) -> p c", p=128), [], ["gT"], "ld4", slow=True)
    K.dma("sp", gT[:, 1, :], norm2_g.rearrange("(c p) -> p c", p=128), [], ["gT"], "ld5", slow=True)
    K.stt("dve", A1[:], modT[:, 8:16], 1.0, gT[:, 0, :], ALU.add, ALU.mult, ["modT", "gT"], ["A1"])
    K.stt("dve", A2[:], modT[:, 32:40], 1.0, gT[:, 1, :], ALU.add, ALU.mult, ["modT", "gT"], ["A2"])
    B1 = modT[:, 0:8]
    B2 = modT[:, 24:32]
    P.barrier()

    if STOP <= 0:
        P.emit()
        return nc
    K.off = PERSIST_END
    winb = K.sb([128, 8, WINC], BF16)
    wuqb = K.sb([128, 3, 8, 256], BF16)
    wukb = K.sb([128, 2, 8, 128], BF16)
    wuvb = K.sb([128, 2, 8, 64], BF16)
    gq = K.sb([128, 3], F32)
    gkv = K.sb([128, 2], F32)
    cmw = K.sb([128, 3, 8], F32)
    cmb = K.sb([128, 8], F32)
    bgb = K.sb([128, 16], F32)
    off_a = K.off
    stg = [K.sb([128, NIN], F32) for _ in range(2)]
    K.dma("sp", gq[:], q_norm_g.rearrange("(c p) -> p c", p=128), [], ["gq"], "ld0", slow=True)
    K.dma("sp", gkv[:], kv_norm_g.rearrange("(c p) -> p c", p=128), [], ["gkv"], "ld1", slow=True)
    K.memset("pool", winb[:, :, NIN:WINC], 0.0, ["winb"])
    K.memset("pool", wuqb[:], 0.0, ["wuqb"])
    K.memset("pool", wukb[:], 0.0, ["wukb"])
    cengs = ["dve", "pool", "act"]
    for kc in range(8):
        sl = kc % 2
        K.dma("sp", stg[sl][:], w_in[kc * 128:(kc + 1) * 128, :], [], ["stg%d" % sl], "ldw%d" % sl)
        K.cp(cengs[kc % 3], winb[:, kc, 0:NIN], stg[sl][:], ["stg%d" % sl], ["winb"])
        K.ts("dve", winb[:, kc, ROT0:ROT0 + 16], stg[sl][:, KR0 + 16:KR0 + 32], -1.0, None, ALU.mult, None,
             ["stg%d" % sl], ["winb"])
        K.cp("dve", winb[:, kc, ROT0 + 16:ROT0 + 32], stg[sl][:, KR0:KR0 + 16], ["stg%d" % sl], ["winb"])
    for kc in range(3):
        sl = kc % 2
        K.dma("sp", stg[sl][:, 0:768], w_uq[kc * 128:(kc + 1) * 128, :], [], ["stg%d" % sl], "ldw%d" % sl)
        src = stg[sl][:, 0:768].rearrange("p (h e) -> p h e", h=8)
        g = gq[:, kc:kc + 1]
        rs, ws = ["stg%d" % sl, "gq"], ["wuqb"]
        K.ts("dve", wuqb[:, kc, :, 0:32], src[:, :, 64:96], g, None, ALU.mult, None, rs, ws)
        K.ts("pool", wuqb[:, kc, :, 32:96], src[:, :, 0:64], g, None, ALU.mult, None, rs, ws)
        K.ts("dve", wuqb[:, kc, :, 128:144], src[:, :, 80:96], g, -1.0, ALU.mult, ALU.mult, rs, ws)
        K.ts("dve", wuqb[:, kc, :, 144:160], src[:, :, 64:80], g, None, ALU.mult, None, rs, ws)
    for kc in range(2):
        sl = (kc + 1) % 2
        K.dma("sp", stg[sl][:, 0:1024], w_ukv[kc * 128:(kc + 1) * 128, :], [], ["stg%d" % sl], "ldw%d" % sl)
        src = stg[sl][:, 0:1024].rearrange("p (h e) -> p h e", h=8)
        g = gkv[:, kc:kc + 1]
        rs = ["stg%d" % sl, "gkv"]
        K.ts("dve", wukb[:, kc, :, 32:96], src[:, :, 0:64], g, None, ALU.mult, None, rs, ["wukb"])
        K.ts("pool", wuvb[:, kc, :, :], src[:, :, 64:128], g, None, ALU.mult, None, rs, ["wuvb"])
    for j in range(3):
        K.dma("sp", cmw[:, j, :], conv_m_w[j].rearrange("(c p) -> p c", p=128), [], ["cmw"], "ld2", slow=True)
    K.dma("sp", cmb[:], conv_m_b.rearrange("(c p) -> p c", p=128), [], ["cmb"], "ld3", slow=True)
    K.dma("sp", bgb[:], b_gate.partition_broadcast(128), [], ["bgb"], "ld4")
    P.barrier()
    K.off = off_a

    xt = [K.sb([128, 4, D], F32) for _ in range(2)]
    xnb = K.sb([128, 4, D], BF16)
    junk = K.sb([128, D], BF16)
    ssq = K.sb([128, 4], F32)
    hT = K.sb([128, 8, 512], BF16)
    cqT = K.sb([128, 3, 512], BF16)
    ckvT = K.sb([128, 2, 512], BF16)
    cst = K.sb([32, 512], F32)
    snt = K.sb([32, 512], F32)
    rt1 = K.sb([32, 512], F32)
    rt2 = K.sb([32, 512], F32)
    krr = K.sb([32, 512], BF16)
    ktile = K.sb([128, 8, 512], BF16)
    qtile = ktile
    vat = K.sb([128, 8, 4, 66], BF16)
    rawb = [K.sb([128, 514], F32) for _ in range(2)]
    cv = [K.sb([128, 512], F32) for _ in range(2)]
    mqk = K.sb([128, 8, 512], BF16)
    mva = K.sb([128, 4, 4, 130], BF16)
    sot = K.sb([128, 4, 512], F32)
    gt = K.sb([128, 4, 16], F32)
    K.memset("pool", vat[:, :, :, 64:66], 0.0, ["vat"])
    K.memset("pool", vat[:, :, :, 64:65], 1.0, ["vat"])
    K.memset("pool", mva[:, :, :, 128:130], 0.0, ["mva"])
    K.memset("pool", mva[:, :, :, 128:129], 1.0, ["mva"])
    carry = K.sb([128, 8, 2], F32)
    K.memset("pool", carry[:], 0.0, ["carry"])

    hT2 = [hT, K.sb([128, 8, 512], BF16)]
    cst2 = [cst, K.sb([32, 512], F32)]
    snt2 = [snt, K.sb([32, 512], F32)]
    ssq2 = [ssq, K.sb([128, 4], F32)]

    def front(ti):
        t0 = ti * 512
        sl = ti % 2
        S_ = "%d" % sl
        K.dma("sp", xt[sl][:], xs[t0:t0 + 512, :].rearrange("(j p) n -> p j n", p=128), [], ["xt" + S_], "ldx" + S_)
        K.dma("sp", cst2[sl][:], cos_d[:, t0:t0 + 512], [], ["cst" + S_], "ldc" + S_)
        K.dma("sp", snt2[sl][:], sin_d[:, t0:t0 + 512], [], ["snt" + S_], "lds" + S_)
        sq = ssq2[sl]
        for j in range(4):
            K.act(junk[:], xt[sl][:, j, :], AF.Square, ["xt" + S_], ["junk", "ssq" + S_], accum=sq[:, j:j + 1])
        K.ts("dve", sq[:], sq[:], 1.0 / D, EPS, ALU.mult, ALU.add, ["ssq" + S_], ["ssq" + S_])
        K.rsqrt(sq[:], "ssq" + S_)
        for j in range(4):
            K.act(xnb[:, j, :], xt[sl][:, j, :], AF.Copy, ["xt" + S_, "ssq" + S_], ["xnb%d" % j], scale=sq[:, j:j + 1])
        for kc in range(8):
            bk = banks[kc % 2]
            pb = bk[:].bitcast(BF16)
            for j in range(4):
                K.tr(pb[:, j * 128:(j + 1) * 128], xnb[:, j, kc * 128:(kc + 1) * 128], identb[:],
                     ["xnb%d" % j, "identb"], ["bk%d" % (kc % 2)])
            if kc % 2 == 0:
                K.ts("dve", hT2[sl][:, kc, :], pb[:, 0:512], A1[:, kc:kc + 1], B1[:, kc:kc + 1], ALU.mult, ALU.add,
                     ["bk%d" % (kc % 2), "A1", "modT"], ["hT%d_%d" % (sl, kc)])
            else:
                K.act(hT2[sl][:, kc, :], pb[:, 0:512], AF.Identity, ["bk%d" % (kc % 2), "A1", "modT"], ["hT%d_%d" % (sl, kc)],
                      bias=B1[:, kc:kc + 1], scale=A1[:, kc:kc + 1])

    lat2 = [K.sb([128, 4, 640], BF16) for _ in range(2)]
    latf2 = [K.sb([128, 640], F32) for _ in range(2)]
    ssl2 = [K.sb([128, 2], F32) for _ in range(2)]
    krr2 = [krr, K.sb([32, 512], BF16)]
    if os.environ.get("KDBG"):
        print("phase A sbuf top", K.off, SBTOP)

    def back1(ti):
        t0 = ti * 512
        own = ti < NTO
        sl = ti % 2
        S_ = "%d" % sl
        hT = hT2[sl]
        cst = cst2[sl]
        snt = snt2[sl]
        hTr = ["hT%d_%d" % (sl, kc) for kc in range(8)]
        for j in range(4):
            tsl = slice(j * 128, (j + 1) * 128)
            lf_ = latf2[j % 2]
            lfk = "latf%d" % (j % 2)
            sk = "ssl%d" % (j % 2)
            ss = ssl2[j % 2]
            lk = "lat%d_%d" % (sl, j)
            needq = ti < NQT
            q0 = 0 if needq else 384
            for (bi, c0, c1, o0) in ((2, q0, 512, q0), (3, 512, 640, 0)):
                for kc in range(8):
                    K.mm(banks[bi][:, o0:o0 + c1 - c0], hT[:, kc, tsl], winb[:, kc, c0:c1], kc == 0, kc == 7,
                         hTr + ["winb"], ["bk%d" % bi])
            K.act(lf_[:, q0:512], banks[2][:, q0:512], AF.Identity, ["bk2"], [lfk])
            K.act(lf_[:, 512:640], banks[3][:, 0:128], AF.Identity, ["bk3"], [lfk])
            for kc in range(8):
                K.mm(banks[6][:], hT[:, kc, tsl], winb[:, kc, MV0:MV0 + 512], kc == 0, kc == 7, hTr + ["winb"], ["bk6"])
            K.cp("act", mva[:, j, :, 0:128], banks[6][:].rearrange("p (h e) -> p h e", h=4), ["bk6"], ["mva"])
            for kc in range(8):
                K.mm(banks[7][:, 0:16], hT[:, kc, tsl], winb[:, kc, G0:G0 + 16], kc == 0, kc == 7, hTr + ["winb"], ["bk7"])
            K.tt("dve", gt[:, j, :], banks[7][:, 0:16], bgb[:], ALU.add, ["bk7", "bgb"], ["gt"])
            if own:
                for kc in range(8):
                    K.mm(banks[5][:], hT[:, kc, tsl], winb[:, kc, MO0:MO0 + 512], kc == 0, kc == 7, hTr + ["winb"], ["bk5"])
                K.act(sot[:, j, :], banks[5][:], AF.Sigmoid, ["bk5"], ["sot"])
            if needq:
                K.act(junk[:, 0:384], lf_[:, 0:384], AF.Square, [lfk], ["junk", sk], accum=ss[:, 0:1])
            else:
                K.memset("pool", ss[:, 0:1], 1.0, [sk])
            K.act(junk[:, 384:640], lf_[:, 384:640], AF.Square, [lfk], ["junk", sk], accum=ss[:, 1:2])
            K.ts("dve", ss[:, 0:1], ss[:, 0:1], 1.0 / 384, EPS, ALU.mult, ALU.add, [sk], [sk])
            K.ts("dve", ss[:, 1:2], ss[:, 1:2], 1.0 / 256, EPS, ALU.mult, ALU.add, [sk], [sk])
            K.rsqrt(ss[:], sk)
            if needq:
                K.ts("dve", lat2[sl][:, j, 0:384], lf_[:, 0:384], ss[:, 0:1], None, ALU.mult, None, [lfk, sk], [lk])
            K.ts("pool", lat2[sl][:, j, 384:640], lf_[:, 384:640], ss[:, 1:2], None, ALU.mult, None, [lfk, sk], [lk])
        gv = gt[:].rearrange("p j (d g h) -> p j d g h", d=2, g=2)
        fcols = gv[:, :, :, 1, :]
        K.act(fcols, fcols, AF.Exp, ["gt"], ["gt"], scale=-1.0)
        K.act(fcols, fcols, AF.Ln, ["gt"], ["gt"], bias=1.0)
        K.ts("dve", fcols, fcols, -1.0, None, ALU.mult, None, ["gt"], ["gt"])
        K.dma("pool", MG[ti * 4:(ti + 1) * 4].rearrange("j p g -> p j g"), gt[:], ["gt"], ["MG"], "st1")
        K.dma("pool", MVA[ti * 4:(ti + 1) * 4].rearrange("j p h e -> p j h e"), mva[:], ["mva"], ["MVA"], "st2")
        if own:
            K.dma("pool", SO[ti * 4:(ti + 1) * 4].rearrange("j p n -> p j n"), sot[:], ["sot"], ["SO"], "st4")
        for kc in range(8):
            K.mm(banks[0][:], winb[:, kc, KR0:KR0 + 128], hT[:, kc, :], kc == 0, kc == 7, hTr + ["winb"], ["bk0"])
        for kc in range(8):
            K.mm(banks[1][:], winb[:, kc, ROT0:ROT0 + 128], hT[:, kc, :], kc == 0, kc == 7, hTr + ["winb"], ["bk1"])
        K.tt("dve", rt1[:], banks[0][0:32, :], cst[:], ALU.mult, ["bk0", "cst" + S_], ["rt1"])
        K.tt("dve", rt2[:], banks[1][0:32, :], snt[:], ALU.mult, ["bk1", "snt" + S_], ["rt2"])
        K.tt("pool", krr2[sl][:], rt1[:], rt2[:], ALU.add, ["rt1", "rt2"], ["krr" + S_])

    def conv_part(ti):
        t0 = ti * 512
        last = ti == NT
        sl = ti % 2
        hT = hT2[sl]
        hTr = ["hT%d_%d" % (sl, kc) for kc in range(8)]
        for c in range(8):
            if c < 4 and not (ti < NTO + 1):
                continue
            rb = rawb[c % 2]
            rk = "rawb%d" % (c % 2)
            K.cp("pool", rb[:, 0:2], carry[:, c, :], ["carry%d" % c], [rk])
            if not last:
                bk = banks[6 + (c % 2)]
                for kc in range(8):
                    K.mm(bk[:], winb[:, kc, MQK0 + c * 128:MQK0 + (c + 1) * 128], hT[:, kc, :], kc == 0, kc == 7,
                         hTr + ["winb"], ["bk%d" % (6 + c % 2)])
                K.cp("act", rb[:, 2:514], bk[:], ["bk%d" % (6 + c % 2)], [rk])
            else:
                K.memset("pool", rb[:, 2:514], 0.0, [rk])
            K.cp("pool", carry[:, c, :], rb[:, 512:514], [rk], ["carry%d" % c])
            cb = cv[c % 2]
            ck = "cv%d" % (c % 2)
            K.ts("dve", cb[:], rb[:, 1:513], cmw[:, 1, c:c + 1], None, ALU.mult, None, [rk, "cmw"], [ck])
            K.stt("dve", cb[:], rb[:, 0:512], cmw[:, 0, c:c + 1], cb[:], ALU.mult, ALU.add, [rk, "cmw", ck], [ck])
            K.stt("dve", cb[:], rb[:, 2:514], cmw[:, 2, c:c + 1], cb[:], ALU.mult, ALU.add, [rk, "cmw", ck], [ck])
            K.act(mqk[:, c, :], cb[:], AF.Silu, [ck, "cmb"], ["mqk%d" % c], bias=cmb[:, c:c + 1])
        if ti < NTO + 1:
            K.dma("pool", MQT[:, :, t0:t0 + 512], mqk[:, 0:4, :], ["mqk%d" % c for c in range(4)], ["MQT"], "st7")
        K.dma("pool", MKT[:, :, t0:t0 + 512], mqk[:, 4:8, :], ["mqk%d" % c for c in range(4, 8)], ["MKT"], "st8")

    def back2(ti):
        t0 = ti * 512
        sl = ti % 2
        S_ = "%d" % sl
        cst = cst2[sl]
        snt = snt2[sl]
        for j in range(4):
            tsl = slice(j * 128, (j + 1) * 128)
            lk = "lat%d_%d" % (sl, j)
            lt = lat2[sl]
            pb = banks[4][:].bitcast(BF16)
            pb2 = banks[1][:].bitcast(BF16)
            if ti < NQT:
                for c in range(3):
                    K.tr(pb[:, c * 128:(c + 1) * 128], lt[:, j, c * 128:(c + 1) * 128], identb[:], [lk, "identb"], ["bk4"])
                K.cp("dve", cqT[:, :, tsl], pb[:, 0:384].rearrange("p (c t) -> p c t", c=3), ["bk4"], ["cqT"])
            for c in range(2):
                K.tr(pb2[:, c * 128:(c + 1) * 128], lt[:, j, (3 + c) * 128:(4 + c) * 128], identb[:], [lk, "identb"], ["bk1"])
            K.cp("act", ckvT[:, :, tsl], pb2[:, 0:256].rearrange("p (c t) -> p c t", c=2), ["bk1"], ["ckvT"])
            for kc in range(2):
                K.mm(banks[5][:], ckvT[:, kc, tsl], wuvb[:, kc, :, :].rearrange("p h e -> p (h e)"), kc == 0, kc == 1,
                     ["ckvT", "wuvb"], ["bk5"])
            K.cp("act", vat[:, :, j, 0:64], banks[5][:].rearrange("p (h e) -> p h e", h=8), ["bk5"], ["vat"])
        K.dma("pool", VAs[:, :, ti * 4:(ti + 1) * 4, :].rearrange("h p j e -> p h (j e)"), vat[:].rearrange("p h j e -> p h (j e)"), ["vat"], ["VAs"], "st3")
        for h in range(8):
            bk = banks[2 + (h % 2)]
            for kc in range(2):
                K.mm(bk[:], wukb[:, kc, h, :], ckvT[:, kc, :], kc == 0, kc == 1, ["ckvT", "wukb"], ["bk%d" % (2 + h % 2)])
            if h % 2 == 0:
                K.cp("act", ktile[:, h, :], bk[:, :], ["bk%d" % (2 + h % 2)], ["ktile"])
            else:
                K.cp("dve", ktile[:, h, :], bk[:, :], ["bk%d" % (2 + h % 2)], ["ktile"])
        K.cp("pool", ktile[0:32, :, :], krr2[sl][:].unsqueeze(1).to_broadcast([32, 8, 512]), ["krr" + S_, "ktile"], ["ktile"])
        K.dma("pool", KTs[:, :, t0:t0 + 512], ktile[:], ["ktile"], ["KTs"], "st5")
        if ti < NQT:
            for h in range(8):
                ba, bb = banks[4], banks[5]
                for kc in range(3):
                    K.mm(ba[:], wuqb[:, kc, h, 0:128], cqT[:, kc, :], kc == 0, kc == 2, ["cqT", "wuqb"], ["bk4"])
                for kc in range(3):
                    K.mm(bb[:], wuqb[:, kc, h, 128:256], cqT[:, kc, :], kc == 0, kc == 2, ["cqT", "wuqb"], ["bk5"])
                K.tt("dve", rt1[:], ba[0:32, :], cst[:], ALU.mult, ["bk4", "cst" + S_], ["rt1"])
                K.tt("dve", rt2[:], bb[0:32, :], snt[:], ALU.mult, ["bk5", "snt" + S_], ["rt2"])
                K.cp("dve", qtile[:, h, :], ba[:, :], ["bk4"], ["ktile"])
                K.tt("pool", qtile[0:32, h, :], rt1[:], rt2[:], ALU.add, ["rt1", "rt2", "ktile"], ["ktile"])
            K.dma("pool", QTs[:, :, t0:t0 + 512], qtile[:], ["ktile"], ["QTs"], "st6")

    front(0)
    back1(0)
    conv_part(0)
    for ti in range(NT):
        if ti + 1 < NT:
            front(ti + 1)
            back1(ti + 1)
            conv_part(ti + 1)
        back2(ti)
        if ti + 2 < NT:
            pass
    conv_part(NT)
    P.barrier()

    if STOP <= 1:
        P.emit()
        return nc
    K.off = PERSIST_END
    kth = [K.sb([128, S], BF16) for _ in range(2)]
    vah = [K.sb([128, NCH * 66 + 64], BF16) for _ in range(2)]
    qb = [K.sb([128, 512], BF16) for _ in range(2)]
    pt = [K.sb([128, 512], BF16) for _ in range(4)]
    osb = K.sb([128, 512], F32)
    rden = K.sb([128, 512], F32)
    atb = [K.sb([64, 512], BF16) for _ in range(2)]
    sel = K.sb([128, 128], F32)
    K.memset("pool", sel[:], 0.0, ["sel"])
    K.memset("pool", sel[64:65, :], 1.0, ["sel"])
    K.memset("pool", rden[:], 0.0, ["rden0"])
    for sl in range(2):
        K.memset("pool", vah[sl][:, NCH * 66:NCH * 66 + 64], 0.0, ["vah%d" % sl])
    osb2 = [osb, K.sb([128, 512], F32)]
    rden2 = [rden, K.sb([128, 512], F32)]
    K.memset("pool", rden2[1][:], 0.0, ["rden1"])
    its = []
    for h in range(8):
        for qi in range(NQT):
            qn = min(512, OWN - qi * 512)
            for kt in range(NCH):
                its.append((h, qi, kt, qn))
    LOOK = 3
    tails = []

    def tail_fn(h, qi, qs, qn, tslot):
        def fn():
            K.mm(banks[6][:, 0:qn], sel[:], rden2[tslot][:, 0:qn], True, True, ["sel", "rden%d" % tslot], ["bk6"])
            K.tt("dve", atb[qs][:, 0:qn], osb2[tslot][0:64, 0:qn], banks[6][0:64, 0:qn], ALU.mult,
                 ["osb%d" % tslot, "bk6"], ["atb%d" % qs])
            K.dma("pool", ATs[h // 2, (h % 2) * 64:(h % 2) * 64 + 64, qi * 512:qi * 512 + qn], atb[qs][:, 0:qn],
                  ["atb%d" % qs], ["ATs"], "sta%d" % qs)
        return fn

    ntile = 0
    for step in range(len(its) + LOOK):
        if step < len(its):
            h, qi, kt, qn = its[step]
            hs = h % 2
            qs = (h * NQT + qi) % 2
            if qi == 0 and kt == 0:
                K.dma("sp", kth[hs][:], KTs[:, h, :], ["KTs"], ["kth%d" % hs], "ldk%d" % hs)
                K.dma("sp", vah[hs][:, 0:NCH * 66], VAs[h].rearrange("p j e -> p (j e)"), ["VAs"], ["vah%d" % hs], "ldv%d" % hs)
            if kt == 0:
                K.dma("sp", qb[qs][:, 0:qn], QTs[:, h, qi * 512:qi * 512 + qn], ["QTs"], ["qb%d" % qs], "ldq%d" % qs)
            sbk = step % 4
            K.mm(banks[sbk][:, 0:qn], kth[hs][:, kt * 128:(kt + 1) * 128], qb[qs][:, 0:qn], True, True,
                 ["kth%d" % hs, "qb%d" % qs], ["bk%d" % sbk])
            K.act(pt[sbk][:, 0:qn], banks[sbk][:, 0:qn], AF.Exp, ["bk%d" % sbk], ["pt%d" % sbk], scale=ASC)
        for (at, fn) in [t for t in tails if t[0] == step]:
            fn()
        tails = [t for t in tails if t[0] != step]
        j = step - LOOK
        if j >= 0:
            h, qi, kt, qn = its[j]
            hs = h % 2
            qs = (h * NQT + qi) % 2
            sbk = j % 4
            ob = banks[4 + qs]
            okey = "bk%d" % (4 + qs)
            K.mm(ob[:, 0:qn], vah[hs][:, kt * 66:kt * 66 + 128], pt[sbk][:, 0:qn], kt == 0, kt == NCH - 1,
                 ["vah%d" % hs, "pt%d" % sbk], [okey])
            if kt == NCH - 1:
                tslot = ntile % 2
                ntile += 1
                K.cp("dve", osb2[tslot][0:65, 0:qn], ob[0:65, 0:qn], [okey], ["osb%d" % tslot])
                K.recip(rden2[tslot][64:65, 0:qn], osb2[tslot][64:65, 0:qn], ["osb%d" % tslot], ["rden%d" % tslot])
                tails.append((step + 2, tail_fn(h, qi, qs, qn, tslot)))
    for (at, fn) in tails:
        fn()
    P.barrier()

    if STOP <= 2:
        P.emit()
        return nc
    K.off = PERSIST_END
    trif = K.sb([128, 2, 128], F32)
    K.dma("sp", trif[:], tri_d.rearrange("d s t -> s d t"), [], ["trif"], "ld0")
    gmh = K.sb([128, 512], F32)
    K.dma("sp", gmh[:], mh_norm_g.partition_broadcast(128), [], ["gmh"], "ld2")
    off_c = K.off
    Cf = K.sb([128, 4, 128], F32)
    Cb = K.sb([128, 4, 128], BF16)
    Cn = K.sb([128, 4], F32)
    Cnb = K.sb([128, 4], BF16)
    qTt = [K.sb([128, 4, 128], BF16) for _ in range(3)]
    kTt = [K.sb([128, 4, 128], BF16) for _ in range(3)]
    vat2 = [K.sb([128, 4, 130], BF16) for _ in range(3)]
    gtt = [K.sb([128, 16], F32) for _ in range(3)]
    hat = [K.sb([128, 512], F32) for _ in range(3)]
    sot2 = [K.sb([128, 512], F32) for _ in range(3)]
    bb8 = [K.sb([128, 8], F32) for _ in range(3)]
    g4 = [K.sb([128, 4], F32) for _ in range(3)]
    egs4 = [K.sb([128, 4], F32) for _ in range(3)]
    ws4 = [K.sb([128, 4], F32) for _ in range(3)]
    dc4 = [K.sb([128, 4], F32) for _ in range(3)]
    wq4 = [K.sb([128, 4], F32) for _ in range(3)]
    dd8 = [K.sb([128, 8], F32) for _ in range(3)]
    den4 = [K.sb([128, 4], F32) for _ in range(3)]
    un4 = [K.sb([128, 4], F32) for _ in range(3)]
    TL4 = [K.sb([128, 4, 128], F32) for _ in range(3)]
    EM4 = [K.sb([128, 4, 128], F32) for _ in range(3)]
    WT4 = [K.sb([128, 4, 128], F32) for _ in range(3)]
    PT4 = [K.sb([128, 4, 128], BF16) for _ in range(3)]
    KW4 = [K.sb([128, 4, 128], BF16) for _ in range(3)]
    tmpi4 = [K.sb([128, 4, 128], F32) for _ in range(3)]
    nd4 = [K.sb([128, 4, 128], F32) for _ in range(3)]
    hout = [K.sb([128, 512], F32) for _ in range(3)]
    st4 = K.sb([128, 4], F32)
    hc = K.sb([128, 512], F32)
    hsq = K.sb([128, 512], F32)
    hnb = K.sb([128, 512], BF16)
    memt = [K.sb([128, 4, 128], BF16) for _ in range(3)]
    lnsb = K.sb([128, 1], F32)
    K.memset("pool", lnsb[:], LNS, ["lnsb"])

    def bc_t(ap4):
        return ap4.unsqueeze(2).to_broadcast([128, 4, 128])

    def stage_E(dirn, c, full, sl):
        S_ = "%d" % sl
        t1 = c * 128 + 1
        K.dma("sp", kTt[sl][:], MKT[:, :, t1:t1 + 128], ["MKT"], ["kTt" + S_], "lck" + S_)
        yield
        K.dma("sp", vat2[sl][:], MVA[c], ["MVA"], ["vat2" + S_], "lcv" + S_)
        yield
        K.dma("sp", gtt[sl][:], MG[c], ["MG"], ["gtt" + S_], "lcg" + S_)
        yield
        if full:
            K.dma("sp", qTt[sl][:], MQT[:, :, t1:t1 + 128], ["MQT"], ["qTt" + S_], "lcq" + S_)
            yield
            pass
        li4 = gtt[sl][:, 8 * dirn:8 * dirn + 4]
        lf4 = gtt[sl][:, 8 * dirn + 4:8 * dirn + 8]
        gk = "gtt" + S_
        b4 = bb8[sl][:, 0:4]
        bl4 = bb8[sl][:, 4:8]
        K.mm(banks[0][:, 0:4], trif[:, dirn, :], lf4, True, True, ["trif", gk], ["bk0"])
        yield
        K.mm(banks[0][:, 4:8], onesf[:], lf4, True, True, ["onesf", gk], ["bk0"])
        yield
        K.cp("dve", bb8[sl][:], banks[0][:, 0:8], ["bk0"], ["bb8" + S_])
        yield
        K.tt("dve", g4[sl][:], li4, b4, ALU.subtract, [gk, "bb8" + S_], ["g4" + S_])
        yield
        K.tt("dve", ws4[sl][:], g4[sl][:], bl4, ALU.add, ["g4" + S_, "bb8" + S_], ["ws4" + S_])
        yield
        K.act(ws4[sl][:], ws4[sl][:], AF.Exp, ["ws4" + S_], ["ws4" + S_])
        yield
        K.act(dc4[sl][:], bl4, AF.Exp, ["bb8" + S_], ["dc4" + S_])
        yield
        pbk = banks[5][:].bitcast(BF16)
        for hd in range(4):
            K.tr(pbk[:, hd * 128:(hd + 1) * 128], kTt[sl][:, hd, :], identb[:], ["kTt" + S_, "identb"], ["bk5"])
        K.tt("dve", KW4[sl][:], pbk[:, 0:512].rearrange("p (h d) -> p h d", h=4), bc_t(ws4[sl][:]), ALU.mult,
             ["bk5", "ws4" + S_], ["KW4" + S_])
        yield
        if full:
            K.act(egs4[sl][:], g4[sl][:], AF.Exp, ["g4" + S_, "lnsb"], ["egs4" + S_], bias=lnsb[:])
            yield
            K.act(wq4[sl][:], b4, AF.Exp, ["bb8" + S_, "lnsb"], ["wq4" + S_], bias=lnsb[:])
            yield
            K.tt("pool", TL4[sl][:], trif[:, dirn, :].unsqueeze(1).to_broadcast([128, 4, 128]), bc_t(lf4), ALU.mult,
                 ["trif", gk], ["TL4" + S_])
            yield
            K.tt("pool", EM4[sl][:], trif[:, dirn, :].unsqueeze(1).to_broadcast([128, 4, 128]), bc_t(egs4[sl][:]), ALU.mult,
                 ["trif", "egs4" + S_], ["EM4" + S_])
            yield
            for hd in range(4):
                K.mm(banks[2][:, hd * 128:(hd + 1) * 128], kTt[sl][:, hd, :], qTt[sl][:, hd, :], True, True,
                     ["kTt" + S_, "qTt" + S_], ["bk2"])
            K.mm(banks[1][:], onesf[:], TL4[sl][:].rearrange("p h t -> p (h t)"), True, True, ["onesf", "TL4" + S_], ["bk1"])
            yield
            K.act(WT4[sl][:].rearrange("p h t -> p (h t)"), banks[1][:], AF.Exp, ["bk1"], ["WT4" + S_])
            yield
            K.tt("pool", WT4[sl][:], WT4[sl][:], EM4[sl][:], ALU.mult, ["WT4" + S_, "EM4" + S_], ["WT4" + S_])
            yield
            K.tt("dve", PT4[sl][:].rearrange("p h t -> p (h t)"), banks[2][:], WT4[sl][:].rearrange("p h t -> p (h t)"),
                 ALU.mult, ["bk2", "WT4" + S_], ["PT4" + S_])
            yield

    def stage_M(dirn, c, full, sl):
        S_ = "%d" % sl
        if full:
            for hd in range(4):
                K.mm(banks[4][:, hd * 128:(hd + 1) * 128], qTt[sl][:, hd, :], Cb[:, hd, :], True, True,
                     ["qTt" + S_, "Cb"], ["bk4"])
            for hd in range(4):
                K.mm(banks[7][:, 4 + hd:5 + hd], qTt[sl][:, hd, :], Cnb[:, hd:hd + 1], True, True,
                     ["qTt" + S_, "Cnb"], ["bk7"])
            K.tt("dve", tmpi4[sl][:], banks[4][:].rearrange("p (h e) -> p h e", h=4), bc_t(wq4[sl][:]), ALU.mult,
                 ["bk4", "wq4" + S_], ["tmpi4" + S_])
            K.cp("dve", dd8[sl][:, 4:8], banks[7][:, 4:8], ["bk7"], ["ddq" + S_])
        for hd in range(4):
            K.mm(banks[6][:, hd * 128:(hd + 1) * 128], KW4[sl][:, hd, :], vat2[sl][:, hd, 0:128], True, True,
                 ["KW4" + S_, "vat2" + S_], ["bk6"])
        for hd in range(4):
            K.mm(banks[7][:, 8 + hd:9 + hd], KW4[sl][:, hd, :], vat2[sl][:, hd, 128:129], True, True,
                 ["KW4" + S_, "vat2" + S_], ["bk7"])
        K.tt("pool", Cf[:], Cf[:], bc_t(dc4[sl][:]), ALU.mult, ["Cf", "dc4" + S_, "Cb"], ["Cf"])
        K.tt("dve", Cf[:].rearrange("p h e -> p (h e)"), Cf[:].rearrange("p h e -> p (h e)"), banks[6][:], ALU.add,
             ["Cf", "bk6"], ["Cf"])
        K.cp("act", Cb[:], Cf[:], ["Cf"], ["Cb"])
        K.cp("dve", un4[sl][:], banks[7][:, 8:12], ["bk7"], ["un4" + S_])
        K.tt("dve", Cn[:], Cn[:], dc4[sl][:], ALU.mult, ["Cn", "dc4" + S_], ["Cn"])
        K.tt("dve", Cn[:], Cn[:], un4[sl][:], ALU.add, ["Cn", "un4" + S_], ["Cn"])
        K.cp("dve", Cnb[:], Cn[:], ["Cn"], ["Cnb"])

    def stage_L(dirn, c, full, sl):
        S_ = "%d" % sl
        if not full:
            return
        yield
        for hd in range(4):
            K.mm(banks[3][:, hd * 128:(hd + 1) * 128], PT4[sl][:, hd, :], vat2[sl][:, hd, 0:128], True, True,
                 ["PT4" + S_, "vat2" + S_], ["bk3"])
            yield
        for hd in range(4):
            K.mm(banks[7][:, hd:hd + 1], PT4[sl][:, hd, :], vat2[sl][:, hd, 128:129], True, True,
                 ["PT4" + S_, "vat2" + S_], ["bk7"])
            yield
        K.tt("dve", nd4[sl][:].rearrange("p h e -> p (h e)"), banks[3][:], tmpi4[sl][:].rearrange("p h e -> p (h e)"),
             ALU.add, ["bk3", "tmpi4" + S_], ["nd4" + S_])
        yield
        K.cp("dve", dd8[sl][:, 0:4], banks[7][:, 0:4], ["bk7"], ["ddi" + S_])
        yield
        K.tt("dve", den4[sl][:], dd8[sl][:, 4:8], wq4[sl][:], ALU.mult, ["ddq" + S_, "wq4" + S_], ["den4" + S_])
        yield
        K.tt("dve", den4[sl][:], den4[sl][:], dd8[sl][:, 0:4], ALU.add, ["den4" + S_, "ddi" + S_], ["den4" + S_])
        yield
        K.act(den4[sl][:], den4[sl][:], AF.Abs, ["den4" + S_], ["den4" + S_])
        yield
        K.ts("dve", den4[sl][:], den4[sl][:], 1.0, None, ALU.max, None, ["den4" + S_], ["den4" + S_])
        yield
        K.recip(den4[sl][:], den4[sl][:], ["den4" + S_], ["den4" + S_])
        yield
        K.tt("pool", hout[sl][:].rearrange("p (h e) -> p h e", h=4), nd4[sl][:], bc_t(den4[sl][:]), ALU.mult,
             ["nd4" + S_, "den4" + S_], ["hout" + S_])
        yield
        K.dma("pool", (HA if dirn == 0 else HB)[c], hout[sl][:], ["hout" + S_], ["HA" if dirn == 0 else "HB"], "sth" + S_)
        yield

    step = 0
    for dirn in range(2):
        K.memset("pool", Cf[:], 0.0, ["Cf"])
        K.memset("pool", Cb[:], 0.0, ["Cb"])
        K.memset("pool", Cn[:], 0.0, ["Cn"])
        K.memset("pool", Cnb[:], 0.0, ["Cnb"])
        if dirn == 0:
            order = [(c, True) for c in range(OWNC)]
        else:
            order = [(c, False) for c in range(NCH - 1, OWNC - 1, -1)] + [(c, True) for c in range(OWNC - 1, -1, -1)]
        sls = [(step + i) % 3 for i in range(len(order))]
        step += len(order)
        n = len(order)
        def run2(ga, gb):
            da = db = False
            while not (da and db):
                if not da:
                    try:
                        next(ga)
                    except StopIteration:
                        da = True
                if not db:
                    try:
                        next(gb)
                    except StopIteration:
                        db = True

        for _ in stage_E(dirn, order[0][0], order[0][1], sls[0]):
            pass
        for i in range(n):
            stage_M(dirn, order[i][0], order[i][1], sls[i])
            gl = stage_L(dirn, order[i][0], order[i][1], sls[i])
            if i + 1 < n:
                run2(stage_E(dirn, order[i + 1][0], order[i + 1][1], sls[i + 1]), gl)
            else:
                for _ in gl:
                    pass
        P.barrier()

    K.off = off_c
    NG = (OWNC + 3) // 4
    lha = [K.sb([128, 4, 512], F32) for _ in range(2)]
    lhb = [K.sb([128, 4, 512], F32) for _ in range(2)]
    lso = [K.sb([128, 4, 512], F32) for _ in range(2)]
    lhc = K.sb([128, 4, 512], F32)
    lsq = K.sb([128, 4, 512], F32)
    lnb = K.sb([128, 4, 512], BF16)
    lst = K.sb([128, 16], F32)
    lmt = [K.sb([128, 4, 128], BF16) for _ in range(2)]
    for g in range(NG):
        c0 = g * 4
        n = min(4, OWNC - c0)
        sl = g % 2
        S_ = "%d" % sl
        K.dma("sp", lha[sl][:, 0:n, :], HA[c0:c0 + n].rearrange("j p e -> p j e"), ["HA"], ["lha" + S_], "lna" + S_)
        K.dma("sp", lhb[sl][:, 0:n, :], HB[c0:c0 + n].rearrange("j p e -> p j e"), ["HB"], ["lhb" + S_], "lnb" + S_)
        K.dma("sp", lso[sl][:, 0:n, :], SO[c0:c0 + n].rearrange("j p e -> p j e"), ["SO"], ["lso" + S_], "lns" + S_)
        hs = lha[sl][:, 0:n, :]
        K.tt("dve", hs, hs, lhb[sl][:, 0:n, :], ALU.add, ["lha" + S_, "lhb" + S_], ["lha" + S_])
        hv = hs.rearrange("p j (h e) -> p (j h) e", h=4)
        stv = lst[:, 0:4 * n]
        K.P.add("dve", lambda e, hv=hv, stv=stv: e.tensor_reduce(out=stv, in_=hv, axis=AX.X, op=ALU.add), ["lha" + S_], ["lst"])
        K.ts("dve", stv, stv, 1.0 / 128, None, ALU.mult, None, ["lst"], ["lst"])
        hcv = lhc[:, 0:n, :].rearrange("p j (h e) -> p (j h) e", h=4)
        stb = stv.unsqueeze(2).to_broadcast([128, 4 * n, 128])
        K.tt("dve", hcv, hv, stb, ALU.subtract, ["lha" + S_, "lst"], ["lhc"])
        K.tt("pool", lsq[:, 0:n, :], lhc[:, 0:n, :], lhc[:, 0:n, :], ALU.mult, ["lhc"], ["lsq"])
        sqv = lsq[:, 0:n, :].rearrange("p j (h e) -> p (j h) e", h=4)
        K.P.add("dve", lambda e, sqv=sqv, stv=stv: e.tensor_reduce(out=stv, in_=sqv, axis=AX.X, op=ALU.add), ["lsq"], ["lst"])
        K.ts("dve", stv, stv, 1.0 / 128, EPS, ALU.mult, ALU.add, ["lst"], ["lst"])
        K.rsqrt(stv, "lst")
        K.tt("pool", hcv, hcv, stb, ALU.mult, ["lhc", "lst"], ["lhc"])
        K.tt("pool", lhc[:, 0:n, :], lhc[:, 0:n, :], gmh[:].unsqueeze(1).to_broadcast([128, n, 512]), ALU.mult,
             ["lhc", "gmh"], ["lhc"])
        K.tt("dve", lnb[:, 0:n, :], lhc[:, 0:n, :], lso[sl][:, 0:n, :], ALU.mult, ["lhc", "lso" + S_], ["lnb"])
        for j in range(n):
            c = c0 + j
            bi = j % 2
            pbk = banks[bi][:].bitcast(BF16)
            for hd in range(4):
                K.tr(pbk[:, hd * 128:(hd + 1) * 128], lnb[:, j, hd * 128:(hd + 1) * 128], identb[:], ["lnb", "identb"], ["bk%d" % bi])
            K.cp("act" if bi == 0 else "dve", lmt[bi][:], pbk[:, 0:512].rearrange("p (h t) -> p h t", h=4), ["bk%d" % bi], ["lmt%d" % bi])
            K.dma("pool", MEMT[:, :, c * 128:(c + 1) * 128].rearrange("h d t -> d h t"), lmt[bi][:], ["lmt%d" % bi], ["MEMT"], "stm%d" % bi)
    P.barrier()

    if STOP <= 3:
        P.emit()
        return nc
    K.off = PERSIST_END
    woutb = K.sb([128, 8, D], BF16)
    stg2 = [K.sb([128, D], F32) for _ in range(2)]
    for kc in range(8):
        sl = kc % 2
        K.dma("sp", stg2[sl][:], w_out[kc * 128:(kc + 1) * 128, :], [], ["stg%d" % sl], "ldw%d" % sl)
        K.tt("dve" if kc % 2 == 0 else "pool", woutb[:, kc, :], stg2[sl][:], g12[:, 0, :], ALU.mult,
             ["stg%d" % sl, "g12"], ["woutb"])
    att = [K.sb([128, 4, 128], BF16) for _ in range(2)]
    met = [K.sb([128, 4, 128], BF16) for _ in range(2)]
    xc = [K.sb([128, D], F32) for _ in range(2)]
    x1t = [K.sb([128, D], F32) for _ in range(2)]
    junk2 = K.sb([128, D], F32)
    s1 = K.sb([128, 1], F32)
    xn2 = K.sb([128, D], BF16)
    h2t = [K.sb([128, 8, 128], BF16) for _ in range(2)]
    def d_stage1(c):
        sl = c % 2
        tsl = slice(c * 128, (c + 1) * 128)
        K.dma("sp", att[sl][:], ATs[:, :, tsl].rearrange("a f t -> f a t"), ["ATs"], ["att%d" % sl], "lda%d" % sl)
        K.dma("sp", met[sl][:], MEMT[:, :, tsl].rearrange("h d t -> d h t"), ["MEMT"], ["met%d" % sl], "ldm%d" % sl)
        K.dma("sp", xc[sl][:], xs[tsl, :], [], ["xc%d" % sl], "ldx%d" % sl)
        for nh in range(2):
            bk = banks[nh]
            for kc in range(8):
                lhs = att[sl][:, kc, :] if kc < 4 else met[sl][:, kc - 4, :]
                K.mm(bk[:], lhs, woutb[:, kc, nh * 512:(nh + 1) * 512], kc == 0, kc == 7,
                     ["att%d" % sl, "met%d" % sl, "woutb"], ["bk%d" % nh])
            K.tt("dve", x1t[sl][:, nh * 512:(nh + 1) * 512], bk[:], xc[sl][:, nh * 512:(nh + 1) * 512], ALU.add,
                 ["bk%d" % nh, "xc%d" % sl], ["x1t%d" % sl])
        K.dma("pool", X1[tsl, :], x1t[sl][:], ["x1t%d" % sl], ["X1"], "stx%d" % sl)
    def d_stage2(c):
        sl = c % 2
        tsl = slice(c * 128, (c + 1) * 128)
        K.act(junk2[:], x1t[sl][:], AF.Square, ["x1t%d" % sl], ["junk2", "s1"], accum=s1[:])
        K.ts("dve", s1[:], s1[:], 1.0 / D, EPS, ALU.mult, ALU.add, ["s1"], ["s1"])
        K.rsqrt(s1[:], "s1")
        K.act(xn2[:], x1t[sl][:], AF.Copy, ["x1t%d" % sl, "s1"], ["xn2"], scale=s1[:])
        for half in range(2):
            pbk = banks[2 + half][:].bitcast(BF16)
            for k4 in range(4):
                kc = half * 4 + k4
                K.tr(pbk[:, k4 * 128:(k4 + 1) * 128], xn2[:, kc * 128:(kc + 1) * 128], identb[:], ["xn2", "identb"],
                     ["bk%d" % (2 + half)])
            for k4 in range(4):
                kc = half * 4 + k4
                if half == 0:
                    K.ts("dve", h2t[sl][:, kc, :], pbk[:, k4 * 128:(k4 + 1) * 128], A2[:, kc:kc + 1], B2[:, kc:kc + 1],
                         ALU.mult, ALU.add, ["bk%d" % (2 + half), "A2", "modT"], ["h2t%d" % sl])
                else:
                    K.act(h2t[sl][:, kc, :], pbk[:, k4 * 128:(k4 + 1) * 128], AF.Identity,
                          ["bk%d" % (2 + half), "A2", "modT"], ["h2t%d" % sl], bias=B2[:, kc:kc + 1], scale=A2[:, kc:kc + 1])
        K.dma("pool", H2T[:, :, tsl].rearrange("k p t -> p k t"), h2t[sl][:], ["h2t%d" % sl], ["H2T"], "sth%d" % sl)
    d_stage1(0)
    for c in range(OWNC):
        if c + 1 < OWNC:
            d_stage1(c + 1)
        d_stage2(c)
    P.barrier()

    if STOP <= 4:
        P.emit()
        return nc
    K.off = PERSIST_END
    wupb = K.sb([128, 8, 2 * DFF], BF16)
    wdnb = K.sb([128, 22, D], BF16)
    cfw = K.sb([128, 3, 44], F32)
    cfb = K.sb([128, 44], F32)
    fgb = K.sb([128, D], F32)
    off_e = K.off
    stg3 = [K.sb([128, DFF], F32) for _ in range(2)]
    i3 = 0
    for kc in range(8):
        for hf in range(2):
            sl = i3 % 2
            K.dma("sp", stg3[sl][:], w_up[kc * 128:(kc + 1) * 128, hf * DFF:(hf + 1) * DFF], [], ["stg%d" % sl], "ldw%d" % sl)
            K.cp(cengs[i3 % 3], wupb[:, kc, hf * DFF:(hf + 1) * DFF], stg3[sl][:], ["stg%d" % sl], ["wupb"])
            i3 += 1
    for kc in range(22):
        sl = i3 % 2
        K.dma("sp", stg3[sl][:, 0:D], w_down[kc * 128:(kc + 1) * 128, :], [], ["stg%d" % sl], "ldw%d" % sl)
        K.tt("dve" if kc % 2 == 0 else "pool", wdnb[:, kc, :], stg3[sl][:, 0:D], g12[:, 1, :], ALU.mult,
             ["stg%d" % sl, "g12"], ["wdnb"])
        i3 += 1
    for j in range(3):
        K.dma("sp", cfw[:, j, :], conv_f_w[j].rearrange("(c p) -> p c", p=128), [], ["cfw"], "ld0", slow=True)
    K.dma("sp", cfb[:], conv_f_b.rearrange("(c p) -> p c", p=128), [], ["cfb"], "ld1", slow=True)
    K.dma("sp", fgb[:], final_g.partition_broadcast(128), [], ["fgb"], "ld2")
    P.barrier()
    K.off = off_e
    h2w = K.sb([128, 8, 514], BF16)
    ub = [K.sb([128, 514], F32) for _ in range(2)]
    ca2 = [K.sb([128, 512], F32) for _ in range(2)]
    cg2 = [K.sb([128, 512], F32) for _ in range(2)]
    actT = K.sb([128, 22, 512], BF16)
    x1c = [K.sb([128, D], F32) for _ in range(2)]
    yc1 = K.sb([128, D], F32)
    yc = [yc1, yc1]
    s2v = K.sb([128, 1], F32)
    for ti in range(NFT):
        t0 = ti * 512
        sl = ti % 2
        hk = "h2w"
        if ti == 0:
            K.memset("pool", h2w[:, :, 0:1], 0.0, [hk])
            K.dma("sp", h2w[:, :, 1:514], H2T[:, :, 0:513].rearrange("k p t -> p k t"), ["H2T"], [hk], "ldh0")
        else:
            K.dma("sp", h2w[:, :, :], H2T[:, :, t0 - 1:t0 + 513].rearrange("k p t -> p k t"), ["H2T"], [hk], "ldh0")
        for i in range(22):
            for (which, fc) in ((0, i), (1, 22 + i)):
                u = ub[which]
                uk = "ub%d" % which
                for nt, (n0, n1) in enumerate(((0, 258), (258, 514))):
                    bk = banks[which * 2 + nt]
                    for kc in range(8):
                        K.mm(bk[:, 0:n1 - n0], wupb[:, kc, fc * 128:(fc + 1) * 128], h2w[:, kc, n0:n1],
                             kc == 0, kc == 7, [hk, "wupb"], ["bk%d" % (which * 2 + nt)])
                    K.cp("act", u[:, n0:n1], bk[:, 0:n1 - n0], ["bk%d" % (which * 2 + nt)], [uk])
                dst = ca2[i % 2] if which == 0 else cg2[i % 2]
                dk = ("ca%d" if which == 0 else "cg%d") % (i % 2)
                eng = "dve"
                K.ts(eng, dst[:], u[:, 1:513], cfw[:, 1, fc:fc + 1], cfb[:, fc:fc + 1], ALU.mult, ALU.add, [uk, "cfw", "cfb"], [dk])
                K.stt(eng, dst[:], u[:, 0:512], cfw[:, 0, fc:fc + 1], dst[:], ALU.mult, ALU.add, [uk, "cfw", dk], [dk])
                K.stt(eng, dst[:], u[:, 2:514], cfw[:, 2, fc:fc + 1], dst[:], ALU.mult, ALU.add, [uk, "cfw", dk], [dk])
            K.act(cg2[i % 2][:], cg2[i % 2][:], AF.Silu, ["cg%d" % (i % 2)], ["cg%d" % (i % 2)])
            K.tt("pool", actT[:, i, :], cg2[i % 2][:], ca2[i % 2][:], ALU.mult, ["cg%d" % (i % 2), "ca%d" % (i % 2)], ["actT%d" % i])
        ak = ["actT%d" % i for i in range(22)]
        for j in range(4):
            cidx = ti * 4 + j
            s3 = cidx % 2
            tsl = slice(t0 + j * 128, t0 + (j + 1) * 128)
            K.dma("sp", x1c[s3][:], X1[tsl, :], ["X1"], ["x1c%d" % s3], "ldx%d" % s3)
            for nh in range(2):
                bk = banks[4 + nh]
                for kc in range(22):
                    K.mm(bk[:], actT[:, kc, j * 128:(j + 1) * 128], wdnb[:, kc, nh * 512:(nh + 1) * 512], kc == 0, kc == 21,
                         ak + ["wdnb"], ["bk%d" % (4 + nh)])
                K.tt("dve", x1c[s3][:, nh * 512:(nh + 1) * 512], bk[:], x1c[s3][:, nh * 512:(nh + 1) * 512], ALU.add,
                     ["bk%d" % (4 + nh), "x1c%d" % s3], ["x1c%d" % s3])
            K.act(yc[s3][:], x1c[s3][:], AF.Square, ["x1c%d" % s3], ["yc", "s2v"], accum=s2v[:])
            K.ts("dve", s2v[:], s2v[:], 1.0 / D, EPS, ALU.mult, ALU.add, ["s2v"], ["s2v"])
            K.rsqrt(s2v[:], "s2v")
            K.act(yc[s3][:], x1c[s3][:], AF.Copy, ["x1c%d" % s3, "s2v"], ["yc"], scale=s2v[:, 0:1])
            K.tt("pool", yc[s3][:], yc[s3][:], fgb[:], ALU.mult, ["yc", "fgb"], ["yc"])
            K.dma("pool", y_out[tsl, :], yc[s3][:], ["yc"], [], "sty%d" % s3)
    P.emit()
    return nc


_CACHE = {}


def _consts(S, flip):
    half = 16
    inv = (1.0 / (np.float32(10000.0) ** (np.arange(half, dtype=np.float32) * np.float32(2.0 / 32)))).astype(np.float32)
    pos = np.arange(S, dtype=np.float32)
    if flip:
        pos = pos[::-1].copy()
    ang = (pos[:, None] * inv[None, :]).astype(np.float32)
    cos = np.cos(ang.astype(np.float64)).astype(np.float32).T
    sin = np.sin(ang.astype(np.float64)).astype(np.float32).T
    cos_t = np.ascontiguousarray(np.concatenate([cos, cos], axis=0))
    sin_t = np.ascontiguousarray(np.concatenate([sin, sin], axis=0))
    s = np.arange(128)[:, None]
    t = np.arange(128)[None, :]
    tri = np.stack([(s <= t), (s >= t)]).astype(np.float32)
    mneg = ((1.0 - tri) * -30000.0).astype(np.float32)
    return dict(ident=np.eye(128, dtype=np.float32), cos_t=cos_t, sin_t=sin_t, tri=tri, mneg=mneg)


def kernel(x_prompt, x_sample, c_prompt, c_sample, norm1_g, w_ada, b_ada, w_in, b_gate, q_norm_g,
           kv_norm_g, w_uq, w_ukv, conv_m_w, conv_m_b, mh_norm_g, w_out, norm2_g, w_up, conv_f_w,
           conv_f_b, w_down, final_g):
    f = lambda a: np.ascontiguousarray(np.asarray(a, dtype=np.float32))
    x_prompt, x_sample, c_prompt, c_sample = f(x_prompt), f(x_sample), f(c_prompt), f(c_sample)
    S = x_prompt.shape[1]
    HALF = S // 2
    seqs = [(x_prompt[0], c_prompt[0]), (x_prompt[1], c_prompt[1]), (x_sample[0], c_sample[0])]
    w_in0 = f(w_in)[0]
    gperm = np.concatenate([np.arange(G0), G0 + np.array([8, 9, 10, 11, 12, 13, 14, 15, 0, 1, 2, 3, 4, 5, 6, 7])])
    shared = dict(norm1_g=f(norm1_g)[0], w_ada=f(w_ada)[0], b_ada=f(b_ada)[0], q_norm_g=f(q_norm_g)[0],
                  kv_norm_g=f(kv_norm_g)[0], w_uq=f(w_uq)[0], w_ukv=f(w_ukv)[0], conv_m_b=f(conv_m_b)[0],
                  mh_norm_g=f(mh_norm_g)[0], w_out=f(w_out)[0], norm2_g=f(norm2_g)[0], w_up=f(w_up)[0],
                  conv_f_b=f(conv_f_b)[0], w_down=f(w_down)[0], final_g=f(final_g))
    per_flip = []
    for flip in (False, True):
        d = dict(shared)
        d.update(_consts(S, flip))
        if flip:
            d["w_in"] = np.ascontiguousarray(w_in0[:, gperm])
            d["b_gate"] = np.ascontiguousarray(f(b_gate)[0][gperm[G0:] - G0])
            d["conv_m_w"] = np.ascontiguousarray(f(conv_m_w)[0][::-1])
            d["conv_f_w"] = np.ascontiguousarray(f(conv_f_w)[0][::-1])
        else:
            d["w_in"] = w_in0
            d["b_gate"] = f(b_gate)[0]
            d["conv_m_w"] = f(conv_m_w)[0]
            d["conv_f_w"] = f(conv_f_w)[0]
        per_flip.append(d)
    in_maps = []
    for core in range(8):
        cc = core % 6
        s, j = cc // 2, cc % 2
        d = dict(per_flip[j])
        xseq, cv = seqs[s]
        d["xs"] = np.ascontiguousarray(xseq[::-1]) if j == 1 else xseq
        d["cvec"] = cv
        in_maps.append(d)
    if S not in _CACHE:
        _CACHE[S] = build_program(S)
    nc = _CACHE[S]
    res = run_bass_kernel_spmd(nc, in_maps, core_ids=list(range(8)))
    outs = []
    for s in range(3):
        y0 = np.asarray(res.results[2 * s]["y"], dtype=np.float32)
        y1 = np.asarray(res.results[2 * s + 1]["y"], dtype=np.float32)[::-1]
        outs.append(np.concatenate([y0, y1], axis=0))
    y_prompt = np.stack([outs[0], outs[1]], axis=0)
    y_sample = outs[2][None]
    return (y_prompt, y_sample)
```
